# Optimizing a Trainium2 kernel written in Bass

```python
import math
import jax, jax.numpy as jnp
from jax import lax
import numpy as np

D_MODEL = 1024
BATCH = 8
SEQ = 8192
DEPTH = 4
DEC_BATCH = 8
DEC_SEQ = 64
PAST_LEN = 1024

CHUNK = 64
N_META = 16
N_HEADS = 8
HEAD_DIM = 64
HD2 = 2 * HEAD_DIM
ATTN_WIDTH = N_HEADS * HD2
N_BUCKETS = 32
MAX_DISTANCE = 128
POOL_WINDOWS = (2, 4, 8, 16)
N_POOL_GROUPS = len(POOL_WINDOWS)
POOL_GROUP = D_MODEL // N_POOL_GROUPS
POOL_STATE = max(POOL_WINDOWS) - 1
D_FF = 4 * D_MODEL
N_ATTN_LAYERS = (DEPTH + 1) // 2
N_POOL_LAYERS = DEPTH // 2
Q_BLOCK = 128
EPS = 1e-6
SUBLN_EPS = 1e-5
NEG_INF = -1e30

kernel_name = "hybrid_diffattn_pool_streaming_step"


def rms_norm(x, g, eps=EPS):
    xf = x.astype(jnp.float32)
    y = xf * lax.rsqrt(jnp.mean(xf * xf, axis=-1, keepdims=True) + eps)
    return (y * g.astype(jnp.float32)).astype(x.dtype)


def t5_bucket(rel):
    nb = N_BUCKETS // 2
    max_exact = nb // 2
    ret = jnp.where(rel > 0, nb, 0)
    n = jnp.abs(rel)
    nf = jnp.maximum(n, 1).astype(jnp.float32)
    large = max_exact + (jnp.log(nf / max_exact) / math.log(MAX_DISTANCE / max_exact)
                         * (nb - max_exact)).astype(jnp.int32)
    large = jnp.minimum(large, nb - 1)
    return ret + jnp.where(n < max_exact, n, large)


def rel_bias(q_pos, k_pos, table):
    bucket = t5_bucket(k_pos[None, :] - q_pos[:, None])
    return jnp.transpose(table[bucket].astype(jnp.float32), (2, 0, 1))


def diff_lambda(lq1, lk1, lq2, lk2, lam_init):
    f32 = lambda a: a.astype(jnp.float32)
    return (jnp.exp(jnp.sum(f32(lq1) * f32(lk1))) - jnp.exp(jnp.sum(f32(lq2) * f32(lk2)))
            + lam_init)


def qkv_proj(h, w_qkv):
    B, L, _ = h.shape
    qkv = jnp.einsum('bld,de->ble', h, w_qkv).reshape(B, L, 3, N_HEADS, HD2)
    return qkv[:, :, 0], qkv[:, :, 1], qkv[:, :, 2]


def diff_attend(q, k, v, bias, mask, lam, subln_g, lam_init):
    scale = HEAD_DIM ** -0.5
    s1 = jnp.einsum('bqhd,bkhd->bhqk', q[..., :HEAD_DIM], k[..., :HEAD_DIM]).astype(jnp.float32) * scale + bias
    s2 = jnp.einsum('bqhd,bkhd->bhqk', q[..., HEAD_DIM:], k[..., HEAD_DIM:]).astype(jnp.float32) * scale + bias
    s1 = jnp.where(mask, s1, NEG_INF)
    s2 = jnp.where(mask, s2, NEG_INF)
    p = jax.nn.softmax(s1, axis=-1) - lam * jax.nn.softmax(s2, axis=-1)
    o = jnp.einsum('bhqk,bkhd->bqhd', p.astype(v.dtype), v)
    o = rms_norm(o, subln_g, SUBLN_EPS) * (1.0 - lam_init)
    B, Lq = q.shape[:2]
    return o.reshape(B, Lq, ATTN_WIDTH)


def attend_rows(q, k_new, v_new, prev_k, prev_v, prev_pos, pos0, lam, subln_g, lam_init, rel_table):
    B, L = q.shape[:2]
    P = prev_k.shape[1]
    k_all = jnp.concatenate([jnp.broadcast_to(prev_k, (B,) + prev_k.shape[1:]), k_new], axis=1)
    v_all = jnp.concatenate([jnp.broadcast_to(prev_v, (B,) + prev_v.shape[1:]), v_new], axis=1)
    k_pos = jnp.concatenate([prev_pos.astype(jnp.int32), pos0 + jnp.arange(L, dtype=jnp.int32)])
    k_chunk = jnp.concatenate([jnp.full((P,), -1, jnp.int32), jnp.arange(L, dtype=jnp.int32) // CHUNK])

    def block(q_blk, start):
        idx = start + jnp.arange(q_blk.shape[1], dtype=jnp.int32)
        mask = k_chunk[None, :] <= (idx // CHUNK)[:, None]
        bias = rel_bias(pos0 + idx, k_pos, rel_table)
        return diff_attend(q_blk, k_all, v_all, bias, mask, lam, subln_g, lam_init)

    if L > Q_BLOCK:
        nblk = L // Q_BLOCK
        qb = q.reshape(B, nblk, Q_BLOCK, N_HEADS, HD2).swapaxes(0, 1)
        o = lax.map(lambda a: block(a[0], a[1] * Q_BLOCK), (qb, jnp.arange(nblk, dtype=jnp.int32)))
        return o.swapaxes(0, 1).reshape(B, L, ATTN_WIDTH)
    return block(q, 0)


def pool_mix(ext, n_prev, w_pool, scale):
    B, T, _ = ext.shape
    h = ext[:, n_prev:]
    L = T - n_prev
    c = jnp.cumsum(ext.astype(jnp.float32), axis=1)
    cnt = jnp.arange(1, T + 1, dtype=jnp.float32)[None, :, None]
    means = []
    for g, w in enumerate(POOL_WINDOWS):
        cg = c[..., g * POOL_GROUP:(g + 1) * POOL_GROUP]
        lower = jnp.pad(cg, ((0, 0), (w, 0), (0, 0)))[:, :T]
        means.append((cg - lower) / jnp.minimum(cnt, w))
    pooled = jnp.concatenate(means, axis=-1)[:, n_prev:]
    d = (pooled - h.astype(jnp.float32)).astype(ext.dtype).reshape(B, L, N_POOL_GROUPS, POOL_GROUP)
    y = jnp.einsum('blgc,gce->blge', d, w_pool).reshape(B, L, D_MODEL)
    return y * scale


def sq_relu_mlp(h, w_up, w_down):
    u = jnp.einsum('bld,df->blf', h, w_up)
    return jnp.einsum('blf,fd->bld', jnp.square(jax.nn.relu(u)), w_down)


def trunk(x, prev_k, prev_v, prev_pos, pos0, pool_prev, state_only_last,
          rel_table, norm_mix_pre, norm_mix_post, norm_ffn_pre, norm_ffn_post,
          w_qkv, lambda_q1, lambda_k1, lambda_q2, lambda_k2, subln_g, w_o,
          w_pool, pool_scale, w_up, w_down):
    ks, vs, tails = [], [], []
    for i in range(DEPTH):
        h = rms_norm(x, norm_mix_pre[i])
        last_state_only = state_only_last and i == DEPTH - 1
        if i % 2 == 0:
            a = i // 2
            q, k, v = qkv_proj(h, w_qkv[a])
            ks.append(k)
            vs.append(v)
            if last_state_only:
                break
            lam_init = 0.8 - 0.6 * math.exp(-0.3 * i)
            lam = diff_lambda(lambda_q1[a], lambda_k1[a], lambda_q2[a], lambda_k2[a], lam_init)
            o = attend_rows(q, k, v, prev_k[a], prev_v[a], prev_pos, pos0, lam, subln_g[a], lam_init, rel_table)
            y = jnp.einsum('ble,ed->bld', o, w_o[a])
        else:
            p = i // 2
            prev = pool_prev[p]
            ext = jnp.concatenate([jnp.broadcast_to(prev, (h.shape[0],) + prev.shape[1:]), h], axis=1)
            tails.append(ext[:, -POOL_STATE:])
            if last_state_only:
                break
            y = pool_mix(ext, prev.shape[1], w_pool[p], pool_scale[p])
        x = x + rms_norm(y, norm_mix_post[i])
        x = x + rms_norm(sq_relu_mlp(rms_norm(x, norm_ffn_pre[i]), w_up[i], w_down[i]), norm_ffn_post[i])
    return x, ks, vs, tails


def setup_inputs(seed: int = 0) -> dict:
    key = jax.random.key(seed)
    ks = jax.random.split(key, 24)
    n = lambda k, s, sc: jax.random.normal(k, s, jnp.float32) * sc
    return {
        "x_prompt": n(ks[0], (BATCH, SEQ, D_MODEL), 1.0),
        "x_sample": n(ks[1], (DEC_BATCH, DEC_SEQ, D_MODEL), 1.0),
        "cache_k": n(ks[2], (N_ATTN_LAYERS, DEC_BATCH, PAST_LEN, N_HEADS, HD2), 1.0),
        "cache_v": n(ks[3], (N_ATTN_LAYERS, DEC_BATCH, PAST_LEN, N_HEADS, HD2), 1.0),
        "state_pool": n(ks[4], (N_POOL_LAYERS, DEC_BATCH, POOL_STATE, D_MODEL), 1.0),
        "meta_tokens": n(ks[5], (N_META, D_MODEL), 1.0),
        "rel_bias_table": n(ks[6], (N_BUCKETS, N_HEADS), 0.5),
        "norm_mix_pre": 1.0 + n(ks[7], (DEPTH, D_MODEL), 0.05),
        "norm_mix_post": 1.0 + n(ks[8], (DEPTH, D_MODEL), 0.05),
        "norm_ffn_pre": 1.0 + n(ks[9], (DEPTH, D_MODEL), 0.05),
        "norm_ffn_post": 1.0 + n(ks[10], (DEPTH, D_MODEL), 0.05),
        "w_qkv": n(ks[11], (N_ATTN_LAYERS, D_MODEL, 3 * ATTN_WIDTH), D_MODEL ** -0.5),
        "lambda_q1": n(ks[12], (N_ATTN_LAYERS, HEAD_DIM), 0.1),
        "lambda_k1": n(ks[13], (N_ATTN_LAYERS, HEAD_DIM), 0.1),
        "lambda_q2": n(ks[14], (N_ATTN_LAYERS, HEAD_DIM), 0.1),
        "lambda_k2": n(ks[15], (N_ATTN_LAYERS, HEAD_DIM), 0.1),
        "subln_g": 1.0 + n(ks[16], (N_ATTN_LAYERS, HD2), 0.05),
        "w_o": n(ks[17], (N_ATTN_LAYERS, ATTN_WIDTH, D_MODEL), ATTN_WIDTH ** -0.5),
        "w_pool": n(ks[18], (N_POOL_LAYERS, N_POOL_GROUPS, POOL_GROUP, POOL_GROUP), POOL_GROUP ** -0.5),
        "pool_scale": 1.0 + n(ks[19], (N_POOL_LAYERS, D_MODEL), 0.1),
        "w_up": n(ks[20], (DEPTH, D_MODEL, D_FF), D_MODEL ** -0.5),
        "w_down": n(ks[21], (DEPTH, D_FF, D_MODEL), D_FF ** -0.5),
    }


def reference(x_prompt, x_sample, cache_k, cache_v, state_pool, meta_tokens, rel_bias_table,
              norm_mix_pre, norm_mix_post, norm_ffn_pre, norm_ffn_post, w_qkv,
              lambda_q1, lambda_k1, lambda_q2, lambda_k2, subln_g, w_o, w_pool, pool_scale,
              w_up, w_down):
    weights = (rel_bias_table, norm_mix_pre, norm_mix_post, norm_ffn_pre, norm_ffn_post,
               w_qkv, lambda_q1, lambda_k1, lambda_q2, lambda_k2, subln_g, w_o,
               w_pool, pool_scale, w_up, w_down)
    mdt = meta_tokens.dtype

    empty_kv = [jnp.zeros((1, 0, N_HEADS, HD2), mdt) for _ in range(N_ATTN_LAYERS)]
    empty_pool = [jnp.zeros((1, 0, D_MODEL), mdt) for _ in range(N_POOL_LAYERS)]
    _, meta_k, meta_v, meta_tail = trunk(meta_tokens[None], empty_kv, empty_kv,
                                         jnp.zeros((0,), jnp.int32), 0, empty_pool, True, *weights)

    y_prompt, k_p, v_p, tail_p = trunk(x_prompt, meta_k, meta_v, jnp.arange(N_META, dtype=jnp.int32),
                                       N_META, meta_tail, False, *weights)

    past = cache_k.shape[2]
    bd = x_sample.shape[0]
    prev_k_s = [jnp.concatenate([jnp.broadcast_to(meta_k[a], (bd,) + meta_k[a].shape[1:]), cache_k[a]], axis=1)
                for a in range(N_ATTN_LAYERS)]
    prev_v_s = [jnp.concatenate([jnp.broadcast_to(meta_v[a], (bd,) + meta_v[a].shape[1:]), cache_v[a]], axis=1)
                for a in range(N_ATTN_LAYERS)]
    prev_pos_s = jnp.concatenate([jnp.arange(N_META, dtype=jnp.int32) - N_META,
                                  jnp.arange(past, dtype=jnp.int32)])
    y_sample, k_s, v_s, tail_s = trunk(x_sample, prev_k_s, prev_v_s, prev_pos_s, past,
                                       [state_pool[p] for p in range(N_POOL_LAYERS)], False, *weights)

    bp = x_prompt.shape[0]
    k_prompt = jnp.stack([jnp.concatenate([jnp.broadcast_to(meta_k[a], (bp,) + meta_k[a].shape[1:]), k_p[a]], axis=1)
                          for a in range(N_ATTN_LAYERS)])
    v_prompt = jnp.stack([jnp.concatenate([jnp.broadcast_to(meta_v[a], (bp,) + meta_v[a].shape[1:]), v_p[a]], axis=1)
                          for a in range(N_ATTN_LAYERS)])
    pool_prompt = jnp.stack(tail_p)
    k_sample = jnp.stack(k_s)
    v_sample = jnp.stack(v_s)
    pool_sample = jnp.stack(tail_s)
    return (y_prompt, y_sample, k_prompt, v_prompt, pool_prompt, k_sample, v_sample, pool_sample)
```

```python
import math
import numpy as np
import concourse.bass as bass
import concourse.mybir as mybir
from concourse.bass_utils import run_bass_kernel_spmd

F32 = mybir.dt.float32
BF16 = mybir.dt.bfloat16
ALU = mybir.AluOpType
AF = mybir.ActivationFunctionType
AX = mybir.AxisListType
AP = bass.AP

NCORES = 8
D = 1024
NH = 8
DFF = 4096
SEQ = 8192
TP = 512
NT_P = SEQ // TP
NMETA = 16
DEC = 64
PAST = 1024
EPS = 1e-6
SUBLN_EPS = 1e-5
LAM_INIT = [0.8 - 0.6 * math.exp(-0.3 * 0), 0.8 - 0.6 * math.exp(-0.3 * 2)]
ZL = 384
NEG = -30000.0
KC = 1024
KTS_LEN = 1152


def _t5_bucket_np(rel):
    nb = 16
    max_exact = 8
    rel = np.asarray(rel, np.int32)
    ret = np.where(rel > 0, nb, 0).astype(np.int32)
    n = np.abs(rel)
    nf = np.maximum(n, 1).astype(np.float32)
    large = max_exact + (np.log(nf / np.float32(max_exact)) / np.float32(math.log(128 / max_exact))
                         * np.float32(nb - max_exact)).astype(np.int32)
    for nn, bb in ((16, 10), (32, 12), (64, 14)):
        large = np.where(n == nn, bb, large)
    large = np.minimum(large, nb - 1)
    return ret + np.where(n < max_exact, n, large)


def _onehot_const():
    rel = 127 - np.arange(ZL)
    bk = _t5_bucket_np(rel)
    oh = np.zeros((32, ZL), np.float32)
    oh[bk, np.arange(ZL)] = 1.0
    return oh


def _invcnt_const():
    inv = np.zeros((4, 16), np.float32)
    for g, w in enumerate((2, 4, 8, 16)):
        for t in range(16):
            inv[g, t] = 1.0 / min(t + 1, w)
    return inv


class Res:
    __slots__ = ("name", "lw", "rd")

    def __init__(self, name):
        self.name = name
        self.lw = None
        self.rd = []


class Op:
    __slots__ = ("eng", "fn", "deps", "sig", "sem", "val", "dma", "key", "n")


class Sched:
    ENGS = ("sp", "act", "dve", "pool", "pe")
    NROT = 4

    def __init__(self, nc):
        self.nc = nc
        self.ops = []
        self.out_ops = []

    def add(self, eng, fn, reads=(), writes=(), dma=False, key=None, is_out=False):
        op = Op()
        op.eng = eng
        op.fn = fn
        op.dma = dma
        op.key = key
        op.sig = False
        op.sem = None
        op.val = 0
        op.n = len(self.ops)
        deps = set()
        for r in reads:
            if r.lw is not None:
                deps.add(r.lw)
        for w in writes:
            if w.lw is not None:
                deps.add(w.lw)
            deps.update(w.rd)
        for r in reads:
            r.rd.append(op)
        for w in writes:
            w.lw = op
            w.rd = []
        deps.discard(op)
        op.deps = deps
        self.ops.append(op)
        if is_out:
            self.out_ops.append(op)
        return op

    def finalize_and_emit(self, sems_pool):
        nc = self.nc
        fin = self.add("sp", None)
        fin.deps = set(self.out_ops)
        for op in self.ops:
            for d in op.deps:
                if d.eng == "pe" and op.eng == "pe" and not d.dma:
                    continue
                d.sig = True
        gk = set(op.key for op in self.ops if op.dma and op.sig and op.key.startswith("g:"))
        for op in self.ops:
            if op.dma and op.key in gk:
                op.sig = True
        semi = iter(sems_pool)
        eng_sems = {e: [next(semi) for _ in range(self.NROT)] for e in self.ENGS if e != "sp"}
        eng_cnt = {e: 0 for e in self.ENGS}
        dma_sems = {}
        dma_cnt = {}
        for op in self.ops:
            if not op.sig:
                continue
            if op.dma:
                k = op.key
                if k not in dma_sems:
                    dma_sems[k] = next(semi)
                    dma_cnt[k] = 0
                dma_cnt[k] += 16
                op.sem = dma_sems[k]
                op.val = dma_cnt[k]
            else:
                n = eng_cnt[op.eng]
                eng_cnt[op.eng] = n + 1
                op.sem = eng_sems[op.eng][n % self.NROT]
                op.val = n // self.NROT + 1
        self.n_sems = 4 * self.NROT + len(dma_sems)
        for op in self.ops:
            if op.dma and op.sig and op.key.startswith("g:"):
                op.val = dma_cnt[op.key]

        def emit(engname, eng):
            known = {}
            for op in self.ops:
                if op.eng != engname:
                    continue
                need = {}
                for d in op.deps:
                    if not d.sig:
                        continue
                    if d.eng == "pe" and engname == "pe" and not d.dma:
                        continue
                    if op.dma and d.dma and op.key == d.key and op.key.startswith("g:"):
                        continue
                    if need.get(d.sem, 0) < d.val:
                        need[d.sem] = d.val
                for s, v in need.items():
                    if known.get(s, 0) < v:
                        eng.wait_ge(s, v)
                        known[s] = v
                if op.fn is None:
                    continue
                inst = op.fn(eng)
                if op.sig:
                    inst.then_inc(op.sem, 16 if op.dma else 1)

        with nc.Block() as block:
            @block.sync
            def _(e):
                emit("sp", e)

            @block.scalar
            def _(e):
                emit("act", e)

            @block.vector
            def _(e):
                emit("dve", e)

            @block.gpsimd
            def _(e):
                emit("pool", e)

            @block.tensor
            def _(e):
                emit("pe", e)


def build_program(stop_after=None, dbg=False):
    nc = bass.Bass("TRN2", target_bir_lowering=False)
    S = Sched(nc)

    def din(name, shape):
        return nc.dram_tensor(name, list(shape), F32, kind="ExternalInput")

    def dout(name, shape):
        return nc.dram_tensor(name, list(shape), F32, kind="ExternalOutput")

    xp = din("xp", [SEQ, D]); xs = din("xs", [DEC, D])
    ck = din("ck", [2, PAST, D]); cv = din("cv", [2, PAST, D]); stp = din("stp", [2, 15, D])
    meta = din("meta", [NMETA, D]); rel = din("rel", [32, 8])
    nmp = din("nmp", [4, D]); nmo = din("nmo", [4, D]); nfp = din("nfp", [4, D]); nfo = din("nfo", [4, D])
    wqkv = din("wqkv", [2, D, 3 * D])
    lq1 = din("lq1", [2, 64]); lk1 = din("lk1", [2, 64]); lq2 = din("lq2", [2, 64]); lk2 = din("lk2", [2, 64])
    subg = din("subg", [2, 128]); wo = din("wo", [2, D, D]); wpool = din("wpool", [2, 4, 256, 256])
    pscale = din("pscale", [2, D]); wup = din("wup", [4, D, DFF]); wdn = din("wdn", [4, DFF, D])
    ohc = din("ohc", [32, ZL]); invc = din("invc", [4, 16]); eye = din("eye", [128, 128])

    yp = dout("yp", [SEQ, D]); ys = dout("ys", [DEC, D])
    kp = dout("kp", [2, NMETA + SEQ, D]); vp = dout("vp", [2, NMETA + SEQ, D]); pp = dout("pp", [2, 15, D])
    kso = dout("kso", [2, DEC, D]); vso = dout("vso", [2, DEC, D]); pso = dout("pso", [2, 15, D])

    NBLK = 2 * 8 + 2 * 1 + 4 * 16
    wsc = nc.dram_tensor("wsc", [NBLK, 128, 4096], BF16)
    KTd = nc.dram_tensor("KTd", [2, NH, 128, SEQ], BF16)
    Vd = nc.dram_tensor("Vd", [2, SEQ, D], BF16)
    KTs = nc.dram_tensor("KTs", [2, NH, 128, KTS_LEN], BF16)
    Vs = nc.dram_tensor("Vs", [2, KTS_LEN, D], BF16)
    Gd = nc.dram_tensor("Gd", [8, ZL], BF16)
    Zb = nc.dram_tensor("Zb", [8, 128, ZL], BF16)

    def blk_attn(a, j):
        return a * 8 + j

    def blk_pool(p):
        return 16 + p

    def blk_up(l, j):
        return 18 + l * 16 + j

    def blk_dn(l, c, jj):
        return 18 + l * 16 + 8 + c * 4 + jj

    import contextlib
    es = contextlib.ExitStack()

    def sb(name, shape, dt):
        return es.enter_context(nc.sbuf_tensor(name, list(shape), dt))

    with es:
        ident = sb("ident", [128, 128], F32)
        identb = sb("identb", [128, 128], BF16)
        onesb = sb("onesb", [128, 128], BF16)
        cH = sb("cH", [128, 8], F32)
        D0 = sb("D0", [128, 8, 256], BF16)
        Dp = sb("Dp", [128, 8, 128], BF16)
        Dm = sb("Dm", [16, 8, 128], BF16)
        neglam = sb("neglam", [128, 2], F32)
        sg = sb("sg", [128, 2], F32)
        gpre = sb("gpre", [128, 8, 8], F32)
        gpost = sb("gpost", [128, 8, D], BF16)
        invct = sb("invct", [128, 4, 16], F32)
        KTm = sb("KTm", [128, 2, NH, NMETA], BF16)
        Vm = sb("Vm", [NMETA, 2, D], BF16)
        stash = sb("stash", [128, 3, 2, 8, 15], F32)
        R_const = Res("const")
        R_KTm = [Res("KTm0"), Res("KTm1")]
        R_Vm = [Res("Vm0"), Res("Vm1")]
        R_stash = [[Res("st%d%d" % (k, p)) for p in range(2)] for k in range(3)]

        psA = es.enter_context(nc.psum_tensor("psA", [128, 4, 512], F32))
        psB = es.enter_context(nc.psum_tensor("psB", [128, 2, 1024], F32))
        R_A = [Res("psA%d" % i) for i in range(4)]
        R_B = [Res("psB%d" % i) for i in range(2)]
        cntA = [0]

        def nextA():
            i = cntA[0] % 4
            cntA[0] += 1
            return i

        def dmaop(q, out_ap, in_ap, reads, writes, key, is_out=False, nonc=False):
            def fn(e, out_ap=out_ap, in_ap=in_ap, nonc=nonc):
                if nonc:
                    return e.dma_start(out=out_ap, in_=in_ap, allow_slow_non_contiguous=True)
                return e.dma_start(out=out_ap, in_=in_ap)
            return S.add(q, fn, reads, writes, dma=True, key=key, is_out=is_out)

        esA = contextlib.ExitStack()
        with esA:
            def sbA(name, shape, dt):
                return esA.enter_context(nc.sbuf_tensor(name, list(shape), dt))

            tab = sbA("tab", [32, 8], F32)
            tab15 = sbA("tab15", [32, 8], F32)
            oht = sbA("oht", [32, ZL], F32)
            Gs = sbA("Gs", [8, ZL], BF16)
            L4 = sbA("L4", [128, 4, 128], F32)
            prod = sbA("prod", [128, 2, 128], F32)
            dots = sbA("dots", [128, 4], F32)
            pscl = sbA("pscl", [128, 2, 8, 256], F32)
            gtmp = sbA("gtmp", [128, 4 * D], F32)
            R_t = Res("tab")
            R_l = Res("lam")
            G0 = "g:c0"
            dmaop("sp", ident[:, :], eye.ap(), [], [R_const], G0)
            dmaop("sp", cH[:, :], AP(rel, 15 * 8, [[0, 128], [1, 8]]), [], [R_const], G0)
            dmaop("sp", tab[:, :], rel.ap(), [], [R_t], G0)
            dmaop("sp", tab15[:, :], AP(rel, 15 * 8, [[0, 32], [1, 8]]), [], [R_t], G0)
            dmaop("sp", oht[:, :], ohc.ap(), [], [R_t], G0)
            for i, t in enumerate((lq1, lk1, lq2, lk2)):
                dmaop("sp", L4[:, i, :], AP(t, 0, [[0, 128], [1, 128]]), [], [R_l], G0)
            dmaop("sp", sg[:, :], AP(subg, 0, [[1, 128], [128, 2]]), [], [R_l], G0, nonc=True)
            dmaop("sp", gpre[:, 0:4, :], AP(nmp, 0, [[1, 128], [D, 4], [128, 8]]), [], [R_const], G0, nonc=True)
            dmaop("sp", gpre[:, 4:8, :], AP(nfp, 0, [[1, 128], [D, 4], [128, 8]]), [], [R_const], G0, nonc=True)
            dmaop("sp", invct[:, :, :], AP(invc, 0, [[0, 128], [1, 64]]), [], [R_const], G0)
            for p in range(2):
                for g in range(4):
                    dmaop("sp", pscl[:, p, 2 * g:2 * g + 2, :], AP(pscale, p * D + 256 * g, [[0, 128], [0, 2], [1, 256]]),
                          [], [R_const], G0)
            S.add("dve", lambda e: e.tensor_copy(out=identb[:, :], in_=ident[:, :]), [R_const], [R_const])
            S.add("pool", lambda e: e.memset(onesb[:, :], 1.0), [], [R_const])
            S.add("dve", lambda e: e.tensor_tensor(out=tab[:, :], in0=tab[:, :], in1=tab15[:, :], op=ALU.subtract),
                  [R_t], [R_t])
            S.add("pe", lambda e: e.matmul(psA[0:8, 0, 0:ZL], lhsT=tab[:, :], rhs=oht[:, :], start=True, stop=True),
                  [R_t], [R_A[0]])
            R_g = Res("Gs")
            S.add("dve", lambda e: e.tensor_copy(out=Gs[:, :], in_=psA[0:8, 0, 0:ZL]), [R_A[0]], [R_g])
            R_gd = Res("Gd")
            dmaop("pool", Gd.ap(), Gs[:, :], [R_g], [R_gd], "c1")
            R_z = Res("Zb")
            dmaop("pool", Zb.ap(), AP(Gd, 0, [[ZL, 8], [0, 128], [1, ZL]]), [R_gd], [R_z], "c1")
            R_D = Res("Dtiles")
            for h in range(NH):
                dmaop("sp", D0[:, h, :], AP(Zb, h * 128 * ZL + 127, [[ZL - 1, 128], [1, 256]]), [R_z], [R_D], "g:c2")
                dmaop("sp", Dp[:, h, :], AP(Zb, h * 128 * ZL + 255, [[ZL - 1, 128], [1, 128]]), [R_z], [R_D], "g:c2")
                dmaop("sp", Dm[:, h, :], AP(Zb, h * 128 * ZL + 143, [[ZL - 1, 16], [1, 128]]), [R_z], [R_D], "g:c2")
            S.add("pool", lambda e: e.memset(D0[64:128, :, 0:64], NEG), [R_D], [R_D, R_const])
            S.add("dve", lambda e: e.tensor_tensor(out=prod[:, 0, :], in0=L4[:, 0, :], in1=L4[:, 1, :], op=ALU.mult),
                  [R_l], [R_l])
            S.add("dve", lambda e: e.tensor_tensor(out=prod[:, 1, :], in0=L4[:, 2, :], in1=L4[:, 3, :], op=ALU.mult),
                  [R_l], [R_l])
            S.add("dve", lambda e: e.reduce_sum(out=dots[:, :], in_=prod[:, :, :].rearrange("p a (b c) -> p (a b) c", c=64),
                                                axis=AX.X), [R_l], [R_l])
            S.add("act", lambda e: e.activation(out=dots[:, :], in_=dots[:, :], func=AF.Exp), [R_l], [R_l])
            S.add("dve", lambda e: e.tensor_tensor(out=neglam[:, :], in0=dots[:, 2:4], in1=dots[:, 0:2], op=ALU.subtract),
                  [R_l], [R_l])
            for a in range(2):
                S.add("dve", lambda e, a=a: e.tensor_scalar(out=neglam[:, a:a + 1], in0=neglam[:, a:a + 1],
                                                            scalar1=-LAM_INIT[a], scalar2=None, op0=ALU.add),
                      [R_l], [R_l, R_const])
            for a in range(2):
                S.add("dve", lambda e, a=a: e.tensor_scalar(out=sg[:, a:a + 1], in0=sg[:, a:a + 1],
                                                            scalar1=1.0 - LAM_INIT[a], scalar2=None, op0=ALU.mult),
                      [R_l], [R_l, R_const])
            R_gt = Res("gtmp")
            for i, t in enumerate((nmo, nfo)):
                dmaop("sp", gtmp[:, :], AP(t, 0, [[0, 128], [1, 4 * D]]), [], [R_gt], "c3")
                S.add("dve", lambda e, i=i: e.tensor_copy(out=gpost[:, 4 * i:4 * i + 4, :], in_=gtmp[:, :]),
                      [R_gt], [R_const])

            cin = [sbA("cin%d" % i, [128, 8, 512], F32) for i in range(2)]
            cout = [sbA("cout%d" % i, [128, 8, 512], BF16) for i in range(2)]
            R_cin = [Res("cin0"), Res("cin1")]
            R_cout = [Res("cout0"), Res("cout1")]
            R_wsc = Res("wsc")
            conv = []
            for a in range(2):
                for j in range(6):
                    conv.append((blk_attn(a, j), AP(wqkv, a * D * 3 * D + 512 * j, [[3 * D, 128], [128 * 3 * D, 8], [1, 512]]),
                                 "row", 2 * a))
                for c in range(2):
                    conv.append((blk_attn(a, 6 + c), AP(wo, a * D * D + 512 * c, [[D, 128], [128 * D, 8], [1, 512]]),
                                 "plain", None))
            for p in range(2):
                conv.append((blk_pool(p), AP(wpool, p * 4 * 256 * 256, [[256, 128], [128 * 256, 8], [1, 256]]), "pool", p))
            for l in range(4):
                for j in range(8):
                    conv.append((blk_up(l, j), AP(wup, l * D * DFF + 512 * j, [[DFF, 128], [128 * DFF, 8], [1, 512]]),
                                 "row", 4 + l))
                for c in range(2):
                    for jj in range(4):
                        conv.append((blk_dn(l, c, jj), AP(wdn, l * DFF * D + (8 * jj * 128) * D + 512 * c,
                                                          [[D, 128], [128 * D, 8], [1, 512]]), "plain", None))
            cengs = ("dve", "act")
            for ci, (blk, src, kind, arg) in enumerate(conv):
                sl = ci % 2
                w = 256 if kind == "pool" else 512
                dmaop("sp", cin[sl][:, :, 0:w], src, [], [R_cin[sl]], "cin%d" % sl)
                ce = cengs[ci % 2]
                if kind == "row":
                    def fn(e, sl=sl, arg=arg, ce=ce):
                        inst = None
                        for kc in range(8):
                            if ce == "act":
                                inst = e.activation(out=cout[sl][:, kc, :], in_=cin[sl][:, kc, :], func=AF.Copy,
                                                    scale=gpre[:, arg, kc:kc + 1])
                            else:
                                inst = e.tensor_scalar(out=cout[sl][:, kc, :], in0=cin[sl][:, kc, :],
                                                       scalar1=gpre[:, arg, kc:kc + 1], scalar2=None, op0=ALU.mult)
                        return inst
                elif kind == "plain":
                    def fn(e, sl=sl, ce=ce):
                        if ce == "act":
                            return e.activation(out=cout[sl][:, :, :], in_=cin[sl][:, :, :], func=AF.Copy)
                        return e.tensor_copy(out=cout[sl][:, :, :], in_=cin[sl][:, :, :])
                else:
                    ce = "dve"

                    def fn(e, sl=sl, arg=arg):
                        return e.tensor_tensor(out=cout[sl][:, :, 0:256], in0=cin[sl][:, :, 0:256],
                                               in1=pscl[:, arg, :, :], op=ALU.mult)
                S.add(ce, fn, [R_cin[sl], R_const], [R_cout[sl]])
                dst = AP(wsc, blk * 128 * 4096, [[4096, 128], [w, 8], [1, w]])
                dmaop("pool", dst, cout[sl][:, :, 0:w], [R_cout[sl]], [R_wsc], "cout%d" % sl)

            R_KTs = Res("KTs")
            R_Vs = Res("Vs")
            ctmp = [sbA("ctmp%d" % i, [128, D], F32) for i in range(2)]
            cstg = [sbA("cstg%d" % i, [128, D], BF16) for i in range(2)]
            R_ct = [Res("ct0"), Res("ct1")]
            R_cs = [Res("cs0"), Res("cs1")]
            cc = 0
            for a in range(2):
                for r in range(PAST // 128):
                    sl = cc % 2
                    cc += 1
                    dmaop("sp", ctmp[sl][:, :], AP(ck, a * PAST * D + r * 128 * D, [[D, 128], [1, D]]), [], [R_ct[sl]],
                          "ct%d" % sl)
                    for half in range(2):
                        ai = nextA()

                        def tfn(e, sl=sl, half=half, ai=ai):
                            inst = None
                            for c4 in range(4):
                                c = half * 4 + c4
                                inst = e.transpose(out=psA[:, ai, c4 * 128:(c4 + 1) * 128],
                                                   in_=ctmp[sl][:, c * 128:(c + 1) * 128], identity=ident[:, :])
                            return inst
                        S.add("pe", tfn, [R_ct[sl], R_const], [R_A[ai]])
                        if half == 0:
                            S.add("dve", lambda e, sl=sl, ai=ai: e.tensor_copy(out=cstg[sl][:, 0:512], in_=psA[:, ai, :]),
                                  [R_A[ai]], [R_cs[sl]])
                        else:
                            S.add("act", lambda e, sl=sl, ai=ai: e.activation(out=cstg[sl][:, 512:1024], in_=psA[:, ai, :],
                                                                             func=AF.Copy), [R_A[ai]], [R_cs[sl]])
                    dmaop("pool", AP(KTs, a * NH * 128 * KTS_LEN + r * 128, [[KTS_LEN, 128], [128 * KTS_LEN, 8], [1, 128]]),
                          cstg[sl][:, :].rearrange("p (h k) -> p h k", h=8), [R_cs[sl]], [R_KTs], "cs%d" % sl)
                    sl = cc % 2
                    cc += 1
                    dmaop("sp", ctmp[sl][:, :], AP(cv, a * PAST * D + r * 128 * D, [[D, 128], [1, D]]), [], [R_ct[sl]],
                          "ct%d" % sl)
                    S.add("act", lambda e, sl=sl: e.activation(out=cstg[sl][:, :], in_=ctmp[sl][:, :], func=AF.Copy),
                          [R_ct[sl]], [R_cs[sl]])
                    dmaop("pool", AP(Vs, a * KTS_LEN * D + r * 128 * D, [[D, 128], [1, D]]), cstg[sl][:, :],
                          [R_cs[sl]], [R_Vs], "cs%d" % sl)

            R_bar = Res("barrier")
            allA = list(S.ops)
            bar_ops = []
            for en in ("sp", "act", "dve", "pool", "pe"):
                o = S.add(en, None)
                o.deps = set(allA)
                bar_ops.append(o)

        xb = sb("xb", [128, 4, D], F32)
        xh = [sb("xh%d" % i, [128, D], F32) for i in range(2)]
        hT = sb("hT", [128, 8, TP], BF16)
        OT = hT
        uT = sb("uT", [128, 32, TP], BF16)
        KTc = uT[:, 0:8, :]
        QT = uT[:, 8:16, :]
        NW = 4
        wr = [sb("wr%d" % i, [128, 8, 512], BF16) for i in range(NW)]
        NKV = 3
        kvK = [sb("kvK%d" % i, [128, KC], BF16) for i in range(NKV)]
        kvV = [sb("kvV%d" % i, [128, KC // 128, 128], BF16) for i in range(NKV)]
        NP_ = 3
        Pt = [sb("Pt%d" % i, [128, 2, TP], BF16) for i in range(NP_)]
        stg = [sb("stg%d" % i, [128, D], F32) for i in range(2)]
        stgb = [sb("stgb%d" % i, [128, D], BF16) for i in range(2)]
        ext = sb("ext", [128, 8, 15 + TP], F32)
        sA_ = sb("sA_", [128, 2, 15 + TP], F32)
        sB_ = sb("sB_", [128, 2, 15 + TP], F32)
        dT = hT
        ptmp = sb("ptmp", [128, D], F32)
        ptmp2 = sb("ptmp2", [128, D], F32)
        xhb = [sb("xhb%d" % i, [128, D], BF16) for i in range(2)]
        fo = sb("fo", [128, 2, TP], F32)
        rtmp2 = sb("rtmp2", [128, 2, TP], F32)
        rtmp = [rtmp2[:, 0, :], rtmp2[:, 1, :]]
        fin12 = rtmp2
        small = sb("small", [128, 64], F32)
        junk = sb("junk", [128, D], BF16)
        fin1 = rtmp[0]
        fin2 = rtmp[1]
        fin3 = ptmp
        finb = junk

        R_x = [[Res("xL%d" % i), Res("xR%d" % i)] for i in range(4)]
        R_xflat = [r for pr in R_x for r in pr]
        R_xh = [Res("xh0"), Res("xh1")]
        R_hT = Res("hT")
        R_QT = [Res("QT%d" % i) for i in range(NH)]
        R_OT = [Res("OT%d" % i) for i in range(NH)]
        R_uT = [Res("uT%d" % i) for i in range(32)]
        R_KTc = R_uT[0:8]
        R_wr = [Res("wr%d" % i) for i in range(NW)]
        R_kv = [Res("kv%d" % i) for i in range(NKV)]
        R_kvV = [Res("kvV%d" % i) for i in range(NKV)]
        R_P = [Res("P%d" % i) for i in range(NP_)]
        R_stg = [Res("stg0"), Res("stg1")]
        R_stgb = [Res("stgb0"), Res("stgb1")]
        R_ext = Res("ext")
        R_sA = Res("sA")
        R_sB = Res("sB")
        R_dT = R_hT
        R_fin = Res("fin")
        R_fo = Res("fo")
        pend = [None]
        R_ptmp = Res("ptmp")
        R_ptmp2 = Res("ptmp2")
        R_xhb = [Res("xhb0"), Res("xhb1")]
        R_rt = [Res("rt0"), Res("rt1")]
        R_small = [Res("small%d" % i) for i in range(64)]
        R_junk = Res("junk")
        R_KTd = [[Res("KTd%d_%d" % (a_, t_)) for t_ in range(NT_P)] for a_ in range(2)]
        R_Vd = [[Res("Vd%d_%d" % (a_, t_)) for t_ in range(NT_P)] for a_ in range(2)]
        R_KTsn = [Res("KTsn0"), Res("KTsn1")]
        R_Vsn = [Res("Vsn0"), Res("Vsn1")]
        R_out = Res("out")

        cnt = {"stg": 0, "stgb": 0, "xh": 0, "P": 0, "small": 0, "rt": 0, "B": 0, "xhb": 0, "ptmp": 0, "small4": 0}

        def rot(name, n):
            i = cnt[name] % n
            cnt[name] += 1
            return i

        wplan = []
        wstate = {"issued": 0, "use": 0}

        def wload_upto(k):
            while wstate["issued"] < min(k, len(wplan)):
                i = wstate["issued"]
                blk = wplan[i]
                sl = i % NW
                w = 256 if blk in (blk_pool(0), blk_pool(1)) else 512
                dmaop("sp", wr[sl][:, :, 0:w], AP(wsc, blk * 128 * 4096, [[4096, 128], [w, 8], [1, w]]),
                      [R_wsc], [R_wr[sl]], "wr%d" % sl)
                wstate["issued"] += 1

        def wuse(blk):
            i = wstate["use"]
            assert wplan[i] == blk, (i, wplan[i], blk)
            wload_upto(i + NW - 1)
            wstate["use"] += 1
            return wr[i % NW], R_wr[i % NW]

        def rstd_from_ssq(ssq_ap, out_ap, n, eps, rs, nparts):
            S.add("act", lambda e: e.activation(out=out_ap, in_=ssq_ap, func=AF.Ln, scale=1.0 / n, bias=eps_t[0:nparts, eps:eps + 1]),
                  rs + [R_const], rs)
            S.add("act", lambda e: e.activation(out=out_ap, in_=out_ap, func=AF.Exp, scale=-0.5), rs, rs)

        eps_t = sb("eps_t", [128, 2], F32)
        S.add("pool", lambda e: e.memset(eps_t[:, 0:1], EPS), [], [R_const])
        S.add("pool", lambda e: e.memset(eps_t[:, 1:2], SUBLN_EPS), [], [R_const])

        def prenorm_T(nt, dst, R_dst_list, scale_vec=None, ext_mode=False, tail_out=None):
            nsub = (nt + 127) // 128
            rows0 = min(128, nt)
            sb0 = 48 + 4 * (cnt["small4"] % 4)
            cnt["small4"] += 1
            rsm = [R_small[sb0 + s] for s in range(nsub)]
            for s in range(nsub):
                rows = min(128, nt - s * 128)
                smc = small[0:rows, sb0 + s:sb0 + s + 1]
                S.add("act", lambda e, s=s, rows=rows, smc=smc: e.activation(out=junk[0:rows, :], in_=xb[0:rows, s, :],
                                                                             func=AF.Square, accum_out=smc),
                      R_x[s], [R_junk, R_small[sb0 + s]])
                rstd_from_ssq(smc, smc, D, 0, [R_small[sb0 + s]], rows)
            if scale_vec is None:
                xis = {}

                def emit_mult(s):
                    rows = min(128, nt - s * 128)
                    sm = small[0:rows, sb0 + s:sb0 + s + 1]
                    xi = rot("xhb", 2)
                    xis[s] = xi
                    S.add("dve", lambda e, s=s, rows=rows, sm=sm, xi=xi: e.tensor_scalar(
                        out=xhb[xi][0:rows, :], in0=xb[0:rows, s, :], scalar1=sm, scalar2=None, op0=ALU.mult),
                        R_x[s] + [R_small[sb0 + s]], [R_xhb[xi]])
                emit_mult(0)
                for s in range(nsub):
                    rows = min(128, nt - s * 128)
                    if s + 1 < nsub:
                        emit_mult(s + 1)
                    xi = xis[s]
                    ai = nextA()
                    pv = psA[:, ai, :].bitcast(BF16)

                    def tfn(e, xi=xi, rows=rows, pv=pv):
                        inst = None
                        for c in range(8):
                            inst = e.transpose(out=pv[:, c * 128:c * 128 + rows], in_=xhb[xi][0:rows, c * 128:(c + 1) * 128],
                                               identity=identb[0:rows, 0:rows])
                        return inst
                    S.add("pe", tfn, [R_xhb[xi], R_const], [R_A[ai]])
                    src_ = pv.rearrange("p (c k) -> p c k", c=8)[:, :, 0:rows]
                    d_ap = dst(0, s, rows, 8)
                    if s % 2 == 0:
                        S.add("dve", lambda e, src_=src_, d_ap=d_ap: e.tensor_copy(out=d_ap, in_=src_), [R_A[ai]], R_dst_list)
                    else:
                        S.add("act", lambda e, src_=src_, d_ap=d_ap: e.activation(out=d_ap, in_=src_, func=AF.Copy),
                              [R_A[ai]], R_dst_list)
                return
            for s in range(nsub):
                rows = min(128, nt - s * 128)
                sm = small[0:rows, sb0 + s:sb0 + s + 1]
                xi = rot("xh", 2)
                S.add("dve", lambda e, s=s, rows=rows, sm=sm, xi=xi: e.tensor_scalar(
                    out=xh[xi][0:rows, :], in0=xb[0:rows, s, :], scalar1=sm, scalar2=None, op0=ALU.mult),
                    R_x[s] + [R_small[sb0 + s]], [R_xh[xi]])
                if tail_out is not None and s == nsub - 1:
                    dram_ap, gi, is_out = tail_out
                    lo = rows - 32
                    dmaop("sp", ptmp[lo:rows, :], AP(nmp, (2 * gi + 1) * D, [[0, 32], [1, D]]), [], [R_ptmp], "gtail")
                    S.add("dve", lambda e, rows=rows, xi=xi, lo=lo: e.tensor_tensor(
                        out=ptmp[lo:rows, :], in0=xh[xi][lo:rows, :], in1=ptmp[lo:rows, :], op=ALU.mult),
                        [R_xh[xi], R_ptmp], [R_ptmp])
                    dmaop("pool", dram_ap, ptmp[rows - 15:rows, :], [R_ptmp], [R_out], "tail", is_out=is_out)
                for half in range(2):
                    ai = nextA()

                    def tfn(e, xi=xi, rows=rows, half=half, ai=ai):
                        inst = None
                        for c4 in range(4):
                            c = half * 4 + c4
                            inst = e.transpose(out=psA[:, ai, c4 * 128:c4 * 128 + rows], in_=xh[xi][0:rows, c * 128:(c + 1) * 128],
                                               identity=ident[0:rows, 0:rows])
                        return inst
                    S.add("pe", tfn, [R_xh[xi], R_const], [R_A[ai]])
                    src_ = psA[:, ai, :].rearrange("p (c k) -> p c k", c=4)[:, :, 0:rows]
                    d_ap = dst(half * 4, s, rows, 4)

                    def fn(e, src_=src_, d_ap=d_ap, half=half):
                        inst = None
                        for c4 in range(4):
                            inst = e.tensor_scalar(out=d_ap[:, c4, :], in0=src_[:, c4, :],
                                                   scalar1=gpre[:, scale_vec, half * 4 + c4:half * 4 + c4 + 1],
                                                   scalar2=None, op0=ALU.mult)
                        return inst
                    S.add("dve", fn, [R_A[ai], R_const], R_dst_list)

        def hT_dst(c0, s, rows, n=4):
            return hT[:, c0:c0 + n, s * 128:s * 128 + rows]

        def ext_dst(c0, s, rows, n=4):
            return ext[:, c0:c0 + n, 15 + s * 128:15 + s * 128 + rows]

        dbgs = []

        def dbg_x(tag, nt):
            if not dbg:
                return
            t_ = nc.dram_tensor("dbg_%s" % tag, [nt, D], F32, kind="ExternalOutput")
            dmaop("pool", t_.ap(), xb[0:nt, 0, :], R_x[0], [R_out], "dbg", is_out=True)

        def dump(tag, ap, reads):
            if not dbg:
                return
            t_ = nc.dram_tensor("dbg_%s" % tag, list(ap.shape), ap.dtype, kind="ExternalOutput")
            dmaop("pool", t_.ap(), ap, reads, [R_out], "dbg", is_out=True)

        def postnorm_add(nt, s, rows, bi, gidx):
            si = rot("small", 48)
            sm = small[0:rows, si:si + 1]
            S.add("act", lambda e: e.activation(out=junk[0:rows, :], in_=psB[0:rows, bi, :], func=AF.Square, accum_out=sm),
                  [R_B[bi]], [R_junk, R_small[si]])
            rstd_from_ssq(sm, sm, D, 0, [R_small[si]], rows)
            pi_ = rot("ptmp", 2)
            pt, Rpt = (ptmp, R_ptmp) if pi_ == 0 else (ptmp2, R_ptmp2)
            S.add("dve", lambda e: e.scalar_tensor_tensor(out=pt[0:rows, :], in0=psB[0:rows, bi, :], scalar=sm,
                                                           in1=gpost[0:rows, gidx, :], op0=ALU.mult, op1=ALU.mult),
                  [R_B[bi], R_small[si], R_const], [Rpt])
            S.add("dve", lambda e: e.tensor_tensor(out=xb[0:rows, s, 0:512], in0=xb[0:rows, s, 0:512], in1=pt[0:rows, 0:512],
                                                    op=ALU.add), [Rpt, R_x[s][0]], [R_x[s][0]])
            S.add("pool", lambda e: e.tensor_tensor(out=xb[0:rows, s, 512:1024], in0=xb[0:rows, s, 512:1024],
                                                     in1=pt[0:rows, 512:1024], op=ALU.add), [Rpt, R_x[s][1]], [R_x[s][1]])

        def proj_fm(wt, cc, src, nt, ai):
            def fn(e):
                inst = None
                for kc in range(8):
                    inst = e.matmul(psA[:, ai, 0:nt], lhsT=wt[:, kc, cc * 128:(cc + 1) * 128], rhs=src[:, kc, 0:nt],
                                    start=(kc == 0), stop=(kc == 7))
                return inst
            return fn

        def proj_tm(wt, src, s, rows, bi, half, kcs, first, last, srcoff=0):
            def fn(e):
                inst = None
                for i, kc in enumerate(kcs):
                    inst = e.matmul(psB[0:rows, bi, half * 512:(half + 1) * 512],
                                    lhsT=src[:, srcoff + kc, s * 128:s * 128 + rows], rhs=wt[:, kc, :],
                                    start=(first and i == 0), stop=(last and i == len(kcs) - 1))
                return inst
            return fn

        def attention_head(a, h, nt, segs, loads=()):
            n = len(segs)
            sbank = [None] * n

            lstate = [0]

            def emit_qk(i):
                sg_ = segs[i]
                c_ = sg_.get("chunk")
                if c_ is not None:
                    while lstate[0] < min(c_ + 2, len(loads)):
                        loads[lstate[0]]()
                        lstate[0] += 1
                nk, q0 = sg_["nk"], sg_["q0"]
                b2 = (cnt["B"] % 2) * 2
                cnt["B"] += 1
                sbank[i] = b2

                def fn(e):
                    hasb = sg_["bias"] is not None
                    e.matmul(psA[0:nk, b2, q0:nt], lhsT=sg_["kt"][0:64, :], rhs=QT[0:64, h, q0:nt], start=True, stop=not hasb)
                    inst = e.matmul(psA[0:nk, b2 + 1, q0:nt], lhsT=sg_["kt"][64:128, :], rhs=QT[64:128, h, q0:nt],
                                    start=True, stop=not hasb)
                    if hasb:
                        dt_, c0 = sg_["bias"]
                        nb = min(dt_.shape[-1], nt - c0)
                        e.matmul(psA[0:nk, b2, c0:c0 + nb], lhsT=identb[0:nk, 0:nk], rhs=dt_[:, 0:nb], start=False, stop=True)
                        inst = e.matmul(psA[0:nk, b2 + 1, c0:c0 + nb], lhsT=identb[0:nk, 0:nk], rhs=dt_[:, 0:nb],
                                        start=False, stop=True)
                    return inst
                S.add("pe", fn, [R_QT[h], R_const] + sg_["res"], [R_A[b2], R_A[b2 + 1]])

            def emit_exp_pv(i):
                sg_ = segs[i]
                nk, q0 = sg_["nk"], sg_["q0"]
                b2 = sbank[i]
                pi = rot("P", NP_)
                S.add("act", lambda e: e.activation(out=Pt[pi][0:nk, :, q0:nt], in_=psA[0:nk, b2:b2 + 2, q0:nt],
                                                    func=AF.Exp, bias=cH[0:nk, h:h + 1]),
                      [R_A[b2], R_A[b2 + 1], R_const], [R_P[pi]])

                def fn(e):
                    st = (i == 0)
                    en = (i == n - 1)
                    e.matmul(psB[:, 0, q0:nt], lhsT=sg_["v"], rhs=Pt[pi][0:nk, 0, q0:nt], start=st, stop=en)
                    e.matmul(psB[:, 0, 512 + q0:512 + nt], lhsT=sg_["v"], rhs=Pt[pi][0:nk, 1, q0:nt], start=st, stop=en)
                    e.matmul(psB[:, 1, q0:nt], lhsT=onesb[0:nk, :], rhs=Pt[pi][0:nk, 0, q0:nt], start=st, stop=en)
                    return e.matmul(psB[:, 1, 512 + q0:512 + nt], lhsT=onesb[0:nk, :], rhs=Pt[pi][0:nk, 1, q0:nt],
                                    start=st, stop=en)
                S.add("pe", fn, [R_P[pi], R_const] + sg_["res"], [R_B[0], R_B[1]])

            emit_qk(0)
            for i in range(n):
                if i + 1 < n:
                    emit_qk(i + 1)
                emit_exp_pv(i)
                if i == min(9, n - 1) and pend[0] is not None:
                    pend[0](sbank[i])
                    pend[0] = None
            S.add("act", lambda e: e.activation(out=fo[:, :, 0:nt], in_=psB[:, 0, :].rearrange("p (a b) -> p a b", a=2)[:, :, 0:nt],
                                                func=AF.Copy), [R_B[0]], [R_fo])
            S.add("dve", lambda e: e.tensor_copy(out=fin12[:, :, 0:nt], in_=psB[:, 1, :].rearrange("p (a b) -> p a b", a=2)[:, :, 0:nt]),
                  [R_B[1]], [R_fin])
            S.add("dve", lambda e: e.reciprocal(out=fin12[:, :, 0:nt], in_=fin12[:, :, 0:nt]), [R_fin], [R_fin])
            S.add("dve", lambda e: e.tensor_tensor(out=fin12[:, 0, 0:nt], in0=fo[:, 0, 0:nt], in1=fin12[:, 0, 0:nt], op=ALU.mult),
                  [R_fo, R_fin], [R_fin])
            S.add("dve", lambda e: e.scalar_tensor_tensor(out=fin12[:, 1, 0:nt], in0=fo[:, 1, 0:nt], scalar=neglam[:, a:a + 1],
                                                           in1=fin12[:, 1, 0:nt], op0=ALU.mult, op1=ALU.mult),
                  [R_fo, R_fin, R_const], [R_fin])
            S.add("dve", lambda e: e.tensor_tensor(out=fin12[:, 0, 0:nt], in0=fin12[:, 0, 0:nt], in1=fin12[:, 1, 0:nt], op=ALU.add),
                  [R_fin], [R_fin])
            def tail(ai=None):
                if ai is None:
                    ai = nextA()
                S.add("act", lambda e: e.activation(out=finb[:, 0:nt], in_=fin12[:, 0, 0:nt], func=AF.Square), [R_fin], [R_fin])
                S.add("pe", lambda e: e.matmul(psA[:, ai, 0:nt], lhsT=onesb[:, :], rhs=finb[:, 0:nt], start=True, stop=True),
                      [R_fin, R_const], [R_A[ai]])
                S.add("act", lambda e: e.activation(out=fin3[:, 0:nt], in_=psA[:, ai, 0:nt], func=AF.Ln, scale=1.0 / 128,
                                                    bias=eps_t[:, 1:2]), [R_A[ai], R_const], [R_fin])
                S.add("act", lambda e: e.activation(out=fin3[:, 0:nt], in_=fin3[:, 0:nt], func=AF.Exp, scale=-0.5),
                      [R_fin], [R_fin])
                S.add("dve", lambda e: e.scalar_tensor_tensor(out=OT[:, h, 0:nt], in0=fin12[:, 0, 0:nt], scalar=sg[:, a:a + 1],
                                                               in1=fin3[:, 0:nt], op0=ALU.mult, op1=ALU.mult),
                      [R_fin, R_const], [R_OT[h]])
            pend[0] = tail

        kvstate = {"n": 0}

        def run_tile(kind, t):
            if kind == "meta":
                nt, xsrc, sk = NMETA, meta.ap(), 0
            elif kind == "sample":
                nt, xsrc, sk = DEC, xs.ap(), 1
            else:
                nt, xsrc, sk = TP, AP(xp, t * TP * D, [[D, TP], [1, D]]), 2
            nsub = (nt + 127) // 128
            subs = [(s, min(128, nt - s * 128)) for s in range(nsub)]
            if nt >= 128:
                for s_ in range(nsub):
                    dmaop("sp", xb[:, s_, :], xsrc[s_ * 128:(s_ + 1) * 128, :], [], R_x[s_], "xload%d" % s_)
            else:
                dmaop("sp", xb[0:nt, 0, :], xsrc, [], R_x[0], "xload")
            last_layer = 4 if kind != "meta" else 4
            for L in range(4):
                is_attn = (L % 2 == 0)
                a = L // 2
                p = L // 2
                meta_last = (kind == "meta" and L == 3)
                if is_attn:
                    prenorm_T(nt, hT_dst, [R_hT])
                    for j in range(2):
                        wt, rw = wuse(blk_attn(a, j))
                        for cc_ in range(4):
                            hc = j * 4 + cc_
                            ai = nextA()
                            S.add("pe", proj_fm(wt, cc_, hT, nt, ai), [rw, R_hT], [R_A[ai]])
                            S.add("act", lambda e, ai=ai, hc=hc: e.activation(out=QT[:, hc, 0:nt], in_=psA[:, ai, 0:nt],
                                                                             func=AF.Copy, scale=0.125),
                                  [R_A[ai]], [R_QT[hc]])
                    wk = [wuse(blk_attn(a, 2)), wuse(blk_attn(a, 3))]
                    for j, (wt, rw) in enumerate(wk):
                        for cc_ in range(4):
                            hc = j * 4 + cc_
                            ai = nextA()
                            S.add("pe", proj_fm(wt, cc_, hT, nt, ai), [rw, R_hT], [R_A[ai]])
                            if kind == "meta":
                                S.add("dve", lambda e, ai=ai, hc=hc, a=a: e.tensor_copy(out=KTm[:, a, hc, :], in_=psA[:, ai, 0:nt]),
                                      [R_A[ai]], [R_KTm[a]])
                            else:
                                S.add("dve", lambda e, ai=ai, hc=hc: e.tensor_copy(out=KTc[:, hc, 0:nt], in_=psA[:, ai, 0:nt]),
                                      [R_A[ai]], [R_uT[hc]])
                    if kind == "sample":
                        dmaop("pool", AP(KTs, a * NH * 128 * KTS_LEN + PAST, [[KTS_LEN, 128], [128 * KTS_LEN, 8], [1, nt]]),
                              KTc[:, :, 0:nt], R_KTc, [R_KTsn[a]], "ktc")
                    elif kind == "prompt":
                        dmaop("pool", AP(KTd, a * NH * 128 * SEQ + t * TP, [[SEQ, 128], [128 * SEQ, 8], [1, nt]]),
                              KTc[:, :, 0:nt], R_KTc, [R_KTd[a][t]], "ktc")
                    for isK in (True, False):
                        if isK:
                            wpair = wk
                        else:
                            wpair = [wuse(blk_attn(a, 4)), wuse(blk_attn(a, 5))]
                        for (s, rows) in subs:
                            bi = rot_B()
                            for half in range(2):
                                wt, rw = wpair[half]
                                S.add("pe", proj_tm(wt, hT, s, rows, bi, half, range(8), True, True), [rw, R_hT], [R_B[bi]])
                            si_ = rot("stg", 2)
                            if isK:
                                S.add("act", lambda e, bi=bi, si_=si_, rows=rows: e.activation(
                                    out=stg[si_][0:rows, :], in_=psB[0:rows, bi, :], func=AF.Copy), [R_B[bi]], [R_stg[si_]])
                            else:
                                S.add("dve", lambda e, bi=bi, si_=si_, rows=rows: e.tensor_copy(
                                    out=stg[si_][0:rows, :], in_=psB[0:rows, bi, :]), [R_B[bi]], [R_stg[si_]])
                            if kind == "meta":
                                dst_t, off = (kp if isK else vp), a * (NMETA + SEQ) * D
                            elif kind == "sample":
                                dst_t, off = (kso if isK else vso), a * DEC * D
                            else:
                                dst_t, off = (kp if isK else vp), a * (NMETA + SEQ) * D + (NMETA + t * TP + s * 128) * D
                            dmaop("pool", AP(dst_t, off, [[D, rows], [1, D]]), stg[si_][0:rows, :],
                                  [R_stg[si_]], [R_out], "stg%d" % si_, is_out=True)
                            if not isK:
                                if kind == "meta":
                                    S.add("pool", lambda e, si_=si_, rows=rows, a=a: e.tensor_copy(
                                        out=Vm[0:rows, a, :], in_=stg[si_][0:rows, :]), [R_stg[si_]], [R_Vm[a]])
                                else:
                                    sb_ = rot("stgb", 2)
                                    S.add("pool", lambda e, si_=si_, sb_=sb_, rows=rows: e.tensor_copy(
                                        out=stgb[sb_][0:rows, :], in_=stg[si_][0:rows, :]), [R_stg[si_]], [R_stgb[sb_]])
                                    if kind == "sample":
                                        dmaop("pool", AP(Vs, a * KTS_LEN * D + (PAST + s * 128) * D, [[D, rows], [1, D]]),
                                              stgb[sb_][0:rows, :], [R_stgb[sb_]], [R_Vsn[a]], "stgb%d" % sb_)
                                    else:
                                        dmaop("pool", AP(Vd, a * SEQ * D + (t * TP + s * 128) * D, [[D, rows], [1, D]]),
                                              stgb[sb_][0:rows, :], [R_stgb[sb_]], [R_Vd[a][t]], "stgb%d" % sb_)
                    if kind == "meta" and a == 0:
                        dump("QT0", QT[:, 0, 0:nt], [R_QT[0]])
                        dump("KTm0", KTm[:, 0, 0, :], [R_KTm[0]])
                        dump("Vm0", Vm[0:16, 0, :], [R_Vm[0]])
                        dump("neglam", neglam[:, :], [R_const])
                        dump("sg", sg[:, :], [R_const])
                        dump("cH", cH[:, :], [R_const])
                        dump("D0", D0[:, 0, :], [R_const])
                    for h in range(NH):
                        segs = []
                        loads = []
                        if kind == "meta":
                            segs.append(dict(kt=KTm[:, a, h, :], v=Vm[0:NMETA, a, h * 128:(h + 1) * 128], nk=NMETA, q0=0,
                                             bias=(D0[0:NMETA, h, 0:NMETA], 0), res=[R_KTm[a], R_Vm[a]]))
                        else:
                            mb = None
                            if kind == "prompt" and t == 0:
                                mb = (Dm[:, h, :], 0)
                            segs.append(dict(kt=KTm[:, a, h, :], v=Vm[0:NMETA, a, h * 128:(h + 1) * 128], nk=NMETA, q0=0,
                                             bias=mb, res=[R_KTm[a], R_Vm[a]]))
                            if kind == "sample":
                                nkeys = PAST + DEC
                                ktsrc, vsrc, klen = KTs, Vs, KTS_LEN
                                rkf = lambda c, a=a: [R_KTs] if c == 0 else [R_KTsn[a]]
                                rvf = lambda c, a=a: [R_Vs] if c == 0 else [R_Vsn[a]]
                                g0 = PAST // 128
                            else:
                                nkeys = (t + 1) * TP
                                ktsrc, vsrc, klen = KTd, Vd, SEQ
                                rkf = lambda c, a=a, t=t: [R_KTd[a][tt] for tt in (2 * c, 2 * c + 1) if tt <= t]
                                rvf = lambda c, a=a, t=t: [R_Vd[a][tt] for tt in (2 * c, 2 * c + 1) if tt <= t]
                                g0 = 4 * t
                            nch = (nkeys + KC - 1) // KC
                            base = kvstate["n"]
                            kvstate["n"] += nch
                            for c in range(nch):
                                k0 = c * KC
                                kn = min(KC, nkeys - k0)
                                nkt = (kn + 127) // 128
                                sl = (base + c) % NKV

                                rk = rkf(c)
                                rv = rvf(c)

                                def ld(sl=sl, k0=k0, kn=kn, a=a, h=h, ktsrc=ktsrc, vsrc=vsrc, rk=rk, rv=rv, klen=klen):
                                    dmaop("sp", kvK[sl][:, 0:kn], AP(ktsrc, (a * NH + h) * 128 * klen + k0, [[klen, 128], [1, kn]]),
                                          rk, [R_kv[sl]], "kv%d" % sl)
                                    nfull = kn // 128
                                    if nfull:
                                        dmaop("sp", kvV[sl][:, 0:nfull, :],
                                              AP(vsrc, a * klen * D + k0 * D + h * 128, [[D, 128], [128 * D, nfull], [1, 128]]),
                                              rv, [R_kvV[sl]], "kv%d" % sl)
                                    rr = kn % 128
                                    if rr:
                                        dmaop("sp", kvV[sl][0:rr, nfull, :],
                                              AP(vsrc, a * klen * D + (k0 + nfull * 128) * D + h * 128, [[D, rr], [1, 128]]),
                                              rv, [R_kvV[sl]], "kv%d" % sl)
                                loads.append(ld)
                                for kt_ in range(nkt):
                                    G = (k0 // 128) + kt_
                                    nk = min(128, kn - kt_ * 128)
                                    j_ = G - g0
                                    if j_ < -1:
                                        b_, q0 = None, 0
                                    elif j_ == -1:
                                        b_, q0 = (Dp[0:nk, h, 0:min(128, nt)], 0), 0
                                    else:
                                        q0 = 128 * j_
                                        b_ = (D0[0:nk, h, :], q0)
                                    segs.append(dict(kt=kvK[sl][:, kt_ * 128:kt_ * 128 + nk], v=kvV[sl][0:nk, kt_, :], nk=nk,
                                                     q0=q0, bias=b_, res=[R_kv[sl], R_kvV[sl]], chunk=c))
                        attention_head(a, h, nt, segs, loads)
                    if pend[0] is not None:
                        pend[0]()
                        pend[0] = None
                    w0, rw0 = wuse(blk_attn(a, 6))
                    w1, rw1 = wuse(blk_attn(a, 7))
                    for (s, rows) in subs:
                        bi = rot_B()
                        S.add("pe", proj_tm(w0, OT, s, rows, bi, 0, range(8), True, True), [rw0] + R_OT, [R_B[bi]])
                        S.add("pe", proj_tm(w1, OT, s, rows, bi, 1, range(8), True, True), [rw1] + R_OT, [R_B[bi]])
                        postnorm_add(nt, s, rows, bi, L)
                    if kind == "meta":
                        dbg_x("mix%d" % L, nt)
                else:
                    if kind == "meta":
                        S.add("pool", lambda e: e.memset(ext[:, :, 0:15], 0.0), [], [R_ext])
                    else:
                        S.add("pool", lambda e, sk=sk, p=p: e.tensor_copy(out=ext[:, :, 0:15], in_=stash[:, sk, p, :, :]),
                              [R_stash[sk][p]], [R_ext])
                    tail_out = None
                    if kind == "sample":
                        tail_out = (AP(pso, p * 15 * D, [[D, 15], [1, D]]), p, True)
                    elif kind == "prompt" and t == NT_P - 1:
                        tail_out = (AP(pp, p * 15 * D, [[D, 15], [1, D]]), p, True)
                    prenorm_T(nt, ext_dst, [R_ext], scale_vec=2 * p + 1, tail_out=tail_out)
                    dk = 2 if kind == "meta" else sk
                    S.add("pool", lambda e, dk=dk, p=p, nt=nt: e.tensor_copy(out=stash[:, dk, p, :, :], in_=ext[:, :, nt:nt + 15]),
                          [R_ext], [R_stash[dk][p]])
                    if meta_last:
                        break
                    W = nt + 15
                    for g in range(4):
                        c0 = 2 * g
                        cur, Rcur = ext[:, c0:c0 + 2, :], R_ext
                        bufs = [(sA_, R_sA), (sB_, R_sB)]
                        sh = 1
                        for lev in range(g + 1):
                            ob, Rob = bufs[lev % 2]
                            lo = 15 - (15 - (2 * sh - 1)) if False else (2 * sh - 1)
                            S.add("dve" if (g + lev) % 2 == 0 else "pool",
                                  lambda e, ob=ob, cur=cur, lo=lo, sh=sh, W=W: e.tensor_tensor(
                                      out=ob[:, :, lo:W], in0=cur[:, :, lo:W], in1=cur[:, :, lo - sh:W - sh], op=ALU.add),
                                  [Rcur], [Rob])
                            cur, Rcur = ob[:, :, :], Rob
                            sh *= 2
                        w_ = 2 ** (g + 1)
                        if kind == "meta":
                            ob2, Rob2 = bufs[(g + 1) % 2]
                            for c2 in range(2):
                                S.add("dve", lambda e, ob2=ob2, cur=cur, g=g, nt=nt, c2=c2: e.tensor_tensor(
                                    out=ob2[:, c2, 15:15 + nt], in0=cur[:, c2, 15:15 + nt],
                                    in1=invct[:, g, 0:nt], op=ALU.mult), [Rcur, R_const], [Rob2])
                            S.add("dve", lambda e, ob2=ob2, c0=c0, nt=nt: e.tensor_tensor(
                                out=dT[:, c0:c0 + 2, 0:nt], in0=ob2[:, :, 15:15 + nt], in1=ext[:, c0:c0 + 2, 15:15 + nt],
                                op=ALU.subtract), [Rob2, R_ext], [R_dT])
                        else:
                            S.add("dve", lambda e, cur=cur, c0=c0, nt=nt, w_=w_: e.scalar_tensor_tensor(
                                out=dT[:, c0:c0 + 2, 0:nt], in0=cur[:, :, 15:15 + nt], scalar=1.0 / w_,
                                in1=ext[:, c0:c0 + 2, 15:15 + nt], op0=ALU.mult, op1=ALU.subtract), [Rcur, R_ext], [R_dT])
                    wt, rw = wuse(blk_pool(p))
                    for (s, rows) in subs:
                        bi = rot_B()

                        def fn(e, s=s, rows=rows, bi=bi, wt=wt):
                            inst = None
                            for g in range(4):
                                for c2 in range(2):
                                    inst = e.matmul(psB[0:rows, bi, g * 256:(g + 1) * 256],
                                                    lhsT=dT[:, 2 * g + c2, s * 128:s * 128 + rows], rhs=wt[:, 2 * g + c2, 0:256],
                                                    start=(c2 == 0), stop=(c2 == 1))
                            return inst
                        S.add("pe", fn, [rw, R_dT], [R_B[bi]])
                        postnorm_add(nt, s, rows, bi, L)
                    if kind == "meta":
                        dbg_x("mix%d" % L, nt)
                prenorm_T(nt, hT_dst, [R_hT])
                for j in range(8):
                    wt, rw = wuse(blk_up(L, j))
                    for cc_ in range(4):
                        fc = 4 * j + cc_
                        ai = nextA()
                        S.add("pe", proj_fm(wt, cc_, hT, nt, ai), [rw, R_hT], [R_A[ai]])
                        ri = rot("rt", 2)
                        S.add("act", lambda e, ai=ai, ri=ri: e.activation(out=rtmp[ri][:, 0:nt], in_=psA[:, ai, 0:nt], func=AF.Relu),
                              [R_A[ai]], [R_rt[ri]])
                        S.add("dve" if fc % 2 == 0 else "pool",
                              lambda e, ri=ri, fc=fc: e.tensor_tensor(out=uT[:, fc, 0:nt], in0=rtmp[ri][:, 0:nt],
                                                                      in1=rtmp[ri][:, 0:nt], op=ALU.mult),
                              [R_rt[ri]], [R_uT[fc]])
                for sp0 in range(0, nsub, 2):
                    pair = subs[sp0:sp0 + 2]
                    bis = {}
                    for (s, rows) in pair:
                        bis[s] = rot_B()
                    for c in range(2):
                        for jj in range(4):
                            wt, rw = wuse(blk_dn(L, c, jj))
                            for (s, rows) in pair:
                                S.add("pe", proj_tm(wt, uT, s, rows, bis[s], c, range(8), jj == 0, jj == 3, srcoff=8 * jj),
                                      [rw] + R_uT[8 * jj:8 * jj + 8], [R_B[bis[s]]])
                    for (s, rows) in pair:
                        postnorm_add(nt, s, rows, bis[s], 4 + L)
                if kind == "meta":
                    dbg_x("ffn%d" % L, nt)
            if kind == "sample":
                dmaop("pool", ys.ap(), xb[0:nt, 0, :], R_x[0], [R_out], "yout", is_out=True)
            elif kind == "prompt":
                for s_ in range(4):
                    dmaop("pool", AP(yp, (t * TP + s_ * 128) * D, [[D, 128], [1, D]]), xb[:, s_, :],
                          R_x[s_], [R_out], "yout%d" % s_, is_out=True)

        bi_map = {}

        def rot_B():
            return rot("B_", 2)
        cnt["B_"] = 0

        def tile_plan(kind):
            pl = []
            for L in range(4):
                if L % 2 == 0:
                    pl += [blk_attn(L // 2, j) for j in range(8)]
                else:
                    if kind == "meta" and L == 3:
                        break
                    pl.append(blk_pool(L // 2))
                nsub = 4 if kind == "prompt" else 1
                pl += [blk_up(L, j) for j in range(8)]
                for sp0 in range(0, nsub, 2):
                    pl += [blk_dn(L, c, jj) for c in range(2) for jj in range(4)]
            return pl

        for p in range(2):
            dmaop("sp", xh[p][0:15, :], AP(stp, p * 15 * D, [[D, 15], [1, D]]), [], [R_xh[p]], "stpl")
            for half in range(2):
                ai = nextA()

                def tfn(e, p=p, half=half, ai=ai):
                    inst = None
                    for c4 in range(4):
                        c = half * 4 + c4
                        inst = e.transpose(out=psA[:, ai, c4 * 128:c4 * 128 + 15], in_=xh[p][0:15, c * 128:(c + 1) * 128],
                                           identity=ident[0:15, 0:15])
                    return inst
                S.add("pe", tfn, [R_xh[p], R_const], [R_A[ai]])
                S.add("dve", lambda e, p=p, half=half, ai=ai: e.tensor_copy(
                    out=stash[:, 1, p, half * 4:half * 4 + 4, :],
                    in_=psA[:, ai, :].rearrange("p (c k) -> p c k", c=4)[:, :, 0:15]), [R_A[ai]], [R_stash[1][p]])
        tiles = [("meta", 0), ("sample", 0)] + [("prompt", t) for t in range(NT_P)]
        if stop_after is not None:
            tiles = tiles[:stop_after]
        for kind, t in tiles:
            wplan.extend(tile_plan(kind))
        for kind, t in tiles:
            run_tile(kind, t)

        sems = [es.enter_context(nc.semaphore("s%d" % i)) for i in range(90)]
        S.finalize_and_emit(sems)
    return nc


_CACHE = {}


def kernel(**inputs):
    f32 = lambda a: np.ascontiguousarray(np.asarray(a, dtype=np.float32))
    x_prompt = f32(inputs["x_prompt"]); x_sample = f32(inputs["x_sample"])
    cache_k = f32(inputs["cache_k"]); cache_v = f32(inputs["cache_v"]); state_pool = f32(inputs["state_pool"])
    if "nc" not in _CACHE:
        _CACHE["nc"] = build_program()
    nc = _CACHE["nc"]
    shared = {
        "meta": f32(inputs["meta_tokens"]), "rel": f32(inputs["rel_bias_table"]),
        "nmp": f32(inputs["norm_mix_pre"]), "nmo": f32(inputs["norm_mix_post"]),
        "nfp": f32(inputs["norm_ffn_pre"]), "nfo": f32(inputs["norm_ffn_post"]),
        "wqkv": f32(inputs["w_qkv"]),
        "lq1": f32(inputs["lambda_q1"]), "lk1": f32(inputs["lambda_k1"]),
        "lq2": f32(inputs["lambda_q2"]), "lk2": f32(inputs["lambda_k2"]),
        "subg": f32(inputs["subln_g"]), "wo": f32(inputs["w_o"]), "wpool": f32(inputs["w_pool"]),
        "pscale": f32(inputs["pool_scale"]), "wup": f32(inputs["w_up"]), "wdn": f32(inputs["w_down"]),
        "ohc": _onehot_const(), "invc": _invcnt_const(), "eye": np.eye(128, dtype=np.float32),
    }
    in_maps = []
    for b in range(NCORES):
        m = dict(shared)
        m["xp"] = x_prompt[b]
        m["xs"] = x_sample[b]
        m["ck"] = np.ascontiguousarray(cache_k[:, b].reshape(2, PAST, D))
        m["cv"] = np.ascontiguousarray(cache_v[:, b].reshape(2, PAST, D))
        m["stp"] = np.ascontiguousarray(state_pool[:, b])
        in_maps.append(m)
    res = run_bass_kernel_spmd(nc, in_maps, core_ids=list(range(NCORES)))
    R = res.results
    y_prompt = np.stack([R[b]["yp"] for b in range(NCORES)], 0)
    y_sample = np.stack([R[b]["ys"] for b in range(NCORES)], 0)
    k_prompt = np.stack([R[b]["kp"].reshape(2, NMETA + SEQ, NH, 128) for b in range(NCORES)], 1)
    v_prompt = np.stack([R[b]["vp"].reshape(2, NMETA + SEQ, NH, 128) for b in range(NCORES)], 1)
    pool_prompt = np.stack([R[b]["pp"] for b in range(NCORES)], 1)
    k_sample = np.stack([R[b]["kso"].reshape(2, DEC, NH, 128) for b in range(NCORES)], 1)
    v_sample = np.stack([R[b]["vso"].reshape(2, DEC, NH, 128) for b in range(NCORES)], 1)
    pool_sample = np.stack([R[b]["pso"] for b in range(NCORES)], 1)
    return (y_prompt, y_sample, k_prompt, v_prompt, pool_prompt, k_sample, v_sample, pool_sample)
```

```python
import math
import numpy as np
import concourse.bass as bass
import concourse.mybir as mybir
from concourse.bass_utils import run_bass_kernel_spmd

F32 = mybir.dt.float32
BF16 = mybir.dt.bfloat16
ALU = mybir.AluOpType
AF = mybir.ActivationFunctionType
AX = mybir.AxisListType
AP = bass.AP

NCORES = 8
D = 1024
NH = 8
DFF = 4096
SEQ = 8192
TP = 512
NT_P = SEQ // TP
NMETA = 16
DEC = 64
PAST = 1024
EPS = 1e-6
SUBLN_EPS = 1e-5
LAM_INIT = [0.8 - 0.6 * math.exp(-0.3 * 0), 0.8 - 0.6 * math.exp(-0.3 * 2)]
ZL = 384
NEG = -30000.0
KC = 1024
KTS_LEN = 1152


def _t5_bucket_np(rel):
    nb = 16
    max_exact = 8
    rel = np.asarray(rel, np.int32)
    ret = np.where(rel > 0, nb, 0).astype(np.int32)
    n = np.abs(rel)
    nf = np.maximum(n, 1).astype(np.float32)
    large = max_exact + (np.log(nf / np.float32(max_exact)) / np.float32(math.log(128 / max_exact))
                         * np.float32(nb - max_exact)).astype(np.int32)
    for nn, bb in ((16, 10), (32, 12), (64, 14)):
        large = np.where(n == nn, bb, large)
    large = np.minimum(large, nb - 1)
    return ret + np.where(n < max_exact, n, large)


def _onehot_const():
    rel = 127 - np.arange(ZL)
    bk = _t5_bucket_np(rel)
    oh = np.zeros((32, ZL), np.float32)
    oh[bk, np.arange(ZL)] = 1.0
    return oh


def _invcnt_const():
    inv = np.zeros((4, 16), np.float32)
    for g, w in enumerate((2, 4, 8, 16)):
        for t in range(16):
            inv[g, t] = 1.0 / min(t + 1, w)
    return inv


def _band_const():
    B = np.zeros((5, 4, 128, 128), np.float32)
    for g, w in enumerate((2, 4, 8, 16)):
        for t in range(128):
            for tp in range(max(0, t - w + 1), t + 1):
                B[0, g, tp, t] += 1.0 / w
            B[0, g, t, t] -= 1.0
            for u in range(-15, 0):
                if u > t - w:
                    B[1, g, 128 + u, t] = 1.0 / w
                    B[2, g, 16 + u, t] = 1.0 / w
                    B[4, g, 15 + u, t] = 1.0 / w
        for t in range(16):
            for tp in range(max(0, t - w + 1), t + 1):
                B[3, g, tp, t] += 1.0 / min(t + 1, w)
            B[3, g, t, t] -= 1.0
    return B


class Res:
    __slots__ = ("name", "lw", "rd")

    def __init__(self, name):
        self.name = name
        self.lw = None
        self.rd = []


class Op:
    __slots__ = ("eng", "fn", "deps", "sig", "sem", "val", "dma", "key", "n")


class Sched:
    ENGS = ("sp", "act", "dve", "pool", "pe")
    NROT = 4

    def __init__(self, nc):
        self.nc = nc
        self.ops = []
        self.out_ops = []

    def add(self, eng, fn, reads=(), writes=(), dma=False, key=None, is_out=False):
        op = Op()
        op.eng = eng
        op.fn = fn
        op.dma = dma
        op.key = key
        op.sig = False
        op.sem = None
        op.val = 0
        op.n = len(self.ops)
        deps = set()
        for r in reads:
            if r.lw is not None:
                deps.add(r.lw)
        for w in writes:
            if w.lw is not None:
                deps.add(w.lw)
            deps.update(w.rd)
        for r in reads:
            r.rd.append(op)
        for w in writes:
            w.lw = op
            w.rd = []
        deps.discard(op)
        op.deps = deps
        self.ops.append(op)
        if is_out:
            self.out_ops.append(op)
        return op

    def finalize_and_emit(self, sems_pool):
        nc = self.nc
        fin = self.add("sp", None)
        fin.deps = set(self.out_ops)
        for op in self.ops:
            for d in op.deps:
                if d.eng == "pe" and op.eng == "pe" and not d.dma:
                    continue
                d.sig = True
        gk = set(op.key for op in self.ops if op.dma and op.sig and op.key.startswith("g:"))
        for op in self.ops:
            if op.dma and op.key in gk:
                op.sig = True
        semi = iter(sems_pool)
        eng_sems = {e: [next(semi) for _ in range(self.NROT)] for e in self.ENGS if e != "sp"}
        eng_cnt = {e: 0 for e in self.ENGS}
        dma_sems = {}
        dma_cnt = {}
        for op in self.ops:
            if not op.sig:
                continue
            if op.dma:
                k = op.key
                if k not in dma_sems:
                    dma_sems[k] = next(semi)
                    dma_cnt[k] = 0
                dma_cnt[k] += 16
                op.sem = dma_sems[k]
                op.val = dma_cnt[k]
            else:
                n = eng_cnt[op.eng]
                eng_cnt[op.eng] = n + 1
                op.sem = eng_sems[op.eng][n % self.NROT]
                op.val = n // self.NROT + 1
        self.n_sems = 4 * self.NROT + len(dma_sems)
        for op in self.ops:
            if op.dma and op.sig and op.key.startswith("g:"):
                op.val = dma_cnt[op.key]

        def emit(engname, eng):
            known = {}
            for op in self.ops:
                if op.eng != engname:
                    continue
                need = {}
                for d in op.deps:
                    if not d.sig:
                        continue
                    if d.eng == "pe" and engname == "pe" and not d.dma:
                        continue
                    if op.dma and d.dma and op.key == d.key and op.key.startswith("g:"):
                        continue
                    if need.get(d.sem, 0) < d.val:
                        need[d.sem] = d.val
                for s, v in need.items():
                    if known.get(s, 0) < v:
                        eng.wait_ge(s, v)
                        known[s] = v
                if op.fn is None:
                    continue
                inst = op.fn(eng)
                if op.sig:
                    inst.then_inc(op.sem, 16 if op.dma else 1)

        with nc.Block() as block:
            @block.sync
            def _(e):
                emit("sp", e)

            @block.scalar
            def _(e):
                emit("act", e)

            @block.vector
            def _(e):
                emit("dve", e)

            @block.gpsimd
            def _(e):
                emit("pool", e)

            @block.tensor
            def _(e):
                emit("pe", e)


def build_program(stop_after=None, dbg=False):
    nc = bass.Bass("TRN2", target_bir_lowering=False)
    S = Sched(nc)

    def din(name, shape):
        return nc.dram_tensor(name, list(shape), F32, kind="ExternalInput")

    def dout(name, shape):
        return nc.dram_tensor(name, list(shape), F32, kind="ExternalOutput")

    xp = din("xp", [SEQ, D]); xs = din("xs", [DEC, D])
    ck = din("ck", [2, PAST, D]); cv = din("cv", [2, PAST, D]); stp = din("stp", [2, 15, D])
    meta = din("meta", [NMETA, D]); rel = din("rel", [32, 8])
    nmp = din("nmp", [4, D]); nmo = din("nmo", [4, D]); nfp = din("nfp", [4, D]); nfo = din("nfo", [4, D])
    wqkv = din("wqkv", [2, D, 3 * D])
    lq1 = din("lq1", [2, 64]); lk1 = din("lk1", [2, 64]); lq2 = din("lq2", [2, 64]); lk2 = din("lk2", [2, 64])
    subg = din("subg", [2, 128]); wo = din("wo", [2, D, D]); wpool = din("wpool", [2, 4, 256, 256])
    pscale = din("pscale", [2, D]); wup = din("wup", [4, D, DFF]); wdn = din("wdn", [4, DFF, D])
    ohc = din("ohc", [32, ZL]); invc = din("invc", [4, 16]); eye = din("eye", [128, 128])
    bandc = din("bandc", [20, 128, 128])

    yp = dout("yp", [SEQ, D]); ys = dout("ys", [DEC, D])
    kp = dout("kp", [2, NMETA + SEQ, D]); vp = dout("vp", [2, NMETA + SEQ, D]); pp = dout("pp", [2, 15, D])
    kso = dout("kso", [2, DEC, D]); vso = dout("vso", [2, DEC, D]); pso = dout("pso", [2, 15, D])

    NBLK = 2 * 8 + 2 * 1 + 4 * 16
    wsc = nc.dram_tensor("wsc", [NBLK, 128, 4096], BF16)
    KTd = nc.dram_tensor("KTd", [2, NH, 128, SEQ], BF16)
    Vd = nc.dram_tensor("Vd", [2, SEQ, D], BF16)
    KTs = nc.dram_tensor("KTs", [2, NH, 128, KTS_LEN], BF16)
    Vs = nc.dram_tensor("Vs", [2, KTS_LEN, D], BF16)
    Gd = nc.dram_tensor("Gd", [8, ZL], BF16)
    Zb = nc.dram_tensor("Zb", [8, 128, ZL], BF16)

    def blk_attn(a, j):
        return a * 8 + j

    def blk_pool(p):
        return 16 + p

    def blk_up(l, j):
        return 18 + l * 16 + j

    def blk_dn(l, c, jj):
        return 18 + l * 16 + 8 + c * 4 + jj

    import contextlib
    es = contextlib.ExitStack()

    def sb(name, shape, dt):
        return es.enter_context(nc.sbuf_tensor(name, list(shape), dt))

    with es:
        ident = sb("ident", [128, 128], F32)
        identb = sb("identb", [128, 128], BF16)
        onesb = sb("onesb", [128, 128], BF16)
        cH = sb("cH", [128, 8], F32)
        D0 = sb("D0", [128, 8, 256], BF16)
        Dp = sb("Dp", [128, 8, 128], BF16)
        Dm = sb("Dm", [16, 8, 128], BF16)
        neglam = sb("neglam", [128, 2], F32)
        sg = sb("sg", [128, 2], F32)
        gpre = sb("gpre", [128, 8, 8], F32)
        gpost = sb("gpost", [128, 8, D], BF16)
        invct = sb("invct", [128, 4, 16], F32)
        KTm = sb("KTm", [128, 2, NH, NMETA], BF16)
        Vm = sb("Vm", [NMETA, 2, D], BF16)
        band = sb("band", [128, 20, 128], BF16)
        gpool = sb("gpool", [128, 2, D], F32)
        hmeta = sb("hmeta", [16, 2, D], BF16)
        hstate = sb("hstate", [16, 2, D], BF16)
        hcar = sb("hcar", [128, 2, D], BF16)
        R_const = Res("const")
        R_KTm = [Res("KTm0"), Res("KTm1")]
        R_Vm = [Res("Vm0"), Res("Vm1")]
        R_hmeta = [Res("hmeta0"), Res("hmeta1")]
        R_hstate = [Res("hstate0"), Res("hstate1")]
        R_hcar = [Res("hcar0"), Res("hcar1")]

        psA = es.enter_context(nc.psum_tensor("psA", [128, 4, 512], F32))
        psB = es.enter_context(nc.psum_tensor("psB", [128, 2, 1024], F32))
        R_A = [Res("psA%d" % i) for i in range(4)]
        R_B = [Res("psB%d" % i) for i in range(2)]
        cntA = [0]

        def nextA():
            i = cntA[0] % 4
            cntA[0] += 1
            return i

        def dmaop(q, out_ap, in_ap, reads, writes, key, is_out=False, nonc=False):
            def fn(e, out_ap=out_ap, in_ap=in_ap, nonc=nonc):
                if nonc:
                    return e.dma_start(out=out_ap, in_=in_ap, allow_slow_non_contiguous=True)
                return e.dma_start(out=out_ap, in_=in_ap)
            return S.add(q, fn, reads, writes, dma=True, key=key, is_out=is_out)

        esA = contextlib.ExitStack()
        with esA:
            def sbA(name, shape, dt):
                return esA.enter_context(nc.sbuf_tensor(name, list(shape), dt))

            tab = sbA("tab", [32, 8], F32)
            tab15 = sbA("tab15", [32, 8], F32)
            oht = sbA("oht", [32, ZL], F32)
            Gs = sbA("Gs", [8, ZL], BF16)
            L4 = sbA("L4", [128, 4, 128], F32)
            prod = sbA("prod", [128, 2, 128], F32)
            dots = sbA("dots", [128, 4], F32)
            pscl = sbA("pscl", [128, 2, 8, 256], F32)
            gtmp = sbA("gtmp", [128, 4 * D], F32)
            R_t = Res("tab")
            R_l = Res("lam")
            G0 = "g:c0"
            dmaop("sp", ident[:, :], eye.ap(), [], [R_const], G0)
            dmaop("sp", cH[:, :], AP(rel, 15 * 8, [[0, 128], [1, 8]]), [], [R_const], G0)
            dmaop("sp", tab[:, :], rel.ap(), [], [R_t], G0)
            dmaop("sp", tab15[:, :], AP(rel, 15 * 8, [[0, 32], [1, 8]]), [], [R_t], G0)
            dmaop("sp", oht[:, :], ohc.ap(), [], [R_t], G0)
            for i, t in enumerate((lq1, lk1, lq2, lk2)):
                dmaop("sp", L4[:, i, :], AP(t, 0, [[0, 128], [1, 128]]), [], [R_l], G0)
            dmaop("sp", sg[:, :], AP(subg, 0, [[1, 128], [128, 2]]), [], [R_l], G0, nonc=True)
            dmaop("sp", gpre[:, 0:4, :], AP(nmp, 0, [[1, 128], [D, 4], [128, 8]]), [], [R_const], G0, nonc=True)
            dmaop("sp", gpre[:, 4:8, :], AP(nfp, 0, [[1, 128], [D, 4], [128, 8]]), [], [R_const], G0, nonc=True)
            dmaop("sp", invct[:, :, :], AP(invc, 0, [[0, 128], [1, 64]]), [], [R_const], G0)
            dmaop("sp", gpool[:, 0, :], AP(nmp, 1 * D, [[0, 128], [1, D]]), [], [R_const], G0)
            dmaop("sp", gpool[:, 1, :], AP(nmp, 3 * D, [[0, 128], [1, D]]), [], [R_const], G0)
            bandf = sbA("bandf", [128, 20, 128], F32)
            R_bf = Res("bandf")
            for fam_ in range(5):
                dmaop("sp", bandf[:, 4 * fam_:4 * fam_ + 4, :], AP(bandc, fam_ * 4 * 128 * 128, [[128, 128], [128 * 128, 4], [1, 128]]),
                      [], [R_bf], G0)
            for p in range(2):
                for g in range(4):
                    dmaop("sp", pscl[:, p, 2 * g:2 * g + 2, :], AP(pscale, p * D + 256 * g, [[0, 128], [0, 2], [1, 256]]),
                          [], [R_const], G0)
            S.add("dve", lambda e: e.tensor_copy(out=band[:, :, :], in_=bandf[:, :, :]), [R_bf], [R_const])
            S.add("dve", lambda e: e.tensor_copy(out=identb[:, :], in_=ident[:, :]), [R_const], [R_const])
            S.add("pool", lambda e: e.memset(onesb[:, :], 1.0), [], [R_const])
            S.add("dve", lambda e: e.tensor_tensor(out=tab[:, :], in0=tab[:, :], in1=tab15[:, :], op=ALU.subtract),
                  [R_t], [R_t])
            S.add("pe", lambda e: e.matmul(psA[0:8, 0, 0:ZL], lhsT=tab[:, :], rhs=oht[:, :], start=True, stop=True),
                  [R_t], [R_A[0]])
            R_g = Res("Gs")
            S.add("dve", lambda e: e.tensor_copy(out=Gs[:, :], in_=psA[0:8, 0, 0:ZL]), [R_A[0]], [R_g])
            R_gd = Res("Gd")
            dmaop("pool", Gd.ap(), Gs[:, :], [R_g], [R_gd], "c1")
            R_z = Res("Zb")
            dmaop("pool", Zb.ap(), AP(Gd, 0, [[ZL, 8], [0, 128], [1, ZL]]), [R_gd], [R_z], "c1")
            R_D = Res("Dtiles")
            for h in range(NH):
                dmaop("sp", D0[:, h, :], AP(Zb, h * 128 * ZL + 127, [[ZL - 1, 128], [1, 256]]), [R_z], [R_D], "g:c2")
                dmaop("sp", Dp[:, h, :], AP(Zb, h * 128 * ZL + 255, [[ZL - 1, 128], [1, 128]]), [R_z], [R_D], "g:c2")
                dmaop("sp", Dm[:, h, :], AP(Zb, h * 128 * ZL + 143, [[ZL - 1, 16], [1, 128]]), [R_z], [R_D], "g:c2")
            S.add("pool", lambda e: e.memset(D0[64:128, :, 0:64], NEG), [R_D], [R_D, R_const])
            S.add("dve", lambda e: e.tensor_tensor(out=prod[:, 0, :], in0=L4[:, 0, :], in1=L4[:, 1, :], op=ALU.mult),
                  [R_l], [R_l])
            S.add("dve", lambda e: e.tensor_tensor(out=prod[:, 1, :], in0=L4[:, 2, :], in1=L4[:, 3, :], op=ALU.mult),
                  [R_l], [R_l])
            S.add("dve", lambda e: e.reduce_sum(out=dots[:, :], in_=prod[:, :, :].rearrange("p a (b c) -> p (a b) c", c=64),
                                                axis=AX.X), [R_l], [R_l])
            S.add("act", lambda e: e.activation(out=dots[:, :], in_=dots[:, :], func=AF.Exp), [R_l], [R_l])
            S.add("dve", lambda e: e.tensor_tensor(out=neglam[:, :], in0=dots[:, 2:4], in1=dots[:, 0:2], op=ALU.subtract),
                  [R_l], [R_l])
            for a in range(2):
                S.add("dve", lambda e, a=a: e.tensor_scalar(out=neglam[:, a:a + 1], in0=neglam[:, a:a + 1],
                                                            scalar1=-LAM_INIT[a], scalar2=None, op0=ALU.add),
                      [R_l], [R_l, R_const])
            for a in range(2):
                S.add("dve", lambda e, a=a: e.tensor_scalar(out=sg[:, a:a + 1], in0=sg[:, a:a + 1],
                                                            scalar1=1.0 - LAM_INIT[a], scalar2=None, op0=ALU.mult),
                      [R_l], [R_l, R_const])
            R_gt = Res("gtmp")
            for i, t in enumerate((nmo, nfo)):
                dmaop("sp", gtmp[:, :], AP(t, 0, [[0, 128], [1, 4 * D]]), [], [R_gt], "c3")
                S.add("dve", lambda e, i=i: e.tensor_copy(out=gpost[:, 4 * i:4 * i + 4, :], in_=gtmp[:, :]),
                      [R_gt], [R_const])

            cin = [sbA("cin%d" % i, [128, 8, 512], F32) for i in range(2)]
            cout = [sbA("cout%d" % i, [128, 8, 512], BF16) for i in range(2)]
            R_cin = [Res("cin0"), Res("cin1")]
            R_cout = [Res("cout0"), Res("cout1")]
            R_wsc = Res("wsc")
            conv = []
            for a in range(2):
                for j in range(6):
                    conv.append((blk_attn(a, j), AP(wqkv, a * D * 3 * D + 512 * j, [[3 * D, 128], [128 * 3 * D, 8], [1, 512]]),
                                 "row", 2 * a))
                for c in range(2):
                    conv.append((blk_attn(a, 6 + c), AP(wo, a * D * D + 512 * c, [[D, 128], [128 * D, 8], [1, 512]]),
                                 "plain", None))
            for p in range(2):
                conv.append((blk_pool(p), AP(wpool, p * 4 * 256 * 256, [[256, 128], [128 * 256, 8], [1, 256]]), "pool", p))
            for l in range(4):
                for j in range(8):
                    conv.append((blk_up(l, j), AP(wup, l * D * DFF + 512 * j, [[DFF, 128], [128 * DFF, 8], [1, 512]]),
                                 "row", 4 + l))
                for c in range(2):
                    for jj in range(4):
                        conv.append((blk_dn(l, c, jj), AP(wdn, l * DFF * D + (8 * jj * 128) * D + 512 * c,
                                                          [[D, 128], [128 * D, 8], [1, 512]]), "plain", None))
            cengs = ("dve", "act")
            for ci, (blk, src, kind, arg) in enumerate(conv):
                sl = ci % 2
                w = 256 if kind == "pool" else 512
                dmaop("sp", cin[sl][:, :, 0:w], src, [], [R_cin[sl]], "cin%d" % sl)
                ce = cengs[ci % 2]
                if kind == "row":
                    def fn(e, sl=sl, arg=arg, ce=ce):
                        inst = None
                        for kc in range(8):
                            if ce == "act":
                                inst = e.activation(out=cout[sl][:, kc, :], in_=cin[sl][:, kc, :], func=AF.Copy,
                                                    scale=gpre[:, arg, kc:kc + 1])
                            else:
                                inst = e.tensor_scalar(out=cout[sl][:, kc, :], in0=cin[sl][:, kc, :],
                                                       scalar1=gpre[:, arg, kc:kc + 1], scalar2=None, op0=ALU.mult)
                        return inst
                elif kind == "plain":
                    def fn(e, sl=sl, ce=ce):
                        if ce == "act":
                            return e.activation(out=cout[sl][:, :, :], in_=cin[sl][:, :, :], func=AF.Copy)
                        return e.tensor_copy(out=cout[sl][:, :, :], in_=cin[sl][:, :, :])
                else:
                    ce = "dve"

                    def fn(e, sl=sl, arg=arg):
                        return e.tensor_tensor(out=cout[sl][:, :, 0:256], in0=cin[sl][:, :, 0:256],
                                               in1=pscl[:, arg, :, :], op=ALU.mult)
                S.add(ce, fn, [R_cin[sl], R_const], [R_cout[sl]])
                dst = AP(wsc, blk * 128 * 4096, [[4096, 128], [w, 8], [1, w]])
                dmaop("pool", dst, cout[sl][:, :, 0:w], [R_cout[sl]], [R_wsc], "cout%d" % sl)

            R_KTs = Res("KTs")
            R_Vs = Res("Vs")
            ctmp = [sbA("ctmp%d" % i, [128, D], F32) for i in range(2)]
            cstg = [sbA("cstg%d" % i, [128, D], BF16) for i in range(2)]
            R_ct = [Res("ct0"), Res("ct1")]
            R_cs = [Res("cs0"), Res("cs1")]
            cc = 0
            for a in range(2):
                for r in range(PAST // 128):
                    sl = cc % 2
                    cc += 1
                    dmaop("sp", ctmp[sl][:, :], AP(ck, a * PAST * D + r * 128 * D, [[D, 128], [1, D]]), [], [R_ct[sl]],
                          "ct%d" % sl)
                    for half in range(2):
                        ai = nextA()

                        def tfn(e, sl=sl, half=half, ai=ai):
                            inst = None
                            for c4 in range(4):
                                c = half * 4 + c4
                                inst = e.transpose(out=psA[:, ai, c4 * 128:(c4 + 1) * 128],
                                                   in_=ctmp[sl][:, c * 128:(c + 1) * 128], identity=ident[:, :])
                            return inst
                        S.add("pe", tfn, [R_ct[sl], R_const], [R_A[ai]])
                        if half == 0:
                            S.add("dve", lambda e, sl=sl, ai=ai: e.tensor_copy(out=cstg[sl][:, 0:512], in_=psA[:, ai, :]),
                                  [R_A[ai]], [R_cs[sl]])
                        else:
                            S.add("act", lambda e, sl=sl, ai=ai: e.activation(out=cstg[sl][:, 512:1024], in_=psA[:, ai, :],
                                                                             func=AF.Copy), [R_A[ai]], [R_cs[sl]])
                    dmaop("pool", AP(KTs, a * NH * 128 * KTS_LEN + r * 128, [[KTS_LEN, 128], [128 * KTS_LEN, 8], [1, 128]]),
                          cstg[sl][:, :].rearrange("p (h k) -> p h k", h=8), [R_cs[sl]], [R_KTs], "cs%d" % sl)
                    sl = cc % 2
                    cc += 1
                    dmaop("sp", ctmp[sl][:, :], AP(cv, a * PAST * D + r * 128 * D, [[D, 128], [1, D]]), [], [R_ct[sl]],
                          "ct%d" % sl)
                    S.add("act", lambda e, sl=sl: e.activation(out=cstg[sl][:, :], in_=ctmp[sl][:, :], func=AF.Copy),
                          [R_ct[sl]], [R_cs[sl]])
                    dmaop("pool", AP(Vs, a * KTS_LEN * D + r * 128 * D, [[D, 128], [1, D]]), cstg[sl][:, :],
                          [R_cs[sl]], [R_Vs], "cs%d" % sl)

            R_bar = Res("barrier")
            allA = list(S.ops)
            bar_ops = []
            for en in ("sp", "act", "dve", "pool", "pe"):
                o = S.add(en, None)
                o.deps = set(allA)
                bar_ops.append(o)

        xb = sb("xb", [128, 4, D], F32)
        xh = None
        hT = sb("hT", [128, 8, TP], BF16)
        OT = hT
        uT = sb("uT", [128, 32, TP], BF16)
        KTc = uT[:, 0:8, :]
        QT = uT[:, 8:16, :]
        NW = 4
        wr = [sb("wr%d" % i, [128, 8, 512], BF16) for i in range(NW)]
        NKV = 4
        kvK = [sb("kvK%d" % i, [128, KC], BF16) for i in range(NKV)]
        kvV = [sb("kvV%d" % i, [128, KC // 128, 128], BF16) for i in range(NKV)]
        NP_ = 3
        Pt = [sb("Pt%d" % i, [128, 2, TP], BF16) for i in range(NP_)]
        stg = [sb("stg%d" % i, [128, D], F32) for i in range(2)]
        stgb = [sb("stgb%d" % i, [128, D], BF16) for i in range(2)]
        hp = [sb("hp%d" % i, [128, D], BF16) for i in range(3)]
        dT = hT
        ptmp = sb("ptmp", [128, D], F32)
        ptmp2 = sb("ptmp2", [128, D], F32)
        xhb = [sb("xhb%d" % i, [128, D], BF16) for i in range(2)]
        fo = sb("fo", [128, 2, TP], F32)
        rtmp2 = sb("rtmp2", [128, 2, TP], F32)
        rtmp = [rtmp2[:, 0, :], rtmp2[:, 1, :]]
        fin12 = rtmp2
        small = sb("small", [128, 64], F32)
        junk = sb("junk", [128, D], BF16)
        fin1 = rtmp[0]
        fin2 = rtmp[1]
        fin3 = ptmp
        finb = junk

        R_x = [[Res("xL%d" % i), Res("xR%d" % i)] for i in range(4)]
        R_xflat = [r for pr in R_x for r in pr]
        R_xh = [Res("xh0"), Res("xh1")]
        R_hT = Res("hT")
        R_QT = [Res("QT%d" % i) for i in range(NH)]
        R_OT = [Res("OT%d" % i) for i in range(NH)]
        R_uT = [Res("uT%d" % i) for i in range(32)]
        R_KTc = R_uT[0:8]
        R_wr = [Res("wr%d" % i) for i in range(NW)]
        R_kv = [Res("kv%d" % i) for i in range(NKV)]
        R_kvV = [Res("kvV%d" % i) for i in range(NKV)]
        R_P = [Res("P%d" % i) for i in range(NP_)]
        R_stg = [Res("stg0"), Res("stg1")]
        R_stgb = [Res("stgb0"), Res("stgb1")]
        R_hp = [Res("hp%d" % i) for i in range(3)]
        R_dT = R_hT
        R_fin = Res("fin")
        R_fo = Res("fo")
        pend = [None]
        R_ptmp = Res("ptmp")
        R_ptmp2 = Res("ptmp2")
        R_xhb = [Res("xhb0"), Res("xhb1")]
        R_rt = [Res("rt0"), Res("rt1")]
        R_small = [Res("small%d" % i) for i in range(64)]
        R_junk = Res("junk")
        R_KTd = [[Res("KTd%d_%d" % (a_, t_)) for t_ in range(NT_P)] for a_ in range(2)]
        R_Vd = [[Res("Vd%d_%d" % (a_, t_)) for t_ in range(NT_P)] for a_ in range(2)]
        R_KTsn = [Res("KTsn0"), Res("KTsn1")]
        R_Vsn = [Res("Vsn0"), Res("Vsn1")]
        R_out = Res("out")

        cnt = {"stg": 0, "stgb": 0, "xh": 0, "P": 0, "small": 0, "rt": 0, "B": 0, "xhb": 0, "ptmp": 0, "small4": 0, "hp": 0}

        def rot(name, n):
            i = cnt[name] % n
            cnt[name] += 1
            return i

        wplan = []
        wstate = {"issued": 0, "use": 0}

        def wload_upto(k):
            while wstate["issued"] < min(k, len(wplan)):
                i = wstate["issued"]
                blk = wplan[i]
                sl = i % NW
                w = 256 if blk in (blk_pool(0), blk_pool(1)) else 512
                dmaop("sp", wr[sl][:, :, 0:w], AP(wsc, blk * 128 * 4096, [[4096, 128], [w, 8], [1, w]]),
                      [R_wsc], [R_wr[sl]], "wr%d" % sl)
                wstate["issued"] += 1

        def wuse(blk):
            i = wstate["use"]
            assert wplan[i] == blk, (i, wplan[i], blk)
            wload_upto(i + NW - 1)
            wstate["use"] += 1
            return wr[i % NW], R_wr[i % NW]

        def rstd_from_ssq(ssq_ap, out_ap, n, eps, rs, nparts):
            S.add("act", lambda e: e.activation(out=out_ap, in_=ssq_ap, func=AF.Ln, scale=1.0 / n, bias=eps_t[0:nparts, eps:eps + 1]),
                  rs + [R_const], rs)
            S.add("act", lambda e: e.activation(out=out_ap, in_=out_ap, func=AF.Exp, scale=-0.5), rs, rs)

        eps_t = sb("eps_t", [128, 2], F32)
        S.add("pool", lambda e: e.memset(eps_t[:, 0:1], EPS), [], [R_const])
        S.add("pool", lambda e: e.memset(eps_t[:, 1:2], SUBLN_EPS), [], [R_const])

        def prenorm_T(nt, dst, R_dst_list, scale_vec=None, ext_mode=False, tail_out=None):
            nsub = (nt + 127) // 128
            rows0 = min(128, nt)
            sb0 = 48 + 4 * (cnt["small4"] % 4)
            cnt["small4"] += 1
            rsm = [R_small[sb0 + s] for s in range(nsub)]
            for s in range(nsub):
                rows = min(128, nt - s * 128)
                smc = small[0:rows, sb0 + s:sb0 + s + 1]
                S.add("act", lambda e, s=s, rows=rows, smc=smc: e.activation(out=junk[0:rows, :], in_=xb[0:rows, s, :],
                                                                             func=AF.Square, accum_out=smc),
                      R_x[s], [R_junk, R_small[sb0 + s]])
                rstd_from_ssq(smc, smc, D, 0, [R_small[sb0 + s]], rows)
            if scale_vec is None:
                xis = {}

                def emit_mult(s):
                    rows = min(128, nt - s * 128)
                    sm = small[0:rows, sb0 + s:sb0 + s + 1]
                    xi = rot("xhb", 2)
                    xis[s] = xi
                    S.add("dve", lambda e, s=s, rows=rows, sm=sm, xi=xi: e.tensor_scalar(
                        out=xhb[xi][0:rows, :], in0=xb[0:rows, s, :], scalar1=sm, scalar2=None, op0=ALU.mult),
                        R_x[s] + [R_small[sb0 + s]], [R_xhb[xi]])
                emit_mult(0)
                for s in range(nsub):
                    rows = min(128, nt - s * 128)
                    if s + 1 < nsub:
                        emit_mult(s + 1)
                    xi = xis[s]
                    ai = nextA()
                    pv = psA[:, ai, :].bitcast(BF16)

                    def tfn(e, xi=xi, rows=rows, pv=pv):
                        inst = None
                        for c in range(8):
                            inst = e.transpose(out=pv[:, c * 128:c * 128 + rows], in_=xhb[xi][0:rows, c * 128:(c + 1) * 128],
                                               identity=identb[0:rows, 0:rows])
                        return inst
                    S.add("pe", tfn, [R_xhb[xi], R_const], [R_A[ai]])
                    src_ = pv.rearrange("p (c k) -> p c k", c=8)[:, :, 0:rows]
                    d_ap = dst(0, s, rows, 8)
                    if s % 2 == 0:
                        S.add("dve", lambda e, src_=src_, d_ap=d_ap: e.tensor_copy(out=d_ap, in_=src_), [R_A[ai]], R_dst_list)
                    else:
                        S.add("act", lambda e, src_=src_, d_ap=d_ap: e.activation(out=d_ap, in_=src_, func=AF.Copy),
                              [R_A[ai]], R_dst_list)
                return
            for s in range(nsub):
                rows = min(128, nt - s * 128)
                sm = small[0:rows, sb0 + s:sb0 + s + 1]
                xi = rot("xh", 2)
                S.add("dve", lambda e, s=s, rows=rows, sm=sm, xi=xi: e.tensor_scalar(
                    out=xh[xi][0:rows, :], in0=xb[0:rows, s, :], scalar1=sm, scalar2=None, op0=ALU.mult),
                    R_x[s] + [R_small[sb0 + s]], [R_xh[xi]])
                if tail_out is not None and s == nsub - 1:
                    dram_ap, gi, is_out = tail_out
                    lo = rows - 32
                    dmaop("sp", ptmp[lo:rows, :], AP(nmp, (2 * gi + 1) * D, [[0, 32], [1, D]]), [], [R_ptmp], "gtail")
                    S.add("dve", lambda e, rows=rows, xi=xi, lo=lo: e.tensor_tensor(
                        out=ptmp[lo:rows, :], in0=xh[xi][lo:rows, :], in1=ptmp[lo:rows, :], op=ALU.mult),
                        [R_xh[xi], R_ptmp], [R_ptmp])
                    dmaop("pool", dram_ap, ptmp[rows - 15:rows, :], [R_ptmp], [R_out], "tail", is_out=is_out)
                for half in range(2):
                    ai = nextA()

                    def tfn(e, xi=xi, rows=rows, half=half, ai=ai):
                        inst = None
                        for c4 in range(4):
                            c = half * 4 + c4
                            inst = e.transpose(out=psA[:, ai, c4 * 128:c4 * 128 + rows], in_=xh[xi][0:rows, c * 128:(c + 1) * 128],
                                               identity=ident[0:rows, 0:rows])
                        return inst
                    S.add("pe", tfn, [R_xh[xi], R_const], [R_A[ai]])
                    src_ = psA[:, ai, :].rearrange("p (c k) -> p c k", c=4)[:, :, 0:rows]
                    d_ap = dst(half * 4, s, rows, 4)

                    def fn(e, src_=src_, d_ap=d_ap, half=half):
                        inst = None
                        for c4 in range(4):
                            inst = e.tensor_scalar(out=d_ap[:, c4, :], in0=src_[:, c4, :],
                                                   scalar1=gpre[:, scale_vec, half * 4 + c4:half * 4 + c4 + 1],
                                                   scalar2=None, op0=ALU.mult)
                        return inst
                    S.add("dve", fn, [R_A[ai], R_const], R_dst_list)

        def hT_dst(c0, s, rows, n=4):
            return hT[:, c0:c0 + n, s * 128:s * 128 + rows]

        def ext_dst(c0, s, rows, n=4):
            return ext[:, c0:c0 + n, 15 + s * 128:15 + s * 128 + rows]

        dbgs = []

        def dbg_x(tag, nt):
            if not dbg:
                return
            t_ = nc.dram_tensor("dbg_%s" % tag, [nt, D], F32, kind="ExternalOutput")
            dmaop("pool", t_.ap(), xb[0:nt, 0, :], R_x[0], [R_out], "dbg", is_out=True)

        def dump(tag, ap, reads):
            if not dbg:
                return
            t_ = nc.dram_tensor("dbg_%s" % tag, list(ap.shape), ap.dtype, kind="ExternalOutput")
            dmaop("pool", t_.ap(), ap, reads, [R_out], "dbg", is_out=True)

        def postnorm_add(nt, s, rows, bi, gidx):
            si = rot("small", 48)
            sm = small[0:rows, si:si + 1]
            S.add("act", lambda e: e.activation(out=junk[0:rows, :], in_=psB[0:rows, bi, :], func=AF.Square, accum_out=sm),
                  [R_B[bi]], [R_junk, R_small[si]])
            rstd_from_ssq(sm, sm, D, 0, [R_small[si]], rows)
            pi_ = rot("ptmp", 2)
            pt, Rpt = (ptmp, R_ptmp) if pi_ == 0 else (ptmp2, R_ptmp2)
            S.add("dve", lambda e: e.scalar_tensor_tensor(out=pt[0:rows, :], in0=psB[0:rows, bi, :], scalar=sm,
                                                           in1=gpost[0:rows, gidx, :], op0=ALU.mult, op1=ALU.mult),
                  [R_B[bi], R_small[si], R_const], [Rpt])
            S.add("dve", lambda e: e.tensor_tensor(out=xb[0:rows, s, 0:512], in0=xb[0:rows, s, 0:512], in1=pt[0:rows, 0:512],
                                                    op=ALU.add), [Rpt, R_x[s][0]], [R_x[s][0]])
            S.add("pool", lambda e: e.tensor_tensor(out=xb[0:rows, s, 512:1024], in0=xb[0:rows, s, 512:1024],
                                                     in1=pt[0:rows, 512:1024], op=ALU.add), [Rpt, R_x[s][1]], [R_x[s][1]])

        def proj_fm(wt, cc, src, nt, ai):
            def fn(e):
                inst = None
                for kc in range(8):
                    inst = e.matmul(psA[:, ai, 0:nt], lhsT=wt[:, kc, cc * 128:(cc + 1) * 128], rhs=src[:, kc, 0:nt],
                                    start=(kc == 0), stop=(kc == 7))
                return inst
            return fn

        def proj_tm(wt, src, s, rows, bi, half, kcs, first, last, srcoff=0):
            def fn(e):
                inst = None
                for i, kc in enumerate(kcs):
                    inst = e.matmul(psB[0:rows, bi, half * 512:(half + 1) * 512],
                                    lhsT=src[:, srcoff + kc, s * 128:s * 128 + rows], rhs=wt[:, kc, :],
                                    start=(first and i == 0), stop=(last and i == len(kcs) - 1))
                return inst
            return fn

        def attention_head(a, h, nt, segs, loads=()):
            n = len(segs)
            sbank = [None] * n

            lstate = [0]

            def emit_qk(i):
                sg_ = segs[i]
                c_ = sg_.get("chunk")
                if c_ is not None:
                    while lstate[0] < min(c_ + 3, len(loads)):
                        loads[lstate[0]]()
                        lstate[0] += 1
                nk, q0 = sg_["nk"], sg_["q0"]
                b2 = (cnt["B"] % 2) * 2
                cnt["B"] += 1
                sbank[i] = b2

                def fn(e):
                    hasb = sg_["bias"] is not None
                    e.matmul(psA[0:nk, b2, q0:nt], lhsT=sg_["kt"][0:64, :], rhs=QT[0:64, h, q0:nt], start=True, stop=not hasb)
                    inst = e.matmul(psA[0:nk, b2 + 1, q0:nt], lhsT=sg_["kt"][64:128, :], rhs=QT[64:128, h, q0:nt],
                                    start=True, stop=not hasb)
                    if hasb:
                        dt_, c0 = sg_["bias"]
                        nb = min(dt_.shape[-1], nt - c0)
                        e.matmul(psA[0:nk, b2, c0:c0 + nb], lhsT=identb[0:nk, 0:nk], rhs=dt_[:, 0:nb], start=False, stop=True)
                        inst = e.matmul(psA[0:nk, b2 + 1, c0:c0 + nb], lhsT=identb[0:nk, 0:nk], rhs=dt_[:, 0:nb],
                                        start=False, stop=True)
                    return inst
                S.add("pe", fn, [R_QT[h], R_const] + sg_["res"], [R_A[b2], R_A[b2 + 1]])

            def emit_exp_pv(i):
                sg_ = segs[i]
                nk, q0 = sg_["nk"], sg_["q0"]
                b2 = sbank[i]
                pi = rot("P", NP_)
                S.add("act", lambda e: e.activation(out=Pt[pi][0:nk, :, q0:nt], in_=psA[0:nk, b2:b2 + 2, q0:nt], func=AF.Exp),
                      [R_A[b2], R_A[b2 + 1]], [R_P[pi]])

                def fn(e):
                    st = (i == 0)
                    en = (i == n - 1)
                    e.matmul(psB[:, 0, q0:nt], lhsT=sg_["v"], rhs=Pt[pi][0:nk, 0, q0:nt], start=st, stop=en)
                    e.matmul(psB[:, 0, 512 + q0:512 + nt], lhsT=sg_["v"], rhs=Pt[pi][0:nk, 1, q0:nt], start=st, stop=en)
                    e.matmul(psB[:, 1, q0:nt], lhsT=onesb[0:nk, :], rhs=Pt[pi][0:nk, 0, q0:nt], start=st, stop=en)
                    return e.matmul(psB[:, 1, 512 + q0:512 + nt], lhsT=onesb[0:nk, :], rhs=Pt[pi][0:nk, 1, q0:nt],
                                    start=st, stop=en)
                S.add("pe", fn, [R_P[pi], R_const] + sg_["res"], [R_B[0], R_B[1]])

            emit_qk(0)
            for i in range(n):
                if i + 1 < n:
                    emit_qk(i + 1)
                emit_exp_pv(i)
                if i == min(9, n - 1) and pend[0] is not None:
                    pend[0](sbank[i])
                    pend[0] = None
            S.add("act", lambda e: e.activation(out=fo[:, :, 0:nt], in_=psB[:, 0, :].rearrange("p (a b) -> p a b", a=2)[:, :, 0:nt],
                                                func=AF.Copy), [R_B[0]], [R_fo])
            S.add("dve", lambda e: e.tensor_copy(out=fin12[:, :, 0:nt], in_=psB[:, 1, :].rearrange("p (a b) -> p a b", a=2)[:, :, 0:nt]),
                  [R_B[1]], [R_fin])
            S.add("dve", lambda e: e.reciprocal(out=fin12[:, :, 0:nt], in_=fin12[:, :, 0:nt]), [R_fin], [R_fin])
            S.add("dve", lambda e: e.tensor_tensor(out=fin12[:, 0, 0:nt], in0=fo[:, 0, 0:nt], in1=fin12[:, 0, 0:nt], op=ALU.mult),
                  [R_fo, R_fin], [R_fin])
            S.add("dve", lambda e: e.scalar_tensor_tensor(out=fin12[:, 1, 0:nt], in0=fo[:, 1, 0:nt], scalar=neglam[:, a:a + 1],
                                                           in1=fin12[:, 1, 0:nt], op0=ALU.mult, op1=ALU.mult),
                  [R_fo, R_fin, R_const], [R_fin])
            S.add("dve", lambda e: e.tensor_tensor(out=fin12[:, 0, 0:nt], in0=fin12[:, 0, 0:nt], in1=fin12[:, 1, 0:nt], op=ALU.add),
                  [R_fin], [R_fin])
            def tail(ai=None):
                if ai is None:
                    ai = nextA()
                S.add("act", lambda e: e.activation(out=finb[:, 0:nt], in_=fin12[:, 0, 0:nt], func=AF.Square), [R_fin], [R_fin])
                S.add("pe", lambda e: e.matmul(psA[:, ai, 0:nt], lhsT=onesb[:, :], rhs=finb[:, 0:nt], start=True, stop=True),
                      [R_fin, R_const], [R_A[ai]])
                S.add("act", lambda e: e.activation(out=fin3[:, 0:nt], in_=psA[:, ai, 0:nt], func=AF.Ln, scale=1.0 / 128,
                                                    bias=eps_t[:, 1:2]), [R_A[ai], R_const], [R_fin])
                S.add("act", lambda e: e.activation(out=fin3[:, 0:nt], in_=fin3[:, 0:nt], func=AF.Exp, scale=-0.5),
                      [R_fin], [R_fin])
                S.add("dve", lambda e: e.scalar_tensor_tensor(out=OT[:, h, 0:nt], in0=fin12[:, 0, 0:nt], scalar=sg[:, a:a + 1],
                                                               in1=fin3[:, 0:nt], op0=ALU.mult, op1=ALU.mult),
                      [R_fin, R_const], [R_OT[h]])
            pend[0] = tail

        kvstate = {"n": 0}

        def run_tile(kind, t):
            if kind == "meta":
                nt, xsrc, sk = NMETA, meta.ap(), 0
            elif kind == "sample":
                nt, xsrc, sk = DEC, xs.ap(), 1
            else:
                nt, xsrc, sk = TP, AP(xp, t * TP * D, [[D, TP], [1, D]]), 2
            nsub = (nt + 127) // 128
            subs = [(s, min(128, nt - s * 128)) for s in range(nsub)]
            if nt >= 128:
                for s_ in range(nsub):
                    dmaop("sp", xb[:, s_, :], xsrc[s_ * 128:(s_ + 1) * 128, :], [], R_x[s_], "xload%d" % s_)
            else:
                dmaop("sp", xb[0:nt, 0, :], xsrc, [], R_x[0], "xload")
            last_layer = 4 if kind != "meta" else 4
            for L in range(4):
                is_attn = (L % 2 == 0)
                a = L // 2
                p = L // 2
                meta_last = (kind == "meta" and L == 3)
                if is_attn:
                    prenorm_T(nt, hT_dst, [R_hT])
                    for j in range(2):
                        wt, rw = wuse(blk_attn(a, j))
                        for cc_ in range(4):
                            hc = j * 4 + cc_
                            ai = nextA()
                            S.add("pe", proj_fm(wt, cc_, hT, nt, ai), [rw, R_hT], [R_A[ai]])
                            S.add("act", lambda e, ai=ai, hc=hc: e.activation(out=QT[:, hc, 0:nt], in_=psA[:, ai, 0:nt],
                                                                             func=AF.Copy, scale=0.125),
                                  [R_A[ai]], [R_QT[hc]])
                    wk = [wuse(blk_attn(a, 2)), wuse(blk_attn(a, 3))]
                    for isK in (True, False):
                        if isK:
                            wpair = wk
                        else:
                            wpair = [wuse(blk_attn(a, 4)), wuse(blk_attn(a, 5))]
                        for (s, rows) in subs:
                            bi = rot_B()
                            for half in range(2):
                                wt, rw = wpair[half]
                                S.add("pe", proj_tm(wt, hT, s, rows, bi, half, range(8), True, True), [rw, R_hT], [R_B[bi]])
                            si_ = rot("stg", 2)
                            if isK:
                                S.add("act", lambda e, bi=bi, si_=si_, rows=rows: e.activation(
                                    out=stg[si_][0:rows, :], in_=psB[0:rows, bi, :], func=AF.Copy), [R_B[bi]], [R_stg[si_]])
                            else:
                                S.add("dve", lambda e, bi=bi, si_=si_, rows=rows: e.tensor_copy(
                                    out=stg[si_][0:rows, :], in_=psB[0:rows, bi, :]), [R_B[bi]], [R_stg[si_]])
                            if kind == "meta":
                                dst_t, off = (kp if isK else vp), a * (NMETA + SEQ) * D
                            elif kind == "sample":
                                dst_t, off = (kso if isK else vso), a * DEC * D
                            else:
                                dst_t, off = (kp if isK else vp), a * (NMETA + SEQ) * D + (NMETA + t * TP + s * 128) * D
                            dmaop("pool", AP(dst_t, off, [[D, rows], [1, D]]), stg[si_][0:rows, :],
                                  [R_stg[si_]], [R_out], "stg%d" % si_, is_out=True)
                            if isK:
                                sb_ = rot("stgb", 2)
                                S.add("pool", lambda e, si_=si_, sb_=sb_, rows=rows: e.tensor_copy(
                                    out=stgb[sb_][0:rows, :], in_=stg[si_][0:rows, :]), [R_stg[si_]], [R_stgb[sb_]])
                                ai = nextA()
                                pv = psA[:, ai, :].bitcast(BF16)

                                def ktfn(e, sb_=sb_, rows=rows, pv=pv):
                                    inst = None
                                    for c in range(8):
                                        inst = e.transpose(out=pv[:, c * 128:c * 128 + rows],
                                                           in_=stgb[sb_][0:rows, c * 128:(c + 1) * 128],
                                                           identity=identb[0:rows, 0:rows])
                                    return inst
                                S.add("pe", ktfn, [R_stgb[sb_], R_const], [R_A[ai]])
                                src_ = pv.rearrange("p (c k) -> p c k", c=8)[:, :, 0:rows]
                                if kind == "meta":
                                    S.add("dve", lambda e, src_=src_, a=a: e.tensor_copy(out=KTm[:, a, :, :], in_=src_),
                                          [R_A[ai]], [R_KTm[a]])
                                else:
                                    S.add("dve", lambda e, src_=src_, s=s, rows=rows: e.tensor_copy(
                                        out=KTc[:, :, s * 128:s * 128 + rows], in_=src_), [R_A[ai]], R_KTc)
                            if not isK:
                                if kind == "meta":
                                    S.add("pool", lambda e, si_=si_, rows=rows, a=a: e.tensor_copy(
                                        out=Vm[0:rows, a, :], in_=stg[si_][0:rows, :]), [R_stg[si_]], [R_Vm[a]])
                                else:
                                    sb_ = rot("stgb", 2)
                                    S.add("pool", lambda e, si_=si_, sb_=sb_, rows=rows: e.tensor_copy(
                                        out=stgb[sb_][0:rows, :], in_=stg[si_][0:rows, :]), [R_stg[si_]], [R_stgb[sb_]])
                                    if kind == "sample":
                                        dmaop("pool", AP(Vs, a * KTS_LEN * D + (PAST + s * 128) * D, [[D, rows], [1, D]]),
                                              stgb[sb_][0:rows, :], [R_stgb[sb_]], [R_Vsn[a]], "stgb%d" % sb_)
                                    else:
                                        dmaop("pool", AP(Vd, a * SEQ * D + (t * TP + s * 128) * D, [[D, rows], [1, D]]),
                                              stgb[sb_][0:rows, :], [R_stgb[sb_]], [R_Vd[a][t]], "stgb%d" % sb_)
                        if isK and kind == "sample":
                            dmaop("pool", AP(KTs, a * NH * 128 * KTS_LEN + PAST, [[KTS_LEN, 128], [128 * KTS_LEN, 8], [1, nt]]),
                                  KTc[:, :, 0:nt], R_KTc, [R_KTsn[a]], "ktc")
                        elif isK and kind == "prompt":
                            dmaop("pool", AP(KTd, a * NH * 128 * SEQ + t * TP, [[SEQ, 128], [128 * SEQ, 8], [1, nt]]),
                                  KTc[:, :, 0:nt], R_KTc, [R_KTd[a][t]], "ktc")
                    if kind == "meta" and a == 0:
                        dump("QT0", QT[:, 0, 0:nt], [R_QT[0]])
                        dump("KTm0", KTm[:, 0, 0, :], [R_KTm[0]])
                        dump("Vm0", Vm[0:16, 0, :], [R_Vm[0]])
                        dump("neglam", neglam[:, :], [R_const])
                        dump("sg", sg[:, :], [R_const])
                        dump("cH", cH[:, :], [R_const])
                        dump("D0", D0[:, 0, :], [R_const])
                    for h in range(NH):
                        segs = []
                        loads = []
                        if kind == "meta":
                            segs.append(dict(kt=KTm[:, a, h, :], v=Vm[0:NMETA, a, h * 128:(h + 1) * 128], nk=NMETA, q0=0,
                                             bias=(D0[0:NMETA, h, 0:NMETA], 0), res=[R_KTm[a], R_Vm[a]]))
                        else:
                            mb = None
                            if kind == "prompt" and t == 0:
                                mb = (Dm[:, h, :], 0)
                            segs.append(dict(kt=KTm[:, a, h, :], v=Vm[0:NMETA, a, h * 128:(h + 1) * 128], nk=NMETA, q0=0,
                                             bias=mb, res=[R_KTm[a], R_Vm[a]]))
                            if kind == "sample":
                                nkeys = PAST + DEC
                                ktsrc, vsrc, klen = KTs, Vs, KTS_LEN
                                rkf = lambda c, a=a: [R_KTs] if c == 0 else [R_KTsn[a]]
                                rvf = lambda c, a=a: [R_Vs] if c == 0 else [R_Vsn[a]]
                                g0 = PAST // 128
                            else:
                                nkeys = (t + 1) * TP
                                ktsrc, vsrc, klen = KTd, Vd, SEQ
                                rkf = lambda c, a=a, t=t: [R_KTd[a][tt] for tt in (2 * c, 2 * c + 1) if tt <= t]
                                rvf = lambda c, a=a, t=t: [R_Vd[a][tt] for tt in (2 * c, 2 * c + 1) if tt <= t]
                                g0 = 4 * t
                            nch = (nkeys + KC - 1) // KC
                            base = kvstate["n"]
                            kvstate["n"] += nch
                            for c in range(nch):
                                k0 = c * KC
                                kn = min(KC, nkeys - k0)
                                nkt = (kn + 127) // 128
                                sl = (base + c) % NKV

                                rk = rkf(c)
                                rv = rvf(c)

                                def ld(sl=sl, k0=k0, kn=kn, a=a, h=h, ktsrc=ktsrc, vsrc=vsrc, rk=rk, rv=rv, klen=klen):
                                    dmaop("sp", kvK[sl][:, 0:kn], AP(ktsrc, (a * NH + h) * 128 * klen + k0, [[klen, 128], [1, kn]]),
                                          rk, [R_kv[sl]], "kv%d" % sl)
                                    nfull = kn // 128
                                    if nfull:
                                        dmaop("sp", kvV[sl][:, 0:nfull, :],
                                              AP(vsrc, a * klen * D + k0 * D + h * 128, [[D, 128], [128 * D, nfull], [1, 128]]),
                                              rv, [R_kvV[sl]], "kv%d" % sl)
                                    rr = kn % 128
                                    if rr:
                                        dmaop("sp", kvV[sl][0:rr, nfull, :],
                                              AP(vsrc, a * klen * D + (k0 + nfull * 128) * D + h * 128, [[D, rr], [1, 128]]),
                                              rv, [R_kvV[sl]], "kv%d" % sl)
                                loads.append(ld)
                                for kt_ in range(nkt):
                                    G = (k0 // 128) + kt_
                                    nk = min(128, kn - kt_ * 128)
                                    j_ = G - g0
                                    if j_ < -1:
                                        b_, q0 = None, 0
                                    elif j_ == -1:
                                        b_, q0 = (Dp[0:nk, h, 0:min(128, nt)], 0), 0
                                    else:
                                        q0 = 128 * j_
                                        b_ = (D0[0:nk, h, :], q0)
                                    segs.append(dict(kt=kvK[sl][:, kt_ * 128:kt_ * 128 + nk], v=kvV[sl][0:nk, kt_, :], nk=nk,
                                                     q0=q0, bias=b_, res=[R_kv[sl], R_kvV[sl]], chunk=c))
                        attention_head(a, h, nt, segs, loads)
                    if pend[0] is not None:
                        pend[0]()
                        pend[0] = None
                    w0, rw0 = wuse(blk_attn(a, 6))
                    w1, rw1 = wuse(blk_attn(a, 7))
                    for sp0 in range(0, nsub, 2):
                        pair = subs[sp0:sp0 + 2]
                        bis = {s: rot_B() for (s, rows) in pair}
                        for (s, rows) in pair:
                            S.add("pe", proj_tm(w0, OT, s, rows, bis[s], 0, range(7), True, False), [rw0] + R_OT[0:7], [R_B[bis[s]]])
                            S.add("pe", proj_tm(w1, OT, s, rows, bis[s], 1, range(7), True, False), [rw1] + R_OT[0:7], [R_B[bis[s]]])
                        for (s, rows) in pair:
                            S.add("pe", proj_tm(w0, OT, s, rows, bis[s], 0, [7], False, True), [rw0, R_OT[7]], [R_B[bis[s]]])
                            S.add("pe", proj_tm(w1, OT, s, rows, bis[s], 1, [7], False, True), [rw1, R_OT[7]], [R_B[bis[s]]])
                            postnorm_add(nt, s, rows, bis[s], L)
                    if kind == "meta":
                        dbg_x("mix%d" % L, nt)
                else:
                    sb0 = 48 + 4 * (cnt["small4"] % 4)
                    cnt["small4"] += 1
                    for (s, rows) in subs:
                        smc = small[0:rows, sb0 + s:sb0 + s + 1]
                        S.add("act", lambda e, s=s, rows=rows, smc=smc: e.activation(out=junk[0:rows, :], in_=xb[0:rows, s, :],
                                                                                     func=AF.Square, accum_out=smc),
                              R_x[s], [R_junk, R_small[sb0 + s]])
                        rstd_from_ssq(smc, smc, D, 0, [R_small[sb0 + s]], rows)
                    if kind == "meta":
                        prev, cur_fam = None, 3
                    elif kind == "sample":
                        prev, cur_fam = (hstate[0:16, p, :], 4, 0, 16, R_hstate[p]), 0
                    elif t == 0:
                        prev, cur_fam = (hmeta[0:16, p, :], 2, 0, 16, R_hmeta[p]), 0
                    else:
                        prev, cur_fam = (hcar[64:128, p, :], 1, 64, 128, R_hcar[p]), 0
                    hpi = 0
                    for (s, rows) in subs:
                        hpi = rot("hp", 3)
                        smc = small[0:rows, sb0 + s:sb0 + s + 1]
                        S.add("dve", lambda e, s=s, rows=rows, smc=smc, hpi=hpi, p=p: e.scalar_tensor_tensor(
                            out=hp[hpi][0:rows, :], in0=xb[0:rows, s, :], scalar=smc, in1=gpool[0:rows, p, :],
                            op0=ALU.mult, op1=ALU.mult), R_x[s] + [R_small[sb0 + s], R_const], [R_hp[hpi]])
                        if s == len(subs) - 1 and (kind == "sample" or (kind == "prompt" and t == NT_P - 1)):
                            dst_t = pso if kind == "sample" else pp
                            lo = rows - 32
                            S.add("dve", lambda e, s=s, rows=rows, lo=lo, p=p: e.scalar_tensor_tensor(
                                out=ptmp[lo:rows, :], in0=xb[lo:rows, s, :], scalar=small[lo:rows, sb0 + s:sb0 + s + 1],
                                in1=gpool[lo:rows, p, :], op0=ALU.mult, op1=ALU.mult),
                                R_x[s] + [R_small[sb0 + s], R_const], [R_ptmp])
                            dmaop("pool", AP(dst_t, p * 15 * D, [[D, 15], [1, D]]), ptmp[rows - 15:rows, :], [R_ptmp], [R_out],
                                  "tail", is_out=True)
                        if not meta_last:
                            for half in range(2):
                                ai = nextA()

                                def bfn(e, s=s, rows=rows, hpi=hpi, half=half, ai=ai, prev=prev, cur_fam=cur_fam):
                                    inst = None
                                    for c4 in range(4):
                                        c = half * 4 + c4
                                        g = c // 2
                                        inst = e.matmul(psA[:, ai, c4 * 128:c4 * 128 + rows],
                                                        lhsT=hp[hpi][0:rows, c * 128:(c + 1) * 128],
                                                        rhs=band[0:rows, cur_fam * 4 + g, 0:rows], start=True, stop=(prev is None))
                                        if prev is not None:
                                            pb, fam, r0, r1, _ = prev
                                            inst = e.matmul(psA[:, ai, c4 * 128:c4 * 128 + rows],
                                                            lhsT=pb[:, c * 128:(c + 1) * 128],
                                                            rhs=band[r0:r1, fam * 4 + g, 0:rows], start=False, stop=True)
                                    return inst
                                S.add("pe", bfn, [R_hp[hpi], R_const] + ([prev[4]] if prev is not None else []), [R_A[ai]])
                                src_ = psA[:, ai, :].rearrange("p (c k) -> p c k", c=4)[:, :, 0:rows]
                                d_ap = dT[:, half * 4:half * 4 + 4, s * 128:s * 128 + rows]
                                if half == 0:
                                    S.add("dve", lambda e, src_=src_, d_ap=d_ap: e.tensor_copy(out=d_ap, in_=src_),
                                          [R_A[ai]], [R_dT])
                                else:
                                    S.add("act", lambda e, src_=src_, d_ap=d_ap: e.activation(out=d_ap, in_=src_, func=AF.Copy),
                                          [R_A[ai]], [R_dT])
                        prev = (hp[hpi][64:128, :], 1, 64, 128, R_hp[hpi])
                    if kind == "meta":
                        S.add("pool", lambda e, hpi=hpi, p=p: e.tensor_copy(out=hmeta[0:16, p, :], in_=hp[hpi][0:16, :]),
                              [R_hp[hpi]], [R_hmeta[p]])
                    elif kind == "prompt" and t < NT_P - 1:
                        S.add("pool", lambda e, hpi=hpi, p=p: e.tensor_copy(out=hcar[64:128, p, :], in_=hp[hpi][64:128, :]),
                              [R_hp[hpi]], [R_hcar[p]])
                    if meta_last:
                        break
                    wt, rw = wuse(blk_pool(p))
                    for (s, rows) in subs:
                        bi = rot_B()

                        def fn(e, s=s, rows=rows, bi=bi, wt=wt):
                            inst = None
                            for g in range(4):
                                for c2 in range(2):
                                    inst = e.matmul(psB[0:rows, bi, g * 256:(g + 1) * 256],
                                                    lhsT=dT[:, 2 * g + c2, s * 128:s * 128 + rows], rhs=wt[:, 2 * g + c2, 0:256],
                                                    start=(c2 == 0), stop=(c2 == 1))
                            return inst
                        S.add("pe", fn, [rw, R_dT], [R_B[bi]])
                        postnorm_add(nt, s, rows, bi, L)
                    if kind == "meta":
                        dbg_x("mix%d" % L, nt)
                prenorm_T(nt, hT_dst, [R_hT])
                for j in range(8):
                    wt, rw = wuse(blk_up(L, j))
                    for cc_ in range(4):
                        fc = 4 * j + cc_
                        ai = nextA()
                        S.add("pe", proj_fm(wt, cc_, hT, nt, ai), [rw, R_hT], [R_A[ai]])
                        ri = rot("rt", 2)
                        S.add("act", lambda e, ai=ai, ri=ri: e.activation(out=rtmp[ri][:, 0:nt], in_=psA[:, ai, 0:nt], func=AF.Relu),
                              [R_A[ai]], [R_rt[ri]])
                        S.add("dve" if fc % 2 == 0 else "pool",
                              lambda e, ri=ri, fc=fc: e.tensor_tensor(out=uT[:, fc, 0:nt], in0=rtmp[ri][:, 0:nt],
                                                                      in1=rtmp[ri][:, 0:nt], op=ALU.mult),
                              [R_rt[ri]], [R_uT[fc]])
                for sp0 in range(0, nsub, 2):
                    pair = subs[sp0:sp0 + 2]
                    bis = {}
                    for (s, rows) in pair:
                        bis[s] = rot_B()
                    for c in range(2):
                        for jj in range(4):
                            wt, rw = wuse(blk_dn(L, c, jj))
                            for (s, rows) in pair:
                                S.add("pe", proj_tm(wt, uT, s, rows, bis[s], c, range(8), jj == 0, jj == 3, srcoff=8 * jj),
                                      [rw] + R_uT[8 * jj:8 * jj + 8], [R_B[bis[s]]])
                    for (s, rows) in pair:
                        postnorm_add(nt, s, rows, bis[s], 4 + L)
                if kind == "meta":
                    dbg_x("ffn%d" % L, nt)
            if kind == "sample":
                dmaop("pool", ys.ap(), xb[0:nt, 0, :], R_x[0], [R_out], "yout", is_out=True)
            elif kind == "prompt":
                for s_ in range(4):
                    dmaop("pool", AP(yp, (t * TP + s_ * 128) * D, [[D, 128], [1, D]]), xb[:, s_, :],
                          R_x[s_], [R_out], "yout%d" % s_, is_out=True)

        bi_map = {}

        def rot_B():
            return rot("B_", 2)
        cnt["B_"] = 0

        def tile_plan(kind):
            pl = []
            for L in range(4):
                if L % 2 == 0:
                    pl += [blk_attn(L // 2, j) for j in range(8)]
                else:
                    if kind == "meta" and L == 3:
                        break
                    pl.append(blk_pool(L // 2))
                nsub = 4 if kind == "prompt" else 1
                pl += [blk_up(L, j) for j in range(8)]
                for sp0 in range(0, nsub, 2):
                    pl += [blk_dn(L, c, jj) for c in range(2) for jj in range(4)]
            return pl

        S.add("pool", lambda e: e.memset(hstate[:, :, :], 0.0), [], R_hstate)
        for p in range(2):
            dmaop("sp", ptmp[0:15, :], AP(stp, p * 15 * D, [[D, 15], [1, D]]), [], [R_ptmp], "stpl")
            S.add("dve", lambda e, p=p: e.tensor_copy(out=hstate[0:15, p, :], in_=ptmp[0:15, :]), [R_ptmp], [R_hstate[p]])
        tiles = [("meta", 0), ("sample", 0)] + [("prompt", t) for t in range(NT_P)]
        if stop_after is not None:
            tiles = tiles[:stop_after]
        for kind, t in tiles:
            wplan.extend(tile_plan(kind))
        for kind, t in tiles:
            run_tile(kind, t)

        sems = [es.enter_context(nc.semaphore("s%d" % i)) for i in range(90)]
        S.finalize_and_emit(sems)
    return nc


_CACHE = {}


def kernel(**inputs):
    f32 = lambda a: np.ascontiguousarray(np.asarray(a, dtype=np.float32))
    x_prompt = f32(inputs["x_prompt"]); x_sample = f32(inputs["x_sample"])
    cache_k = f32(inputs["cache_k"]); cache_v = f32(inputs["cache_v"]); state_pool = f32(inputs["state_pool"])
    if "nc" not in _CACHE:
        _CACHE["nc"] = build_program()
    nc = _CACHE["nc"]
    shared = {
        "meta": f32(inputs["meta_tokens"]), "rel": f32(inputs["rel_bias_table"]),
        "nmp": f32(inputs["norm_mix_pre"]), "nmo": f32(inputs["norm_mix_post"]),
        "nfp": f32(inputs["norm_ffn_pre"]), "nfo": f32(inputs["norm_ffn_post"]),
        "wqkv": f32(inputs["w_qkv"]),
        "lq1": f32(inputs["lambda_q1"]), "lk1": f32(inputs["lambda_k1"]),
        "lq2": f32(inputs["lambda_q2"]), "lk2": f32(inputs["lambda_k2"]),
        "subg": f32(inputs["subln_g"]), "wo": f32(inputs["w_o"]), "wpool": f32(inputs["w_pool"]),
        "pscale": f32(inputs["pool_scale"]), "wup": f32(inputs["w_up"]), "wdn": f32(inputs["w_down"]),
        "ohc": _onehot_const(), "invc": _invcnt_const(), "eye": np.eye(128, dtype=np.float32),
        "bandc": _band_const().reshape(20, 128, 128),
    }
    in_maps = []
    for b in range(NCORES):
        m = dict(shared)
        m["xp"] = x_prompt[b]
        m["xs"] = x_sample[b]
        m["ck"] = np.ascontiguousarray(cache_k[:, b].reshape(2, PAST, D))
        m["cv"] = np.ascontiguousarray(cache_v[:, b].reshape(2, PAST, D))
        m["stp"] = np.ascontiguousarray(state_pool[:, b])
        in_maps.append(m)
    res = run_bass_kernel_spmd(nc, in_maps, core_ids=list(range(NCORES)))
    R = res.results
    y_prompt = np.stack([R[b]["yp"] for b in range(NCORES)], 0)
    y_sample = np.stack([R[b]["ys"] for b in range(NCORES)], 0)
    k_prompt = np.stack([R[b]["kp"].reshape(2, NMETA + SEQ, NH, 128) for b in range(NCORES)], 1)
    v_prompt = np.stack([R[b]["vp"].reshape(2, NMETA + SEQ, NH, 128) for b in range(NCORES)], 1)
    pool_prompt = np.stack([R[b]["pp"] for b in range(NCORES)], 1)
    k_sample = np.stack([R[b]["kso"].reshape(2, DEC, NH, 128) for b in range(NCORES)], 1)
    v_sample = np.stack([R[b]["vso"].reshape(2, DEC, NH, 128) for b in range(NCORES)], 1)
    pool_sample = np.stack([R[b]["pso"] for b in range(NCORES)], 1)
    return (y_prompt, y_sample, k_prompt, v_prompt, pool_prompt, k_sample, v_sample, pool_sample)
```

```python
import math
import numpy as np
import concourse.bass as bass
import concourse.mybir as mybir
from concourse.bass_utils import run_bass_kernel_spmd

F32 = mybir.dt.float32
BF16 = mybir.dt.bfloat16
ALU = mybir.AluOpType
AF = mybir.ActivationFunctionType
AX = mybir.AxisListType
AP = bass.AP

NCORES = 8
D = 1024
NH = 8
DFF = 4096
SEQ = 8192
TP = 512
NT_P = SEQ // TP
NMETA = 16
DEC = 64
PAST = 1024
EPS = 1e-6
SUBLN_EPS = 1e-5
LAM_INIT = [0.8 - 0.6 * math.exp(-0.3 * 0), 0.8 - 0.6 * math.exp(-0.3 * 2)]
ZL = 384
NEG = -30000.0
KC = 1024
KTS_LEN = 1152


def _t5_bucket_np(rel):
    nb = 16
    max_exact = 8
    rel = np.asarray(rel, np.int32)
    ret = np.where(rel > 0, nb, 0).astype(np.int32)
    n = np.abs(rel)
    nf = np.maximum(n, 1).astype(np.float32)
    large = max_exact + (np.log(nf / np.float32(max_exact)) / np.float32(math.log(128 / max_exact))
                         * np.float32(nb - max_exact)).astype(np.int32)
    for nn, bb in ((16, 10), (32, 12), (64, 14)):
        large = np.where(n == nn, bb, large)
    large = np.minimum(large, nb - 1)
    return ret + np.where(n < max_exact, n, large)


def _onehot_const():
    rel = 127 - np.arange(ZL)
    bk = _t5_bucket_np(rel)
    oh = np.zeros((32, ZL), np.float32)
    oh[bk, np.arange(ZL)] = 1.0
    return oh


def _invcnt_const():
    inv = np.zeros((4, 16), np.float32)
    for g, w in enumerate((2, 4, 8, 16)):
        for t in range(16):
            inv[g, t] = 1.0 / min(t + 1, w)
    return inv


def _band_const():
    B = np.zeros((5, 4, 128, 128), np.float32)
    for g, w in enumerate((2, 4, 8, 16)):
        for t in range(128):
            for tp in range(max(0, t - w + 1), t + 1):
                B[0, g, tp, t] += 1.0 / w
            B[0, g, t, t] -= 1.0
            for u in range(-15, 0):
                if u > t - w:
                    B[1, g, 128 + u, t] = 1.0 / w
                    B[2, g, 16 + u, t] = 1.0 / w
                    B[4, g, 15 + u, t] = 1.0 / w
        for t in range(16):
            for tp in range(max(0, t - w + 1), t + 1):
                B[3, g, tp, t] += 1.0 / min(t + 1, w)
            B[3, g, t, t] -= 1.0
    return B


class Res:
    __slots__ = ("name", "lw", "rd")

    def __init__(self, name):
        self.name = name
        self.lw = None
        self.rd = []


class Op:
    __slots__ = ("eng", "fn", "deps", "sig", "sem", "val", "dma", "key", "n")


class Sched:
    ENGS = ("sp", "act", "dve", "pool", "pe")
    NROT = 4

    def __init__(self, nc):
        self.nc = nc
        self.ops = []
        self.out_ops = []

    def add(self, eng, fn, reads=(), writes=(), dma=False, key=None, is_out=False):
        op = Op()
        op.eng = eng
        op.fn = fn
        op.dma = dma
        op.key = key
        op.sig = False
        op.sem = None
        op.val = 0
        op.n = len(self.ops)
        deps = set()
        for r in reads:
            if r.lw is not None:
                deps.add(r.lw)
        for w in writes:
            if w.lw is not None:
                deps.add(w.lw)
            deps.update(w.rd)
        for r in reads:
            r.rd.append(op)
        for w in writes:
            w.lw = op
            w.rd = []
        deps.discard(op)
        op.deps = deps
        self.ops.append(op)
        if is_out:
            self.out_ops.append(op)
        return op

    def finalize_and_emit(self, sems_pool):
        nc = self.nc
        fin = self.add("sp", None)
        fin.deps = set(self.out_ops)
        for op in self.ops:
            for d in op.deps:
                if d.eng == "pe" and op.eng == "pe" and not d.dma:
                    continue
                d.sig = True
        gk = set(op.key for op in self.ops if op.dma and op.sig and op.key.startswith("g:"))
        for op in self.ops:
            if op.dma and op.key in gk:
                op.sig = True
        semi = iter(sems_pool)
        eng_sems = {e: [next(semi) for _ in range(self.NROT)] for e in self.ENGS if e != "sp"}
        eng_cnt = {e: 0 for e in self.ENGS}
        dma_sems = {}
        dma_cnt = {}
        for op in self.ops:
            if not op.sig:
                continue
            if op.dma:
                k = op.key
                if k not in dma_sems:
                    dma_sems[k] = next(semi)
                    dma_cnt[k] = 0
                dma_cnt[k] += 16
                op.sem = dma_sems[k]
                op.val = dma_cnt[k]
            else:
                n = eng_cnt[op.eng]
                eng_cnt[op.eng] = n + 1
                op.sem = eng_sems[op.eng][n % self.NROT]
                op.val = n // self.NROT + 1
        self.n_sems = 4 * self.NROT + len(dma_sems)
        for op in self.ops:
            if op.dma and op.sig and op.key.startswith("g:"):
                op.val = dma_cnt[op.key]

        def emit(engname, eng):
            known = {}
            for op in self.ops:
                if op.eng != engname:
                    continue
                need = {}
                for d in op.deps:
                    if not d.sig:
                        continue
                    if d.eng == "pe" and engname == "pe" and not d.dma:
                        continue
                    if op.dma and d.dma and op.key == d.key and op.key.startswith("g:"):
                        continue
                    if need.get(d.sem, 0) < d.val:
                        need[d.sem] = d.val
                for s, v in need.items():
                    if known.get(s, 0) < v:
                        eng.wait_ge(s, v)
                        known[s] = v
                if op.fn is None:
                    continue
                inst = op.fn(eng)
                if op.sig:
                    inst.then_inc(op.sem, 16 if op.dma else 1)

        with nc.Block() as block:
            @block.sync
            def _(e):
                emit("sp", e)

            @block.scalar
            def _(e):
                emit("act", e)

            @block.vector
            def _(e):
                emit("dve", e)

            @block.gpsimd
            def _(e):
                emit("pool", e)

            @block.tensor
            def _(e):
                emit("pe", e)


def build_program(stop_after=None, dbg=False):
    nc = bass.Bass("TRN2", target_bir_lowering=False)
    S = Sched(nc)

    def din(name, shape):
        return nc.dram_tensor(name, list(shape), F32, kind="ExternalInput")

    def dout(name, shape):
        return nc.dram_tensor(name, list(shape), F32, kind="ExternalOutput")

    xp = din("xp", [SEQ, D]); xs = din("xs", [DEC, D])
    ck = din("ck", [2, PAST, D]); cv = din("cv", [2, PAST, D]); stp = din("stp", [2, 15, D])
    meta = din("meta", [NMETA, D]); rel = din("rel", [32, 8])
    nmp = din("nmp", [4, D]); nmo = din("nmo", [4, D]); nfp = din("nfp", [4, D]); nfo = din("nfo", [4, D])
    wqkv = din("wqkv", [2, D, 3 * D])
    lq1 = din("lq1", [2, 64]); lk1 = din("lk1", [2, 64]); lq2 = din("lq2", [2, 64]); lk2 = din("lk2", [2, 64])
    subg = din("subg", [2, 128]); wo = din("wo", [2, D, D]); wpool = din("wpool", [2, 4, 256, 256])
    pscale = din("pscale", [2, D]); wup = din("wup", [4, D, DFF]); wdn = din("wdn", [4, DFF, D])
    ohc = din("ohc", [32, ZL]); invc = din("invc", [4, 16]); eye = din("eye", [128, 128])
    bandc = din("bandc", [20, 128, 128])

    yp = dout("yp", [SEQ, D]); ys = dout("ys", [DEC, D])
    kp = dout("kp", [2, NMETA + SEQ, D]); vp = dout("vp", [2, NMETA + SEQ, D]); pp = dout("pp", [2, 15, D])
    kso = dout("kso", [2, DEC, D]); vso = dout("vso", [2, DEC, D]); pso = dout("pso", [2, 15, D])

    NBLK = 2 * 8 + 2 * 1 + 4 * 16
    wsc = nc.dram_tensor("wsc", [NBLK, 128, 4096], BF16)
    KTd = nc.dram_tensor("KTd", [2, NH, 128, SEQ], BF16)
    Vd = nc.dram_tensor("Vd", [2, SEQ, D], BF16)
    KTs = nc.dram_tensor("KTs", [2, NH, 128, KTS_LEN], BF16)
    Vs = nc.dram_tensor("Vs", [2, KTS_LEN, D], BF16)
    Gd = nc.dram_tensor("Gd", [8, ZL], BF16)
    Zb = nc.dram_tensor("Zb", [8, 128, ZL], BF16)

    def blk_attn(a, j):
        return a * 8 + j

    def blk_pool(p):
        return 16 + p

    def blk_up(l, j):
        return 18 + l * 16 + j

    def blk_dn(l, c, jj):
        return 18 + l * 16 + 8 + c * 4 + jj

    import contextlib
    es = contextlib.ExitStack()

    def sb(name, shape, dt):
        return es.enter_context(nc.sbuf_tensor(name, list(shape), dt))

    with es:
        ident = sb("ident", [128, 128], F32)
        identb = sb("identb", [128, 128], BF16)
        onesb = sb("onesb", [128, 128], BF16)
        cH = sb("cH", [128, 8], F32)
        D0 = sb("D0", [128, 8, 256], BF16)
        Dp = sb("Dp", [128, 8, 128], BF16)
        Dm = sb("Dm", [16, 8, 128], BF16)
        neglam = sb("neglam", [128, 2], F32)
        sg = sb("sg", [128, 2], F32)
        gpre = sb("gpre", [128, 8, 8], F32)
        gpost = sb("gpost", [128, 8, D], BF16)
        invct = sb("invct", [128, 4, 16], F32)
        KTm = sb("KTm", [128, 2, NH, NMETA], BF16)
        Vm = sb("Vm", [NMETA, 2, D], BF16)
        band = sb("band", [128, 20, 128], BF16)
        gpool = sb("gpool", [128, 2, D], F32)
        hmeta = sb("hmeta", [16, 2, D], BF16)
        hstate = sb("hstate", [16, 2, D], BF16)
        hcar = sb("hcar", [128, 2, D], BF16)
        R_const = Res("const")
        R_KTm = [Res("KTm0"), Res("KTm1")]
        R_Vm = [Res("Vm0"), Res("Vm1")]
        R_hmeta = [Res("hmeta0"), Res("hmeta1")]
        R_hstate = [Res("hstate0"), Res("hstate1")]
        R_hcar = [Res("hcar0"), Res("hcar1")]

        psA = es.enter_context(nc.psum_tensor("psA", [128, 4, 512], F32))
        psB = es.enter_context(nc.psum_tensor("psB", [128, 2, 1024], F32))
        R_A = [Res("psA%d" % i) for i in range(4)]
        R_B = [Res("psB%d" % i) for i in range(2)]
        cntA = [0]

        def nextA():
            i = cntA[0] % 4
            cntA[0] += 1
            return i

        def dmaop(q, out_ap, in_ap, reads, writes, key, is_out=False, nonc=False):
            def fn(e, out_ap=out_ap, in_ap=in_ap, nonc=nonc):
                if nonc:
                    return e.dma_start(out=out_ap, in_=in_ap, allow_slow_non_contiguous=True)
                return e.dma_start(out=out_ap, in_=in_ap)
            return S.add(q, fn, reads, writes, dma=True, key=key, is_out=is_out)

        esA = contextlib.ExitStack()
        with esA:
            def sbA(name, shape, dt):
                return esA.enter_context(nc.sbuf_tensor(name, list(shape), dt))

            tab = sbA("tab", [32, 8], F32)
            tab15 = sbA("tab15", [32, 8], F32)
            oht = sbA("oht", [32, ZL], F32)
            Gs = sbA("Gs", [8, ZL], BF16)
            L4 = sbA("L4", [128, 4, 128], F32)
            prod = sbA("prod", [128, 2, 128], F32)
            dots = sbA("dots", [128, 4], F32)
            pscl = sbA("pscl", [128, 2, 8, 256], F32)
            gtmp = sbA("gtmp", [128, 4 * D], F32)
            R_t = Res("tab")
            R_l = Res("lam")
            G0 = "g:c0"
            dmaop("sp", ident[:, :], eye.ap(), [], [R_const], G0)
            dmaop("sp", cH[:, :], AP(rel, 15 * 8, [[0, 128], [1, 8]]), [], [R_const], G0)
            dmaop("sp", tab[:, :], rel.ap(), [], [R_t], G0)
            dmaop("sp", tab15[:, :], AP(rel, 15 * 8, [[0, 32], [1, 8]]), [], [R_t], G0)
            dmaop("sp", oht[:, :], ohc.ap(), [], [R_t], G0)
            for i, t in enumerate((lq1, lk1, lq2, lk2)):
                dmaop("sp", L4[:, i, :], AP(t, 0, [[0, 128], [1, 128]]), [], [R_l], G0)
            dmaop("sp", sg[:, :], AP(subg, 0, [[1, 128], [128, 2]]), [], [R_l], G0, nonc=True)
            dmaop("sp", gpre[:, 0:4, :], AP(nmp, 0, [[1, 128], [D, 4], [128, 8]]), [], [R_const], G0, nonc=True)
            dmaop("sp", gpre[:, 4:8, :], AP(nfp, 0, [[1, 128], [D, 4], [128, 8]]), [], [R_const], G0, nonc=True)
            dmaop("sp", invct[:, :, :], AP(invc, 0, [[0, 128], [1, 64]]), [], [R_const], G0)
            dmaop("sp", gpool[:, 0, :], AP(nmp, 1 * D, [[0, 128], [1, D]]), [], [R_const], G0)
            dmaop("sp", gpool[:, 1, :], AP(nmp, 3 * D, [[0, 128], [1, D]]), [], [R_const], G0)
            bandf = sbA("bandf", [128, 20, 128], F32)
            R_bf = Res("bandf")
            for fam_ in range(5):
                dmaop("sp", bandf[:, 4 * fam_:4 * fam_ + 4, :], AP(bandc, fam_ * 4 * 128 * 128, [[128, 128], [128 * 128, 4], [1, 128]]),
                      [], [R_bf], G0)
            for p in range(2):
                for g in range(4):
                    dmaop("sp", pscl[:, p, 2 * g:2 * g + 2, :], AP(pscale, p * D + 256 * g, [[0, 128], [0, 2], [1, 256]]),
                          [], [R_const], G0)
            S.add("dve", lambda e: e.tensor_copy(out=band[:, :, :], in_=bandf[:, :, :]), [R_bf], [R_const])
            S.add("dve", lambda e: e.tensor_copy(out=identb[:, :], in_=ident[:, :]), [R_const], [R_const])
            S.add("pool", lambda e: e.memset(onesb[:, :], 1.0), [], [R_const])
            S.add("dve", lambda e: e.tensor_tensor(out=tab[:, :], in0=tab[:, :], in1=tab15[:, :], op=ALU.subtract),
                  [R_t], [R_t])
            S.add("pe", lambda e: e.matmul(psA[0:8, 0, 0:ZL], lhsT=tab[:, :], rhs=oht[:, :], start=True, stop=True),
                  [R_t], [R_A[0]])
            R_g = Res("Gs")
            S.add("dve", lambda e: e.tensor_copy(out=Gs[:, :], in_=psA[0:8, 0, 0:ZL]), [R_A[0]], [R_g])
            R_gd = Res("Gd")
            dmaop("pool", Gd.ap(), Gs[:, :], [R_g], [R_gd], "c1")
            R_z = Res("Zb")
            dmaop("pool", Zb.ap(), AP(Gd, 0, [[ZL, 8], [0, 128], [1, ZL]]), [R_gd], [R_z], "c1")
            R_D = Res("Dtiles")
            for h in range(NH):
                dmaop("sp", D0[:, h, :], AP(Zb, h * 128 * ZL + 127, [[ZL - 1, 128], [1, 256]]), [R_z], [R_D], "g:c2")
                dmaop("sp", Dp[:, h, :], AP(Zb, h * 128 * ZL + 255, [[ZL - 1, 128], [1, 128]]), [R_z], [R_D], "g:c2")
                dmaop("sp", Dm[:, h, :], AP(Zb, h * 128 * ZL + 143, [[ZL - 1, 16], [1, 128]]), [R_z], [R_D], "g:c2")
            S.add("pool", lambda e: e.memset(D0[64:128, :, 0:64], NEG), [R_D], [R_D, R_const])
            S.add("dve", lambda e: e.tensor_tensor(out=prod[:, 0, :], in0=L4[:, 0, :], in1=L4[:, 1, :], op=ALU.mult),
                  [R_l], [R_l])
            S.add("dve", lambda e: e.tensor_tensor(out=prod[:, 1, :], in0=L4[:, 2, :], in1=L4[:, 3, :], op=ALU.mult),
                  [R_l], [R_l])
            S.add("dve", lambda e: e.reduce_sum(out=dots[:, :], in_=prod[:, :, :].rearrange("p a (b c) -> p (a b) c", c=64),
                                                axis=AX.X), [R_l], [R_l])
            S.add("act", lambda e: e.activation(out=dots[:, :], in_=dots[:, :], func=AF.Exp), [R_l], [R_l])
            S.add("dve", lambda e: e.tensor_tensor(out=neglam[:, :], in0=dots[:, 2:4], in1=dots[:, 0:2], op=ALU.subtract),
                  [R_l], [R_l])
            for a in range(2):
                S.add("dve", lambda e, a=a: e.tensor_scalar(out=neglam[:, a:a + 1], in0=neglam[:, a:a + 1],
                                                            scalar1=-LAM_INIT[a], scalar2=None, op0=ALU.add),
                      [R_l], [R_l, R_const])
            for a in range(2):
                S.add("dve", lambda e, a=a: e.tensor_scalar(out=sg[:, a:a + 1], in0=sg[:, a:a + 1],
                                                            scalar1=1.0 - LAM_INIT[a], scalar2=None, op0=ALU.mult),
                      [R_l], [R_l, R_const])
            R_gt = Res("gtmp")
            for i, t in enumerate((nmo, nfo)):
                dmaop("sp", gtmp[:, :], AP(t, 0, [[0, 128], [1, 4 * D]]), [], [R_gt], "c3")
                S.add("dve", lambda e, i=i: e.tensor_copy(out=gpost[:, 4 * i:4 * i + 4, :], in_=gtmp[:, :]),
                      [R_gt], [R_const])

            cin = [sbA("cin%d" % i, [128, 8, 512], F32) for i in range(2)]
            cout = [sbA("cout%d" % i, [128, 8, 512], BF16) for i in range(2)]
            R_cin = [Res("cin0"), Res("cin1")]
            R_cout = [Res("cout0"), Res("cout1")]
            R_wsc = Res("wsc")
            conv = []
            for a in range(2):
                for j in range(6):
                    conv.append((blk_attn(a, j), AP(wqkv, a * D * 3 * D + 512 * j, [[3 * D, 128], [128 * 3 * D, 8], [1, 512]]),
                                 "row", 2 * a))
                for c in range(2):
                    conv.append((blk_attn(a, 6 + c), AP(wo, a * D * D + 512 * c, [[D, 128], [128 * D, 8], [1, 512]]),
                                 "plain", None))
            for p in range(2):
                conv.append((blk_pool(p), AP(wpool, p * 4 * 256 * 256, [[256, 128], [128 * 256, 8], [1, 256]]), "pool", p))
            for l in range(4):
                for j in range(8):
                    conv.append((blk_up(l, j), AP(wup, l * D * DFF + 512 * j, [[DFF, 128], [128 * DFF, 8], [1, 512]]),
                                 "row", 4 + l))
                for c in range(2):
                    for jj in range(4):
                        conv.append((blk_dn(l, c, jj), AP(wdn, l * DFF * D + (8 * jj * 128) * D + 512 * c,
                                                          [[D, 128], [128 * D, 8], [1, 512]]), "plain", None))
            cengs = ("dve", "act")
            for ci, (blk, src, kind, arg) in enumerate(conv):
                sl = ci % 2
                w = 256 if kind == "pool" else 512
                dmaop("sp", cin[sl][:, :, 0:w], src, [], [R_cin[sl]], "cin%d" % sl)
                ce = cengs[ci % 2]
                if kind == "row":
                    def fn(e, sl=sl, arg=arg, ce=ce):
                        inst = None
                        for kc in range(8):
                            if ce == "act":
                                inst = e.activation(out=cout[sl][:, kc, :], in_=cin[sl][:, kc, :], func=AF.Copy,
                                                    scale=gpre[:, arg, kc:kc + 1])
                            else:
                                inst = e.tensor_scalar(out=cout[sl][:, kc, :], in0=cin[sl][:, kc, :],
                                                       scalar1=gpre[:, arg, kc:kc + 1], scalar2=None, op0=ALU.mult)
                        return inst
                elif kind == "plain":
                    def fn(e, sl=sl, ce=ce):
                        if ce == "act":
                            return e.activation(out=cout[sl][:, :, :], in_=cin[sl][:, :, :], func=AF.Copy)
                        return e.tensor_copy(out=cout[sl][:, :, :], in_=cin[sl][:, :, :])
                else:
                    ce = "dve"

                    def fn(e, sl=sl, arg=arg):
                        return e.tensor_tensor(out=cout[sl][:, :, 0:256], in0=cin[sl][:, :, 0:256],
                                               in1=pscl[:, arg, :, :], op=ALU.mult)
                S.add(ce, fn, [R_cin[sl], R_const], [R_cout[sl]])
                dst = AP(wsc, blk * 128 * 4096, [[4096, 128], [w, 8], [1, w]])
                dmaop("pool", dst, cout[sl][:, :, 0:w], [R_cout[sl]], [R_wsc], "cout%d" % sl)

            R_KTs = Res("KTs")
            R_Vs = Res("Vs")
            ctmp = [sbA("ctmp%d" % i, [128, D], F32) for i in range(2)]
            cstg = [sbA("cstg%d" % i, [128, D], BF16) for i in range(2)]
            R_ct = [Res("ct0"), Res("ct1")]
            R_cs = [Res("cs0"), Res("cs1")]
            cc = 0
            for a in range(2):
                for r in range(PAST // 128):
                    sl = cc % 2
                    cc += 1
                    dmaop("sp", ctmp[sl][:, :], AP(ck, a * PAST * D + r * 128 * D, [[D, 128], [1, D]]), [], [R_ct[sl]],
                          "ct%d" % sl)
                    for half in range(2):
                        ai = nextA()

                        def tfn(e, sl=sl, half=half, ai=ai):
                            inst = None
                            for c4 in range(4):
                                c = half * 4 + c4
                                inst = e.transpose(out=psA[:, ai, c4 * 128:(c4 + 1) * 128],
                                                   in_=ctmp[sl][:, c * 128:(c + 1) * 128], identity=ident[:, :])
                            return inst
                        S.add("pe", tfn, [R_ct[sl], R_const], [R_A[ai]])
                        if half == 0:
                            S.add("dve", lambda e, sl=sl, ai=ai: e.tensor_copy(out=cstg[sl][:, 0:512], in_=psA[:, ai, :]),
                                  [R_A[ai]], [R_cs[sl]])
                        else:
                            S.add("act", lambda e, sl=sl, ai=ai: e.activation(out=cstg[sl][:, 512:1024], in_=psA[:, ai, :],
                                                                             func=AF.Copy), [R_A[ai]], [R_cs[sl]])
                    dmaop("pool", AP(KTs, a * NH * 128 * KTS_LEN + r * 128, [[KTS_LEN, 128], [128 * KTS_LEN, 8], [1, 128]]),
                          cstg[sl][:, :].rearrange("p (h k) -> p h k", h=8), [R_cs[sl]], [R_KTs], "cs%d" % sl)
                    sl = cc % 2
                    cc += 1
                    dmaop("sp", ctmp[sl][:, :], AP(cv, a * PAST * D + r * 128 * D, [[D, 128], [1, D]]), [], [R_ct[sl]],
                          "ct%d" % sl)
                    S.add("act", lambda e, sl=sl: e.activation(out=cstg[sl][:, :], in_=ctmp[sl][:, :], func=AF.Copy),
                          [R_ct[sl]], [R_cs[sl]])
                    dmaop("pool", AP(Vs, a * KTS_LEN * D + r * 128 * D, [[D, 128], [1, D]]), cstg[sl][:, :],
                          [R_cs[sl]], [R_Vs], "cs%d" % sl)

            R_bar = Res("barrier")
            allA = list(S.ops)
            bar_ops = []
            for en in ("sp", "act", "dve", "pool", "pe"):
                o = S.add(en, None)
                o.deps = set(allA)
                bar_ops.append(o)

        xb = sb("xb", [128, 4, D], F32)
        xh = None
        hT = sb("hT", [128, 8, TP], BF16)
        OT = hT
        uT = sb("uT", [128, 32, TP], BF16)
        KTc = uT[:, 0:8, :]
        QT = uT[:, 8:16, :]
        NW = 4
        wr = [sb("wr%d" % i, [128, 8, 512], BF16) for i in range(NW)]
        NKV = 4
        kvK = [sb("kvK%d" % i, [128, KC], BF16) for i in range(NKV)]
        kvV = [sb("kvV%d" % i, [128, KC // 128, 128], BF16) for i in range(NKV)]
        NP_ = 3
        Pt = [sb("Pt%d" % i, [128, 2, TP], BF16) for i in range(NP_)]
        stg = [sb("stg%d" % i, [128, D], F32) for i in range(2)]
        stgb = [sb("stgb%d" % i, [128, D], BF16) for i in range(2)]
        hp = [sb("hp%d" % i, [128, D], BF16) for i in range(3)]
        dT = hT
        ptmp = sb("ptmp", [128, D], F32)
        ptmp2 = sb("ptmp2", [128, D], F32)
        xhb = [sb("xhb%d" % i, [128, D], BF16) for i in range(2)]
        fo = sb("fo", [128, 2, TP], F32)
        rtmp2 = sb("rtmp2", [128, 2, TP], F32)
        rtmp = [rtmp2[:, 0, :], rtmp2[:, 1, :]]
        fin12 = rtmp2
        small = sb("small", [128, 64], F32)
        junk = sb("junk", [128, D], BF16)
        fin1 = rtmp[0]
        fin2 = rtmp[1]
        fin3 = ptmp
        finb = junk

        R_x = [[Res("xL%d" % i), Res("xR%d" % i)] for i in range(4)]
        R_xflat = [r for pr in R_x for r in pr]
        R_xh = [Res("xh0"), Res("xh1")]
        R_hT = Res("hT")
        R_QT = [Res("QT%d" % i) for i in range(NH)]
        R_OT = [Res("OT%d" % i) for i in range(NH)]
        R_uT = [Res("uT%d" % i) for i in range(32)]
        R_KTc = R_uT[0:8]
        R_wr = [Res("wr%d" % i) for i in range(NW)]
        R_kv = [Res("kv%d" % i) for i in range(NKV)]
        R_kvV = [Res("kvV%d" % i) for i in range(NKV)]
        R_P = [Res("P%d" % i) for i in range(NP_)]
        R_stg = [Res("stg0"), Res("stg1")]
        R_stgb = [Res("stgb0"), Res("stgb1")]
        R_hp = [Res("hp%d" % i) for i in range(3)]
        R_dT = R_hT
        R_fin = Res("fin")
        R_fo = Res("fo")
        pend = [None]
        R_ptmp = Res("ptmp")
        R_ptmp2 = Res("ptmp2")
        R_xhb = [Res("xhb0"), Res("xhb1")]
        R_rt = [Res("rt0"), Res("rt1")]
        R_small = [Res("small%d" % i) for i in range(64)]
        R_junk = Res("junk")
        R_KTd = [[Res("KTd%d_%d" % (a_, t_)) for t_ in range(NT_P)] for a_ in range(2)]
        R_Vd = [[Res("Vd%d_%d" % (a_, t_)) for t_ in range(NT_P)] for a_ in range(2)]
        R_KTsn = [Res("KTsn0"), Res("KTsn1")]
        R_Vsn = [Res("Vsn0"), Res("Vsn1")]
        R_out = Res("out")

        cnt = {"stg": 0, "stgb": 0, "xh": 0, "P": 0, "small": 0, "rt": 0, "B": 0, "xhb": 0, "ptmp": 0, "small4": 0, "hp": 0}

        def rot(name, n):
            i = cnt[name] % n
            cnt[name] += 1
            return i

        wplan = []
        wstate = {"issued": 0, "use": 0}

        def wload_upto(k):
            while wstate["issued"] < min(k, len(wplan)):
                i = wstate["issued"]
                blk = wplan[i]
                sl = i % NW
                w = 256 if blk in (blk_pool(0), blk_pool(1)) else 512
                dmaop("sp", wr[sl][:, :, 0:w], AP(wsc, blk * 128 * 4096, [[4096, 128], [w, 8], [1, w]]),
                      [R_wsc], [R_wr[sl]], "wr%d" % sl)
                wstate["issued"] += 1

        def wuse(blk):
            i = wstate["use"]
            assert wplan[i] == blk, (i, wplan[i], blk)
            wload_upto(i + NW - 1)
            wstate["use"] += 1
            return wr[i % NW], R_wr[i % NW]

        def rstd_from_ssq(ssq_ap, out_ap, n, eps, rs, nparts):
            S.add("act", lambda e: e.activation(out=out_ap, in_=ssq_ap, func=AF.Ln, scale=1.0 / n, bias=eps_t[0:nparts, eps:eps + 1]),
                  rs + [R_const], rs)
            S.add("act", lambda e: e.activation(out=out_ap, in_=out_ap, func=AF.Exp, scale=-0.5), rs, rs)

        eps_t = sb("eps_t", [128, 2], F32)
        S.add("pool", lambda e: e.memset(eps_t[:, 0:1], EPS), [], [R_const])
        S.add("pool", lambda e: e.memset(eps_t[:, 1:2], SUBLN_EPS), [], [R_const])

        def prenorm_T(nt, dst, R_dst_list, scale_vec=None, ext_mode=False, tail_out=None):
            nsub = (nt + 127) // 128
            rows0 = min(128, nt)
            sb0 = 48 + 4 * (cnt["small4"] % 4)
            cnt["small4"] += 1
            rsm = [R_small[sb0 + s] for s in range(nsub)]
            for s in range(nsub):
                rows = min(128, nt - s * 128)
                smc = small[0:rows, sb0 + s:sb0 + s + 1]
                S.add("act", lambda e, s=s, rows=rows, smc=smc: e.activation(out=junk[0:rows, :], in_=xb[0:rows, s, :],
                                                                             func=AF.Square, accum_out=smc),
                      R_x[s], [R_junk, R_small[sb0 + s]])
                rstd_from_ssq(smc, smc, D, 0, [R_small[sb0 + s]], rows)
            if scale_vec is None:
                xis = {}

                def emit_mult(s):
                    rows = min(128, nt - s * 128)
                    sm = small[0:rows, sb0 + s:sb0 + s + 1]
                    xi = rot("xhb", 2)
                    xis[s] = xi
                    S.add("dve", lambda e, s=s, rows=rows, sm=sm, xi=xi: e.tensor_scalar(
                        out=xhb[xi][0:rows, :], in0=xb[0:rows, s, :], scalar1=sm, scalar2=None, op0=ALU.mult),
                        R_x[s] + [R_small[sb0 + s]], [R_xhb[xi]])
                emit_mult(0)
                for s in range(nsub):
                    rows = min(128, nt - s * 128)
                    if s + 1 < nsub:
                        emit_mult(s + 1)
                    xi = xis[s]
                    ai = nextA()
                    pv = psA[:, ai, :].bitcast(BF16)

                    def tfn(e, xi=xi, rows=rows, pv=pv):
                        inst = None
                        for c in range(8):
                            inst = e.transpose(out=pv[:, c * 128:c * 128 + rows], in_=xhb[xi][0:rows, c * 128:(c + 1) * 128],
                                               identity=identb[0:rows, 0:rows])
                        return inst
                    S.add("pe", tfn, [R_xhb[xi], R_const], [R_A[ai]])
                    src_ = pv.rearrange("p (c k) -> p c k", c=8)[:, :, 0:rows]
                    d_ap = dst(0, s, rows, 8)
                    if s % 2 == 0:
                        S.add("dve", lambda e, src_=src_, d_ap=d_ap: e.tensor_copy(out=d_ap, in_=src_), [R_A[ai]], R_dst_list)
                    else:
                        S.add("act", lambda e, src_=src_, d_ap=d_ap: e.activation(out=d_ap, in_=src_, func=AF.Copy),
                              [R_A[ai]], R_dst_list)
                return
            for s in range(nsub):
                rows = min(128, nt - s * 128)
                sm = small[0:rows, sb0 + s:sb0 + s + 1]
                xi = rot("xh", 2)
                S.add("dve", lambda e, s=s, rows=rows, sm=sm, xi=xi: e.tensor_scalar(
                    out=xh[xi][0:rows, :], in0=xb[0:rows, s, :], scalar1=sm, scalar2=None, op0=ALU.mult),
                    R_x[s] + [R_small[sb0 + s]], [R_xh[xi]])
                if tail_out is not None and s == nsub - 1:
                    dram_ap, gi, is_out = tail_out
                    lo = rows - 32
                    dmaop("sp", ptmp[lo:rows, :], AP(nmp, (2 * gi + 1) * D, [[0, 32], [1, D]]), [], [R_ptmp], "gtail")
                    S.add("dve", lambda e, rows=rows, xi=xi, lo=lo: e.tensor_tensor(
                        out=ptmp[lo:rows, :], in0=xh[xi][lo:rows, :], in1=ptmp[lo:rows, :], op=ALU.mult),
                        [R_xh[xi], R_ptmp], [R_ptmp])
                    dmaop("pool", dram_ap, ptmp[rows - 15:rows, :], [R_ptmp], [R_out], "tail", is_out=is_out)
                for half in range(2):
                    ai = nextA()

                    def tfn(e, xi=xi, rows=rows, half=half, ai=ai):
                        inst = None
                        for c4 in range(4):
                            c = half * 4 + c4
                            inst = e.transpose(out=psA[:, ai, c4 * 128:c4 * 128 + rows], in_=xh[xi][0:rows, c * 128:(c + 1) * 128],
                                               identity=ident[0:rows, 0:rows])
                        return inst
                    S.add("pe", tfn, [R_xh[xi], R_const], [R_A[ai]])
                    src_ = psA[:, ai, :].rearrange("p (c k) -> p c k", c=4)[:, :, 0:rows]
                    d_ap = dst(half * 4, s, rows, 4)

                    def fn(e, src_=src_, d_ap=d_ap, half=half):
                        inst = None
                        for c4 in range(4):
                            inst = e.tensor_scalar(out=d_ap[:, c4, :], in0=src_[:, c4, :],
                                                   scalar1=gpre[:, scale_vec, half * 4 + c4:half * 4 + c4 + 1],
                                                   scalar2=None, op0=ALU.mult)
                        return inst
                    S.add("dve", fn, [R_A[ai], R_const], R_dst_list)

        def hT_dst(c0, s, rows, n=4):
            return hT[:, c0:c0 + n, s * 128:s * 128 + rows]

        def ext_dst(c0, s, rows, n=4):
            return ext[:, c0:c0 + n, 15 + s * 128:15 + s * 128 + rows]

        dbgs = []

        def dbg_x(tag, nt):
            if not dbg:
                return
            t_ = nc.dram_tensor("dbg_%s" % tag, [nt, D], F32, kind="ExternalOutput")
            dmaop("pool", t_.ap(), xb[0:nt, 0, :], R_x[0], [R_out], "dbg", is_out=True)

        def dump(tag, ap, reads):
            if not dbg:
                return
            t_ = nc.dram_tensor("dbg_%s" % tag, list(ap.shape), ap.dtype, kind="ExternalOutput")
            dmaop("pool", t_.ap(), ap, reads, [R_out], "dbg", is_out=True)

        def postnorm_add(nt, s, rows, bi, gidx):
            si = rot("small", 48)
            sm = small[0:rows, si:si + 1]
            S.add("act", lambda e: e.activation(out=junk[0:rows, :], in_=psB[0:rows, bi, :], func=AF.Square, accum_out=sm),
                  [R_B[bi]], [R_junk, R_small[si]])
            rstd_from_ssq(sm, sm, D, 0, [R_small[si]], rows)
            pi_ = rot("ptmp", 2)
            pt, Rpt = (ptmp, R_ptmp) if pi_ == 0 else (ptmp2, R_ptmp2)
            S.add("dve", lambda e: e.scalar_tensor_tensor(out=pt[0:rows, :], in0=psB[0:rows, bi, :], scalar=sm,
                                                           in1=gpost[0:rows, gidx, :], op0=ALU.mult, op1=ALU.mult),
                  [R_B[bi], R_small[si], R_const], [Rpt])
            S.add("dve", lambda e: e.tensor_tensor(out=xb[0:rows, s, 0:512], in0=xb[0:rows, s, 0:512], in1=pt[0:rows, 0:512],
                                                    op=ALU.add), [Rpt, R_x[s][0]], [R_x[s][0]])
            S.add("pool", lambda e: e.tensor_tensor(out=xb[0:rows, s, 512:1024], in0=xb[0:rows, s, 512:1024],
                                                     in1=pt[0:rows, 512:1024], op=ALU.add), [Rpt, R_x[s][1]], [R_x[s][1]])

        def proj_fm(wt, cc, src, nt, ai):
            def fn(e):
                inst = None
                for kc in range(8):
                    inst = e.matmul(psA[:, ai, 0:nt], lhsT=wt[:, kc, cc * 128:(cc + 1) * 128], rhs=src[:, kc, 0:nt],
                                    start=(kc == 0), stop=(kc == 7))
                return inst
            return fn

        def proj_tm(wt, src, s, rows, bi, half, kcs, first, last, srcoff=0):
            def fn(e):
                inst = None
                for i, kc in enumerate(kcs):
                    inst = e.matmul(psB[0:rows, bi, half * 512:(half + 1) * 512],
                                    lhsT=src[:, srcoff + kc, s * 128:s * 128 + rows], rhs=wt[:, kc, :],
                                    start=(first and i == 0), stop=(last and i == len(kcs) - 1))
                return inst
            return fn

        def attention_head(a, h, nt, segs, loads=()):
            n = len(segs)
            sbank = [None] * n

            lstate = [0]

            def emit_qk(i):
                sg_ = segs[i]
                c_ = sg_.get("chunk")
                if c_ is not None:
                    while lstate[0] < min(c_ + 3, len(loads)):
                        loads[lstate[0]]()
                        lstate[0] += 1
                nk, q0 = sg_["nk"], sg_["q0"]
                b2 = (cnt["B"] % 2) * 2
                cnt["B"] += 1
                sbank[i] = b2

                def fn(e):
                    hasb = sg_["bias"] is not None
                    e.matmul(psA[0:nk, b2, q0:nt], lhsT=sg_["kt"][0:64, :], rhs=QT[0:64, h, q0:nt], start=True, stop=not hasb)
                    inst = e.matmul(psA[0:nk, b2 + 1, q0:nt], lhsT=sg_["kt"][64:128, :], rhs=QT[64:128, h, q0:nt],
                                    start=True, stop=not hasb)
                    if hasb:
                        dt_, c0 = sg_["bias"]
                        nb = min(dt_.shape[-1], nt - c0)
                        e.matmul(psA[0:nk, b2, c0:c0 + nb], lhsT=identb[0:nk, 0:nk], rhs=dt_[:, 0:nb], start=False, stop=True)
                        inst = e.matmul(psA[0:nk, b2 + 1, c0:c0 + nb], lhsT=identb[0:nk, 0:nk], rhs=dt_[:, 0:nb],
                                        start=False, stop=True)
                    return inst
                S.add("pe", fn, [R_QT[h], R_const] + sg_["res"], [R_A[b2], R_A[b2 + 1]])

            def emit_exp_pv(i):
                sg_ = segs[i]
                nk, q0 = sg_["nk"], sg_["q0"]
                b2 = sbank[i]
                pi = rot("P", NP_)
                S.add("act", lambda e: e.activation(out=Pt[pi][0:nk, :, q0:nt], in_=psA[0:nk, b2:b2 + 2, q0:nt], func=AF.Exp),
                      [R_A[b2], R_A[b2 + 1]], [R_P[pi]])

                def fn(e):
                    st = (i == 0)
                    en = (i == n - 1)
                    e.matmul(psB[:, 0, q0:nt], lhsT=sg_["v"], rhs=Pt[pi][0:nk, 0, q0:nt], start=st, stop=en)
                    e.matmul(psB[:, 0, 512 + q0:512 + nt], lhsT=sg_["v"], rhs=Pt[pi][0:nk, 1, q0:nt], start=st, stop=en)
                    e.matmul(psB[:, 1, q0:nt], lhsT=onesb[0:nk, :], rhs=Pt[pi][0:nk, 0, q0:nt], start=st, stop=en)
                    return e.matmul(psB[:, 1, 512 + q0:512 + nt], lhsT=onesb[0:nk, :], rhs=Pt[pi][0:nk, 1, q0:nt],
                                    start=st, stop=en)
                S.add("pe", fn, [R_P[pi], R_const] + sg_["res"], [R_B[0], R_B[1]])

            emit_qk(0)
            for i in range(n):
                if i + 1 < n:
                    emit_qk(i + 1)
                emit_exp_pv(i)
                if i == min(9, n - 1) and pend[0] is not None:
                    pend[0](sbank[i])
                    pend[0] = None
            S.add("act", lambda e: e.activation(out=fo[:, :, 0:nt], in_=psB[:, 0, :].rearrange("p (a b) -> p a b", a=2)[:, :, 0:nt],
                                                func=AF.Copy), [R_B[0]], [R_fo])
            S.add("dve", lambda e: e.tensor_copy(out=fin12[:, :, 0:nt], in_=psB[:, 1, :].rearrange("p (a b) -> p a b", a=2)[:, :, 0:nt]),
                  [R_B[1]], [R_fin])
            S.add("dve", lambda e: e.reciprocal(out=fin12[:, :, 0:nt], in_=fin12[:, :, 0:nt]), [R_fin], [R_fin])
            S.add("dve", lambda e: e.tensor_tensor(out=fin12[:, 0, 0:nt], in0=fo[:, 0, 0:nt], in1=fin12[:, 0, 0:nt], op=ALU.mult),
                  [R_fo, R_fin], [R_fin])
            S.add("dve", lambda e: e.scalar_tensor_tensor(out=fin12[:, 1, 0:nt], in0=fo[:, 1, 0:nt], scalar=neglam[:, a:a + 1],
                                                           in1=fin12[:, 1, 0:nt], op0=ALU.mult, op1=ALU.mult),
                  [R_fo, R_fin, R_const], [R_fin])
            S.add("dve", lambda e: e.tensor_tensor(out=fin12[:, 0, 0:nt], in0=fin12[:, 0, 0:nt], in1=fin12[:, 1, 0:nt], op=ALU.add),
                  [R_fin], [R_fin])
            def tail(ai=None):
                if ai is None:
                    ai = nextA()
                S.add("act", lambda e: e.activation(out=finb[:, 0:nt], in_=fin12[:, 0, 0:nt], func=AF.Square), [R_fin], [R_fin])
                S.add("pe", lambda e: e.matmul(psA[:, ai, 0:nt], lhsT=onesb[:, :], rhs=finb[:, 0:nt], start=True, stop=True),
                      [R_fin, R_const], [R_A[ai]])
                S.add("act", lambda e: e.activation(out=fin3[:, 0:nt], in_=psA[:, ai, 0:nt], func=AF.Ln, scale=1.0 / 128,
                                                    bias=eps_t[:, 1:2]), [R_A[ai], R_const], [R_fin])
                S.add("act", lambda e: e.activation(out=fin3[:, 0:nt], in_=fin3[:, 0:nt], func=AF.Exp, scale=-0.5),
                      [R_fin], [R_fin])
                S.add("dve", lambda e: e.scalar_tensor_tensor(out=OT[:, h, 0:nt], in0=fin12[:, 0, 0:nt], scalar=sg[:, a:a + 1],
                                                               in1=fin3[:, 0:nt], op0=ALU.mult, op1=ALU.mult),
                      [R_fin, R_const], [R_OT[h]])
            pend[0] = tail

        kvstate = {"n": 0}

        def run_tile(kind, t):
            if kind == "meta":
                nt, xsrc, sk = NMETA, meta.ap(), 0
            elif kind == "sample":
                nt, xsrc, sk = DEC, xs.ap(), 1
            else:
                nt, xsrc, sk = TP, AP(xp, t * TP * D, [[D, TP], [1, D]]), 2
            nsub = (nt + 127) // 128
            subs = [(s, min(128, nt - s * 128)) for s in range(nsub)]
            if nt >= 128:
                for s_ in range(nsub):
                    dmaop("sp", xb[:, s_, :], xsrc[s_ * 128:(s_ + 1) * 128, :], [], R_x[s_], "xload%d" % s_)
            else:
                dmaop("sp", xb[0:nt, 0, :], xsrc, [], R_x[0], "xload")
            last_layer = 4 if kind != "meta" else 4
            for L in range(4):
                is_attn = (L % 2 == 0)
                a = L // 2
                p = L // 2
                meta_last = (kind == "meta" and L == 3)
                if is_attn:
                    prenorm_T(nt, hT_dst, [R_hT])
                    for j in range(2):
                        wt, rw = wuse(blk_attn(a, j))
                        for cc_ in range(4):
                            hc = j * 4 + cc_
                            ai = nextA()
                            S.add("pe", proj_fm(wt, cc_, hT, nt, ai), [rw, R_hT], [R_A[ai]])
                            S.add("act", lambda e, ai=ai, hc=hc: e.activation(out=QT[:, hc, 0:nt], in_=psA[:, ai, 0:nt],
                                                                             func=AF.Copy, scale=0.125),
                                  [R_A[ai]], [R_QT[hc]])
                    wk = [wuse(blk_attn(a, 2)), wuse(blk_attn(a, 3))]
                    for j, (wt, rw) in enumerate(wk):
                        for cc_ in range(4):
                            hc = j * 4 + cc_
                            ai = nextA()
                            S.add("pe", proj_fm(wt, cc_, hT, nt, ai), [rw, R_hT], [R_A[ai]])
                            if kind == "meta":
                                S.add("dve", lambda e, ai=ai, hc=hc, a=a: e.tensor_copy(out=KTm[:, a, hc, :], in_=psA[:, ai, 0:nt]),
                                      [R_A[ai]], [R_KTm[a]])
                            else:
                                S.add("dve", lambda e, ai=ai, hc=hc: e.tensor_copy(out=KTc[:, hc, 0:nt], in_=psA[:, ai, 0:nt]),
                                      [R_A[ai]], [R_uT[hc]])
                    if kind == "sample":
                        dmaop("pool", AP(KTs, a * NH * 128 * KTS_LEN + PAST, [[KTS_LEN, 128], [128 * KTS_LEN, 8], [1, nt]]),
                              KTc[:, :, 0:nt], R_KTc, [R_KTsn[a]], "ktc")
                    elif kind == "prompt":
                        dmaop("pool", AP(KTd, a * NH * 128 * SEQ + t * TP, [[SEQ, 128], [128 * SEQ, 8], [1, nt]]),
                              KTc[:, :, 0:nt], R_KTc, [R_KTd[a][t]], "ktc")
                    for isK in (True, False):
                        if isK:
                            wpair = wk
                        else:
                            wpair = [wuse(blk_attn(a, 4)), wuse(blk_attn(a, 5))]
                        for (s, rows) in subs:
                            bi = rot_B()
                            for half in range(2):
                                wt, rw = wpair[half]
                                S.add("pe", proj_tm(wt, hT, s, rows, bi, half, range(8), True, True), [rw, R_hT], [R_B[bi]])
                            si_ = rot("stg", 2)
                            if isK:
                                S.add("act", lambda e, bi=bi, si_=si_, rows=rows: e.activation(
                                    out=stg[si_][0:rows, :], in_=psB[0:rows, bi, :], func=AF.Copy), [R_B[bi]], [R_stg[si_]])
                            else:
                                S.add("dve", lambda e, bi=bi, si_=si_, rows=rows: e.tensor_copy(
                                    out=stg[si_][0:rows, :], in_=psB[0:rows, bi, :]), [R_B[bi]], [R_stg[si_]])
                            if kind == "meta":
                                dst_t, off = (kp if isK else vp), a * (NMETA + SEQ) * D
                            elif kind == "sample":
                                dst_t, off = (kso if isK else vso), a * DEC * D
                            else:
                                dst_t, off = (kp if isK else vp), a * (NMETA + SEQ) * D + (NMETA + t * TP + s * 128) * D
                            dmaop("pool", AP(dst_t, off, [[D, rows], [1, D]]), stg[si_][0:rows, :],
                                  [R_stg[si_]], [R_out], "stg%d" % si_, is_out=True)
                            if not isK:
                                if kind == "meta":
                                    S.add("pool", lambda e, si_=si_, rows=rows, a=a: e.tensor_copy(
                                        out=Vm[0:rows, a, :], in_=stg[si_][0:rows, :]), [R_stg[si_]], [R_Vm[a]])
                                else:
                                    sb_ = rot("stgb", 2)
                                    S.add("pool", lambda e, si_=si_, sb_=sb_, rows=rows: e.tensor_copy(
                                        out=stgb[sb_][0:rows, :], in_=stg[si_][0:rows, :]), [R_stg[si_]], [R_stgb[sb_]])
                                    if kind == "sample":
                                        dmaop("pool", AP(Vs, a * KTS_LEN * D + (PAST + s * 128) * D, [[D, rows], [1, D]]),
                                              stgb[sb_][0:rows, :], [R_stgb[sb_]], [R_Vsn[a]], "stgb%d" % sb_)
                                    else:
                                        dmaop("pool", AP(Vd, a * SEQ * D + (t * TP + s * 128) * D, [[D, rows], [1, D]]),
                                              stgb[sb_][0:rows, :], [R_stgb[sb_]], [R_Vd[a][t]], "stgb%d" % sb_)
                    if kind == "meta" and a == 0:
                        dump("QT0", QT[:, 0, 0:nt], [R_QT[0]])
                        dump("KTm0", KTm[:, 0, 0, :], [R_KTm[0]])
                        dump("Vm0", Vm[0:16, 0, :], [R_Vm[0]])
                        dump("neglam", neglam[:, :], [R_const])
                        dump("sg", sg[:, :], [R_const])
                        dump("cH", cH[:, :], [R_const])
                        dump("D0", D0[:, 0, :], [R_const])
                    for h in range(NH):
                        segs = []
                        loads = []
                        if kind == "meta":
                            segs.append(dict(kt=KTm[:, a, h, :], v=Vm[0:NMETA, a, h * 128:(h + 1) * 128], nk=NMETA, q0=0,
                                             bias=(D0[0:NMETA, h, 0:NMETA], 0), res=[R_KTm[a], R_Vm[a]]))
                        else:
                            mb = None
                            if kind == "prompt" and t == 0:
                                mb = (Dm[:, h, :], 0)
                            segs.append(dict(kt=KTm[:, a, h, :], v=Vm[0:NMETA, a, h * 128:(h + 1) * 128], nk=NMETA, q0=0,
                                             bias=mb, res=[R_KTm[a], R_Vm[a]]))
                            if kind == "sample":
                                nkeys = PAST + DEC
                                ktsrc, vsrc, klen = KTs, Vs, KTS_LEN
                                rkf = lambda c, a=a: [R_KTs] if c == 0 else [R_KTsn[a]]
                                rvf = lambda c, a=a: [R_Vs] if c == 0 else [R_Vsn[a]]
                                g0 = PAST // 128
                            else:
                                nkeys = (t + 1) * TP
                                ktsrc, vsrc, klen = KTd, Vd, SEQ
                                rkf = lambda c, a=a, t=t: [R_KTd[a][tt] for tt in (2 * c, 2 * c + 1) if tt <= t]
                                rvf = lambda c, a=a, t=t: [R_Vd[a][tt] for tt in (2 * c, 2 * c + 1) if tt <= t]
                                g0 = 4 * t
                            nch = (nkeys + KC - 1) // KC
                            base = kvstate["n"]
                            kvstate["n"] += nch
                            for c in range(nch):
                                k0 = c * KC
                                kn = min(KC, nkeys - k0)
                                nkt = (kn + 127) // 128
                                sl = (base + c) % NKV

                                rk = rkf(c)
                                rv = rvf(c)

                                def ld(sl=sl, k0=k0, kn=kn, a=a, h=h, ktsrc=ktsrc, vsrc=vsrc, rk=rk, rv=rv, klen=klen):
                                    dmaop("sp", kvK[sl][:, 0:kn], AP(ktsrc, (a * NH + h) * 128 * klen + k0, [[klen, 128], [1, kn]]),
                                          rk, [R_kv[sl]], "kv%d" % sl)
                                    nfull = kn // 128
                                    if nfull:
                                        dmaop("sp", kvV[sl][:, 0:nfull, :],
                                              AP(vsrc, a * klen * D + k0 * D + h * 128, [[D, 128], [128 * D, nfull], [1, 128]]),
                                              rv, [R_kvV[sl]], "kv%d" % sl)
                                    rr = kn % 128
                                    if rr:
                                        dmaop("sp", kvV[sl][0:rr, nfull, :],
                                              AP(vsrc, a * klen * D + (k0 + nfull * 128) * D + h * 128, [[D, rr], [1, 128]]),
                                              rv, [R_kvV[sl]], "kv%d" % sl)
                                loads.append(ld)
                                for kt_ in range(nkt):
                                    G = (k0 // 128) + kt_
                                    nk = min(128, kn - kt_ * 128)
                                    j_ = G - g0
                                    if j_ < -1:
                                        b_, q0 = None, 0
                                    elif j_ == -1:
                                        b_, q0 = (Dp[0:nk, h, 0:min(128, nt)], 0), 0
                                    else:
                                        q0 = 128 * j_
                                        b_ = (D0[0:nk, h, :], q0)
                                    segs.append(dict(kt=kvK[sl][:, kt_ * 128:kt_ * 128 + nk], v=kvV[sl][0:nk, kt_, :], nk=nk,
                                                     q0=q0, bias=b_, res=[R_kv[sl], R_kvV[sl]], chunk=c))
                        attention_head(a, h, nt, segs, loads)
                    if pend[0] is not None:
                        pend[0]()
                        pend[0] = None
                    w0, rw0 = wuse(blk_attn(a, 6))
                    w1, rw1 = wuse(blk_attn(a, 7))
                    for sp0 in range(0, nsub, 2):
                        pair = subs[sp0:sp0 + 2]
                        bis = {s: rot_B() for (s, rows) in pair}
                        for (s, rows) in pair:
                            S.add("pe", proj_tm(w0, OT, s, rows, bis[s], 0, range(7), True, False), [rw0] + R_OT[0:7], [R_B[bis[s]]])
                            S.add("pe", proj_tm(w1, OT, s, rows, bis[s], 1, range(7), True, False), [rw1] + R_OT[0:7], [R_B[bis[s]]])
                        for (s, rows) in pair:
                            S.add("pe", proj_tm(w0, OT, s, rows, bis[s], 0, [7], False, True), [rw0, R_OT[7]], [R_B[bis[s]]])
                            S.add("pe", proj_tm(w1, OT, s, rows, bis[s], 1, [7], False, True), [rw1, R_OT[7]], [R_B[bis[s]]])
                            postnorm_add(nt, s, rows, bis[s], L)
                    if kind == "meta":
                        dbg_x("mix%d" % L, nt)
                else:
                    sb0 = 48 + 4 * (cnt["small4"] % 4)
                    cnt["small4"] += 1
                    for (s, rows) in subs:
                        smc = small[0:rows, sb0 + s:sb0 + s + 1]
                        S.add("act", lambda e, s=s, rows=rows, smc=smc: e.activation(out=junk[0:rows, :], in_=xb[0:rows, s, :],
                                                                                     func=AF.Square, accum_out=smc),
                              R_x[s], [R_junk, R_small[sb0 + s]])
                        rstd_from_ssq(smc, smc, D, 0, [R_small[sb0 + s]], rows)
                    if kind == "meta":
                        prev, cur_fam = None, 3
                    elif kind == "sample":
                        prev, cur_fam = (hstate[0:16, p, :], 4, 0, 16, R_hstate[p]), 0
                    elif t == 0:
                        prev, cur_fam = (hmeta[0:16, p, :], 2, 0, 16, R_hmeta[p]), 0
                    else:
                        prev, cur_fam = (hcar[64:128, p, :], 1, 64, 128, R_hcar[p]), 0
                    hpi = 0
                    for (s, rows) in subs:
                        hpi = rot("hp", 3)
                        smc = small[0:rows, sb0 + s:sb0 + s + 1]
                        S.add("dve", lambda e, s=s, rows=rows, smc=smc, hpi=hpi, p=p: e.scalar_tensor_tensor(
                            out=hp[hpi][0:rows, :], in0=xb[0:rows, s, :], scalar=smc, in1=gpool[0:rows, p, :],
                            op0=ALU.mult, op1=ALU.mult), R_x[s] + [R_small[sb0 + s], R_const], [R_hp[hpi]])
                        if s == len(subs) - 1 and (kind == "sample" or (kind == "prompt" and t == NT_P - 1)):
                            dst_t = pso if kind == "sample" else pp
                            lo = rows - 32
                            S.add("dve", lambda e, s=s, rows=rows, lo=lo, p=p: e.scalar_tensor_tensor(
                                out=ptmp[lo:rows, :], in0=xb[lo:rows, s, :], scalar=small[lo:rows, sb0 + s:sb0 + s + 1],
                                in1=gpool[lo:rows, p, :], op0=ALU.mult, op1=ALU.mult),
                                R_x[s] + [R_small[sb0 + s], R_const], [R_ptmp])
                            dmaop("pool", AP(dst_t, p * 15 * D, [[D, 15], [1, D]]), ptmp[rows - 15:rows, :], [R_ptmp], [R_out],
                                  "tail", is_out=True)
                        if not meta_last:
                            for half in range(2):
                                ai = nextA()

                                def bfn(e, s=s, rows=rows, hpi=hpi, half=half, ai=ai, prev=prev, cur_fam=cur_fam):
                                    inst = None
                                    for c4 in range(4):
                                        c = half * 4 + c4
                                        g = c // 2
                                        inst = e.matmul(psA[:, ai, c4 * 128:c4 * 128 + rows],
                                                        lhsT=hp[hpi][0:rows, c * 128:(c + 1) * 128],
                                                        rhs=band[0:rows, cur_fam * 4 + g, 0:rows], start=True, stop=(prev is None))
                                        if prev is not None:
                                            pb, fam, r0, r1, _ = prev
                                            inst = e.matmul(psA[:, ai, c4 * 128:c4 * 128 + rows],
                                                            lhsT=pb[:, c * 128:(c + 1) * 128],
                                                            rhs=band[r0:r1, fam * 4 + g, 0:rows], start=False, stop=True)
                                    return inst
                                S.add("pe", bfn, [R_hp[hpi], R_const] + ([prev[4]] if prev is not None else []), [R_A[ai]])
                                src_ = psA[:, ai, :].rearrange("p (c k) -> p c k", c=4)[:, :, 0:rows]
                                d_ap = dT[:, half * 4:half * 4 + 4, s * 128:s * 128 + rows]
                                if half == 0:
                                    S.add("dve", lambda e, src_=src_, d_ap=d_ap: e.tensor_copy(out=d_ap, in_=src_),
                                          [R_A[ai]], [R_dT])
                                else:
                                    S.add("act", lambda e, src_=src_, d_ap=d_ap: e.activation(out=d_ap, in_=src_, func=AF.Copy),
                                          [R_A[ai]], [R_dT])
                        prev = (hp[hpi][64:128, :], 1, 64, 128, R_hp[hpi])
                    if kind == "meta":
                        S.add("pool", lambda e, hpi=hpi, p=p: e.tensor_copy(out=hmeta[0:16, p, :], in_=hp[hpi][0:16, :]),
                              [R_hp[hpi]], [R_hmeta[p]])
                    elif kind == "prompt" and t < NT_P - 1:
                        S.add("pool", lambda e, hpi=hpi, p=p: e.tensor_copy(out=hcar[64:128, p, :], in_=hp[hpi][64:128, :]),
                              [R_hp[hpi]], [R_hcar[p]])
                    if meta_last:
                        break
                    wt, rw = wuse(blk_pool(p))
                    for (s, rows) in subs:
                        bi = rot_B()

                        def fn(e, s=s, rows=rows, bi=bi, wt=wt):
                            inst = None
                            for g in range(4):
                                for c2 in range(2):
                                    inst = e.matmul(psB[0:rows, bi, g * 256:(g + 1) * 256],
                                                    lhsT=dT[:, 2 * g + c2, s * 128:s * 128 + rows], rhs=wt[:, 2 * g + c2, 0:256],
                                                    start=(c2 == 0), stop=(c2 == 1))
                            return inst
                        S.add("pe", fn, [rw, R_dT], [R_B[bi]])
                        postnorm_add(nt, s, rows, bi, L)
                    if kind == "meta":
                        dbg_x("mix%d" % L, nt)
                prenorm_T(nt, hT_dst, [R_hT])
                for j in range(8):
                    wt, rw = wuse(blk_up(L, j))
                    for cc_ in range(4):
                        fc = 4 * j + cc_
                        ai = nextA()
                        S.add("pe", proj_fm(wt, cc_, hT, nt, ai), [rw, R_hT], [R_A[ai]])
                        ri = rot("rt", 2)
                        S.add("act", lambda e, ai=ai, ri=ri: e.activation(out=rtmp[ri][:, 0:nt], in_=psA[:, ai, 0:nt], func=AF.Relu),
                              [R_A[ai]], [R_rt[ri]])
                        S.add("dve" if fc % 2 == 0 else "pool",
                              lambda e, ri=ri, fc=fc: e.tensor_tensor(out=uT[:, fc, 0:nt], in0=rtmp[ri][:, 0:nt],
                                                                      in1=rtmp[ri][:, 0:nt], op=ALU.mult),
                              [R_rt[ri]], [R_uT[fc]])
                for sp0 in range(0, nsub, 2):
                    pair = subs[sp0:sp0 + 2]
                    bis = {}
                    for (s, rows) in pair:
                        bis[s] = rot_B()
                    for c in range(2):
                        for jj in range(4):
                            wt, rw = wuse(blk_dn(L, c, jj))
                            for (s, rows) in pair:
                                S.add("pe", proj_tm(wt, uT, s, rows, bis[s], c, range(8), jj == 0, jj == 3, srcoff=8 * jj),
                                      [rw] + R_uT[8 * jj:8 * jj + 8], [R_B[bis[s]]])
                    for (s, rows) in pair:
                        postnorm_add(nt, s, rows, bis[s], 4 + L)
                if kind == "meta":
                    dbg_x("ffn%d" % L, nt)
            if kind == "sample":
                dmaop("pool", ys.ap(), xb[0:nt, 0, :], R_x[0], [R_out], "yout", is_out=True)
            elif kind == "prompt":
                for s_ in range(4):
                    dmaop("pool", AP(yp, (t * TP + s_ * 128) * D, [[D, 128], [1, D]]), xb[:, s_, :],
                          R_x[s_], [R_out], "yout%d" % s_, is_out=True)

        bi_map = {}

        def rot_B():
            return rot("B_", 2)
        cnt["B_"] = 0

        def tile_plan(kind):
            pl = []
            for L in range(4):
                if L % 2 == 0:
                    pl += [blk_attn(L // 2, j) for j in range(8)]
                else:
                    if kind == "meta" and L == 3:
                        break
                    pl.append(blk_pool(L // 2))
                nsub = 4 if kind == "prompt" else 1
                pl += [blk_up(L, j) for j in range(8)]
                for sp0 in range(0, nsub, 2):
                    pl += [blk_dn(L, c, jj) for c in range(2) for jj in range(4)]
            return pl

        S.add("pool", lambda e: e.memset(hstate[:, :, :], 0.0), [], R_hstate)
        for p in range(2):
            dmaop("sp", ptmp[0:15, :], AP(stp, p * 15 * D, [[D, 15], [1, D]]), [], [R_ptmp], "stpl")
            S.add("dve", lambda e, p=p: e.tensor_copy(out=hstate[0:15, p, :], in_=ptmp[0:15, :]), [R_ptmp], [R_hstate[p]])
        tiles = [("meta", 0), ("sample", 0)] + [("prompt", t) for t in range(NT_P)]
        if stop_after is not None:
            tiles = tiles[:stop_after]
        for kind, t in tiles:
            wplan.extend(tile_plan(kind))
        for kind, t in tiles:
            run_tile(kind, t)

        sems = [es.enter_context(nc.semaphore("s%d" % i)) for i in range(90)]
        S.finalize_and_emit(sems)
    return nc


_CACHE = {}


def kernel(**inputs):
    f32 = lambda a: np.ascontiguousarray(np.asarray(a, dtype=np.float32))
    x_prompt = f32(inputs["x_prompt"]); x_sample = f32(inputs["x_sample"])
    cache_k = f32(inputs["cache_k"]); cache_v = f32(inputs["cache_v"]); state_pool = f32(inputs["state_pool"])
    if "nc" not in _CACHE:
        _CACHE["nc"] = build_program()
    nc = _CACHE["nc"]
    shared = {
        "meta": f32(inputs["meta_tokens"]), "rel": f32(inputs["rel_bias_table"]),
        "nmp": f32(inputs["norm_mix_pre"]), "nmo": f32(inputs["norm_mix_post"]),
        "nfp": f32(inputs["norm_ffn_pre"]), "nfo": f32(inputs["norm_ffn_post"]),
        "wqkv": f32(inputs["w_qkv"]),
        "lq1": f32(inputs["lambda_q1"]), "lk1": f32(inputs["lambda_k1"]),
        "lq2": f32(inputs["lambda_q2"]), "lk2": f32(inputs["lambda_k2"]),
        "subg": f32(inputs["subln_g"]), "wo": f32(inputs["w_o"]), "wpool": f32(inputs["w_pool"]),
        "pscale": f32(inputs["pool_scale"]), "wup": f32(inputs["w_up"]), "wdn": f32(inputs["w_down"]),
        "ohc": _onehot_const(), "invc": _invcnt_const(), "eye": np.eye(128, dtype=np.float32),
        "bandc": _band_const().reshape(20, 128, 128),
    }
    in_maps = []
    for b in range(NCORES):
        m = dict(shared)
        m["xp"] = x_prompt[b]
        m["xs"] = x_sample[b]
        m["ck"] = np.ascontiguousarray(cache_k[:, b].reshape(2, PAST, D))
        m["cv"] = np.ascontiguousarray(cache_v[:, b].reshape(2, PAST, D))
        m["stp"] = np.ascontiguousarray(state_pool[:, b])
        in_maps.append(m)
    res = run_bass_kernel_spmd(nc, in_maps, core_ids=list(range(NCORES)))
    R = res.results
    y_prompt = np.stack([R[b]["yp"] for b in range(NCORES)], 0)
    y_sample = np.stack([R[b]["ys"] for b in range(NCORES)], 0)
    k_prompt = np.stack([R[b]["kp"].reshape(2, NMETA + SEQ, NH, 128) for b in range(NCORES)], 1)
    v_prompt = np.stack([R[b]["vp"].reshape(2, NMETA + SEQ, NH, 128) for b in range(NCORES)], 1)
    pool_prompt = np.stack([R[b]["pp"] for b in range(NCORES)], 1)
    k_sample = np.stack([R[b]["kso"].reshape(2, DEC, NH, 128) for b in range(NCORES)], 1)
    v_sample = np.stack([R[b]["vso"].reshape(2, DEC, NH, 128) for b in range(NCORES)], 1)
    pool_sample = np.stack([R[b]["pso"] for b in range(NCORES)], 1)
    return (y_prompt, y_sample, k_prompt, v_prompt, pool_prompt, k_sample, v_sample, pool_sample)
```
